# Optimizing a Trainium2 kernel written in Bass

```python
import math
import numpy as np
import jax
import jax.numpy as jnp
from jax import lax

D_MODEL = 1024
BATCH = 16
SEQ = 2048
DEPTH = 2
DEC_BATCH = 32
DEC_SEQ = 64
PAST_LEN = 4096

CHUNK = 64
EPS = 1e-6
D_FF = 4096
SSD_HEADS = 16
SSD_HEAD_DIM = 64
SSD_WIDTH = SSD_HEADS * SSD_HEAD_DIM
SSD_GROUPS = 4
SSD_STATE = 128
SSD_CONV = 4
SSD_CONV_DIM = SSD_WIDTH + 2 * SSD_GROUPS * SSD_STATE
ATT_HEADS = 8
ATT_HEAD_DIM = 64
ATT_WIDTH = ATT_HEADS * ATT_HEAD_DIM
ATT_LEFT_CHUNKS = 8
ATT_PAST = ATT_LEFT_CHUNKS * CHUNK
REL_CLIP = 128
MLSTM_HEADS = 4
MLSTM_HEAD_DIM = 128
MLSTM_WIDTH = MLSTM_HEADS * MLSTM_HEAD_DIM
IN_SIZES = (SSD_WIDTH, SSD_CONV_DIM, SSD_HEADS,
            ATT_WIDTH, ATT_WIDTH, ATT_WIDTH,
            MLSTM_WIDTH, MLSTM_WIDTH, MLSTM_WIDTH, MLSTM_WIDTH, MLSTM_HEADS, MLSTM_HEADS,
            3 * D_MODEL)
IN_DIM = sum(IN_SIZES)

kernel_name = "hybrid_streaming_ssd_band_mlstm_step"


def rmsnorm(x, g):
    xf = x.astype(jnp.float32)
    y = xf * lax.rsqrt(jnp.mean(xf * xf, axis=-1, keepdims=True) + EPS)
    return (y * g.astype(jnp.float32)).astype(x.dtype)


def swiglu(x, w_in, w_out):
    a, b = jnp.split(x @ w_in, 2, axis=-1)
    return (jax.nn.silu(a) * b) @ w_out


def split_cols(a, sizes):
    return jnp.split(a, np.cumsum(sizes)[:-1].tolist(), axis=-1)


def to_chunks(a, q):
    b, l = a.shape[:2]
    return jnp.moveaxis(a.reshape((b, l // q, q) + a.shape[2:]), 1, 0)


def from_chunks(a):
    a = jnp.moveaxis(a, 0, 1)
    return a.reshape((a.shape[0], a.shape[1] * a.shape[2]) + a.shape[3:])


def causal_conv(u, u_prev, w, b):
    L = u.shape[1]
    up = jnp.concatenate([u_prev.astype(u.dtype), u], axis=1)
    out = up[:, 0:L] * w[0]
    for j in range(1, SSD_CONV):
        out = out + up[:, j:j + L] * w[j]
    return out + b, up[:, up.shape[1] - (SSD_CONV - 1):]


def ssd_scan(xs, dt, a, bm, cm, h0):
    bsz, L = xs.shape[:2]
    q = min(CHUNK, L)
    r = SSD_HEADS // SSD_GROUPS
    xs = xs.reshape(bsz, L, SSD_GROUPS, r, SSD_HEAD_DIM)
    dt = dt.reshape(bsz, L, SSD_GROUPS, r)
    a = a.reshape(SSD_GROUPS, r)
    h0 = h0.reshape(bsz, SSD_GROUPS, r, SSD_HEAD_DIM, SSD_STATE)
    causal = jnp.tril(jnp.ones((q, q), bool))[None, :, :, None, None]

    def step(h, inp):
        xc, dtc, bc, cc = inp
        cum = jnp.cumsum(dtc * a, axis=1)
        seg = cum[:, :, None] - cum[:, None, :]
        lmat = jnp.exp(jnp.where(causal, seg, -jnp.inf))
        cb = jnp.einsum('btgn,bsgn->btsg', cc, bc)
        w = cb[..., None] * lmat * dtc[:, None]
        y = jnp.einsum('btsgr,bsgrp->btgrp', w, xc)
        y = y + jnp.exp(cum)[..., None] * jnp.einsum('btgn,bgrpn->btgrp', cc, h)
        last = cum[:, -1]
        wd = jnp.exp(last[:, None] - cum) * dtc
        h = jnp.exp(last)[..., None, None] * h + jnp.einsum('bsgr,bsgn,bsgrp->bgrpn', wd, bc, xc)
        return h, y

    h, ys = lax.scan(step, h0, (to_chunks(xs, q), to_chunks(dt, q), to_chunks(bm, q), to_chunks(cm, q)))
    y = from_chunks(ys).reshape(bsz, L, SSD_HEADS, SSD_HEAD_DIM)
    return y, h.reshape(bsz, SSD_HEADS, SSD_HEAD_DIM, SSD_STATE)


def mlstm_scan(q, k, v, ig, lf, c0, n0, m0):
    L = q.shape[1]
    qs = min(CHUNK, L)
    causal = jnp.tril(jnp.ones((qs, qs), bool))[None, :, :, None]

    def step(carry, inp):
        c, n, m = carry
        qc, kc, vc, ic, fc = inp
        b = jnp.cumsum(fc, axis=1)
        logd = jnp.where(causal, b[:, :, None] - b[:, None, :] + ic[:, None], -jnp.inf)
        inter = b + m[:, None]
        mt = jnp.maximum(jnp.max(logd, axis=2), inter)
        s = jnp.einsum('bthd,bshd->btsh', qc, kc) * jnp.exp(logd - mt[:, :, None])
        ei = jnp.exp(inter - mt)
        num = jnp.einsum('btsh,bshd->bthd', s, vc) + ei[..., None] * jnp.einsum('bthk,bhkv->bthv', qc, c)
        den = jnp.sum(s, axis=2) + ei * jnp.einsum('bthk,bhk->bth', qc, n)
        h = num / jnp.maximum(jnp.abs(den), jnp.exp(-mt))[..., None]
        blast = b[:, -1]
        wlog = blast[:, None] - b + ic
        mnew = jnp.maximum(blast + m, jnp.max(wlog, axis=1))
        w = jnp.exp(wlog - mnew[:, None])
        decay = jnp.exp(blast + m - mnew)
        c = decay[..., None, None] * c + jnp.einsum('bsh,bshk,bshv->bhkv', w, kc, vc)
        n = decay[..., None] * n + jnp.einsum('bsh,bshk->bhk', w, kc)
        return (c, n, mnew), h

    (c, n, m), hs = lax.scan(step, (c0, n0, m0),
                             (to_chunks(q, qs), to_chunks(k, qs), to_chunks(v, qs),
                              to_chunks(ig, qs), to_chunks(lf, qs)))
    return from_chunks(hs), c, n, m


def rel_bias_mask(q_pos, k_pos, table):
    rel = q_pos[:, None] - k_pos[None, :]
    bias = table[jnp.clip(rel, -REL_CLIP, REL_CLIP) + REL_CLIP].astype(jnp.float32)
    qc = q_pos // CHUNK
    kc = k_pos // CHUNK
    ok = ((k_pos[None, :] >= 0) & (kc[None, :] <= qc[:, None])
          & (kc[None, :] >= qc[:, None] - ATT_LEFT_CHUNKS))
    return jnp.transpose(jnp.where(ok[..., None], bias, -jnp.inf), (2, 0, 1))


def band_attend(q, k, v, q_pos, k_pos, table):
    s = jnp.einsum('bqhd,bkhd->bhqk', q, k).astype(jnp.float32) * (ATT_HEAD_DIM ** -0.5)
    s = s + rel_bias_mask(q_pos, k_pos, table)[None]
    p = jax.nn.softmax(s, axis=-1).astype(v.dtype)
    return jnp.einsum('bhqk,bkhd->bqhd', p, v)


def attn_prompt(q, k, v, table):
    L = q.shape[1]
    pad = ((0, 0), (ATT_PAST, 0), (0, 0), (0, 0))
    kp = jnp.pad(k, pad)
    vp = jnp.pad(v, pad)
    band = ATT_PAST + CHUNK

    def one_chunk(c):
        start = c * CHUNK
        qc = lax.dynamic_slice_in_dim(q, start, CHUNK, axis=1)
        kc = lax.dynamic_slice_in_dim(kp, start, band, axis=1)
        vc = lax.dynamic_slice_in_dim(vp, start, band, axis=1)
        q_pos = start + jnp.arange(CHUNK)
        k_pos = start - ATT_PAST + jnp.arange(band)
        return band_attend(qc, kc, vc, q_pos, k_pos, table)

    return from_chunks(lax.map(one_chunk, jnp.arange(L // CHUNK)))


def attn_sample(q, k, v, k_cache, v_cache, table):
    lc, n = k_cache.shape[1], q.shape[1]
    kk = jnp.concatenate([k_cache.astype(k.dtype), k], axis=1)
    vv = jnp.concatenate([v_cache.astype(v.dtype), v], axis=1)
    return band_attend(q, kk, vv, lc + jnp.arange(n), jnp.arange(lc + n), table)


def mixer(u, lp, st):
    f32 = jnp.float32
    bsz, L, _ = u.shape
    (z, xbc, dt_raw, aq, ak, av, mq, mk, mv, mo, mi, mf, gl) = split_cols(u @ lp['w_in'], IN_SIZES)
    if st is None:
        conv_prev = jnp.zeros((bsz, SSD_CONV - 1, SSD_CONV_DIM), u.dtype)
        ssd_h0 = jnp.zeros((bsz, SSD_HEADS, SSD_HEAD_DIM, SSD_STATE), f32)
        c0 = jnp.zeros((bsz, MLSTM_HEADS, MLSTM_HEAD_DIM, MLSTM_HEAD_DIM), f32)
        n0 = jnp.zeros((bsz, MLSTM_HEADS, MLSTM_HEAD_DIM), f32)
        m0 = jnp.zeros((bsz, MLSTM_HEADS), f32)
    else:
        k_cache, v_cache, ssd_h0, conv_prev, c0, n0, m0 = st
    xbc, conv_new = causal_conv(xbc, conv_prev, lp['ssd_conv_w'], lp['ssd_conv_b'])
    xs, bm, cm = split_cols(jax.nn.silu(xbc), (SSD_WIDTH, SSD_GROUPS * SSD_STATE, SSD_GROUPS * SSD_STATE))
    xs = xs.reshape(bsz, L, SSD_HEADS, SSD_HEAD_DIM).astype(f32)
    dt = jax.nn.softplus(dt_raw.astype(f32) + lp['ssd_dt_bias'].astype(f32))
    a = -jnp.exp(lp['ssd_a_log'].astype(f32))
    y, ssd_new = ssd_scan(xs, dt, a,
                          bm.reshape(bsz, L, SSD_GROUPS, SSD_STATE).astype(f32),
                          cm.reshape(bsz, L, SSD_GROUPS, SSD_STATE).astype(f32),
                          ssd_h0.astype(f32))
    y = y + lp['ssd_d'].astype(f32)[:, None] * xs
    y_ssd = rmsnorm(y.reshape(bsz, L, SSD_WIDTH).astype(u.dtype) * jax.nn.silu(z), lp['ssd_norm'])
    q = aq.reshape(bsz, L, ATT_HEADS, ATT_HEAD_DIM)
    k = ak.reshape(bsz, L, ATT_HEADS, ATT_HEAD_DIM)
    v = av.reshape(bsz, L, ATT_HEADS, ATT_HEAD_DIM)
    if st is None:
        y_att = attn_prompt(q, k, v, lp['attn_rel_bias'])
        keep = min(ATT_PAST, L)
        k_new, v_new = k[:, L - keep:], v[:, L - keep:]
    else:
        y_att = attn_sample(q, k, v, k_cache, v_cache, lp['attn_rel_bias'])
        k_new, v_new = k, v
    y_att = y_att.reshape(bsz, L, ATT_WIDTH)
    gb = lp['mlstm_gate_bias'].astype(f32)
    ig = mi.astype(f32) + gb[:MLSTM_HEADS]
    lf = jax.nn.log_sigmoid(mf.astype(f32) + gb[MLSTM_HEADS:])
    mqh = mq.reshape(bsz, L, MLSTM_HEADS, MLSTM_HEAD_DIM).astype(f32)
    mkh = mk.reshape(bsz, L, MLSTM_HEADS, MLSTM_HEAD_DIM).astype(f32) * (MLSTM_HEAD_DIM ** -0.5)
    mvh = mv.reshape(bsz, L, MLSTM_HEADS, MLSTM_HEAD_DIM).astype(f32)
    h, c_new, n_new, m_new = mlstm_scan(mqh, mkh, mvh, ig, lf,
                                        c0.astype(f32), n0.astype(f32), m0.astype(f32))
    h = rmsnorm(h, lp['mlstm_norm'].reshape(MLSTM_HEADS, MLSTM_HEAD_DIM))
    y_ml = h.reshape(bsz, L, MLSTM_WIDTH).astype(u.dtype) * jax.nn.sigmoid(mo)
    g_ssd, g_att, g_ml = jnp.split(jax.nn.sigmoid(gl), 3, axis=-1)
    merged = (g_ssd * (y_ssd @ lp['w_proj_ssd']) + g_att * (y_att @ lp['w_proj_attn'])
              + g_ml * (y_ml @ lp['w_proj_mlstm']))
    return merged @ lp['w_out'], (k_new, v_new, ssd_new, conv_new, c_new, n_new, m_new)


def layer(x, lp, st):
    h = x + 0.5 * swiglu(rmsnorm(x, lp['norm_ffn1']), lp['w_ffn1_in'], lp['w_ffn1_out'])
    mix, new_st = mixer(rmsnorm(h, lp['norm_mix']), lp, st)
    h = h + mix
    h = h + 0.5 * swiglu(rmsnorm(h, lp['norm_ffn2']), lp['w_ffn2_in'], lp['w_ffn2_out'])
    return h, new_st


def setup_inputs(seed: int = 0) -> dict:
    key = jax.random.key(seed)
    ks = iter(jax.random.split(key, 48))
    f = jnp.float32
    lc = min(ATT_PAST, PAST_LEN)

    def nrm(shape, scale):
        return scale * jax.random.normal(next(ks), shape, f)

    def gain(shape):
        return 1.0 + nrm(shape, 0.02)

    dt0 = jnp.exp(jax.random.uniform(next(ks), (DEPTH, SSD_HEADS), f, math.log(1e-3), math.log(1e-1)))
    return {
        'x_prompt': nrm((BATCH, SEQ, D_MODEL), 1.0),
        'x_sample': nrm((DEC_BATCH, DEC_SEQ, D_MODEL), 1.0),
        'cache_attn_k': nrm((DEPTH, DEC_BATCH, lc, ATT_HEADS, ATT_HEAD_DIM), 1.0),
        'cache_attn_v': nrm((DEPTH, DEC_BATCH, lc, ATT_HEADS, ATT_HEAD_DIM), 1.0),
        'state_ssd': nrm((DEPTH, DEC_BATCH, SSD_HEADS, SSD_HEAD_DIM, SSD_STATE), 0.1),
        'state_ssd_conv': nrm((DEPTH, DEC_BATCH, SSD_CONV - 1, SSD_CONV_DIM), 1.0),
        'state_mlstm_c': nrm((DEPTH, DEC_BATCH, MLSTM_HEADS, MLSTM_HEAD_DIM, MLSTM_HEAD_DIM), 0.1),
        'state_mlstm_n': nrm((DEPTH, DEC_BATCH, MLSTM_HEADS, MLSTM_HEAD_DIM), 0.1),
        'state_mlstm_m': nrm((DEPTH, DEC_BATCH, MLSTM_HEADS), 1.0),
        'norm_ffn1': gain((DEPTH, D_MODEL)),
        'w_ffn1_in': nrm((DEPTH, D_MODEL, 2 * D_FF), D_MODEL ** -0.5),
        'w_ffn1_out': nrm((DEPTH, D_FF, D_MODEL), D_FF ** -0.5),
        'norm_mix': gain((DEPTH, D_MODEL)),
        'w_in': nrm((DEPTH, D_MODEL, IN_DIM), D_MODEL ** -0.5),
        'ssd_conv_w': nrm((DEPTH, SSD_CONV, SSD_CONV_DIM), SSD_CONV ** -0.5),
        'ssd_conv_b': nrm((DEPTH, SSD_CONV_DIM), 0.01),
        'ssd_dt_bias': dt0 + jnp.log(-jnp.expm1(-dt0)),
        'ssd_a_log': jnp.log(jax.random.uniform(next(ks), (DEPTH, SSD_HEADS), f, 1.0, 16.0)),
        'ssd_d': 1.0 + nrm((DEPTH, SSD_HEADS), 0.1),
        'ssd_norm': gain((DEPTH, SSD_WIDTH)),
        'attn_rel_bias': nrm((DEPTH, 2 * REL_CLIP + 1, ATT_HEADS), 0.5),
        'mlstm_gate_bias': jnp.concatenate([nrm((DEPTH, MLSTM_HEADS), 0.1),
                                            3.0 + nrm((DEPTH, MLSTM_HEADS), 0.5)], axis=-1),
        'mlstm_norm': gain((DEPTH, MLSTM_WIDTH)),
        'w_proj_ssd': nrm((DEPTH, SSD_WIDTH, D_MODEL), SSD_WIDTH ** -0.5),
        'w_proj_attn': nrm((DEPTH, ATT_WIDTH, D_MODEL), ATT_WIDTH ** -0.5),
        'w_proj_mlstm': nrm((DEPTH, MLSTM_WIDTH, D_MODEL), MLSTM_WIDTH ** -0.5),
        'w_out': nrm((DEPTH, D_MODEL, D_MODEL), D_MODEL ** -0.5),
        'norm_ffn2': gain((DEPTH, D_MODEL)),
        'w_ffn2_in': nrm((DEPTH, D_MODEL, 2 * D_FF), D_MODEL ** -0.5),
        'w_ffn2_out': nrm((DEPTH, D_FF, D_MODEL), D_FF ** -0.5),
        'final_norm': gain((D_MODEL,)),
    }


def stack_states(states, i):
    return jnp.stack([s[i] for s in states])


def reference(x_prompt, x_sample, cache_attn_k, cache_attn_v, state_ssd, state_ssd_conv,
              state_mlstm_c, state_mlstm_n, state_mlstm_m,
              norm_ffn1, w_ffn1_in, w_ffn1_out, norm_mix, w_in,
              ssd_conv_w, ssd_conv_b, ssd_dt_bias, ssd_a_log, ssd_d, ssd_norm,
              attn_rel_bias, mlstm_gate_bias, mlstm_norm,
              w_proj_ssd, w_proj_attn, w_proj_mlstm, w_out,
              norm_ffn2, w_ffn2_in, w_ffn2_out, final_norm):
    hp, hs = x_prompt, x_sample
    new_p, new_s = [], []
    for l in range(DEPTH):
        lp = {
            'norm_ffn1': norm_ffn1[l], 'w_ffn1_in': w_ffn1_in[l], 'w_ffn1_out': w_ffn1_out[l],
            'norm_mix': norm_mix[l], 'w_in': w_in[l],
            'ssd_conv_w': ssd_conv_w[l], 'ssd_conv_b': ssd_conv_b[l], 'ssd_dt_bias': ssd_dt_bias[l],
            'ssd_a_log': ssd_a_log[l], 'ssd_d': ssd_d[l], 'ssd_norm': ssd_norm[l],
            'attn_rel_bias': attn_rel_bias[l],
            'mlstm_gate_bias': mlstm_gate_bias[l], 'mlstm_norm': mlstm_norm[l],
            'w_proj_ssd': w_proj_ssd[l], 'w_proj_attn': w_proj_attn[l], 'w_proj_mlstm': w_proj_mlstm[l],
            'w_out': w_out[l],
            'norm_ffn2': norm_ffn2[l], 'w_ffn2_in': w_ffn2_in[l], 'w_ffn2_out': w_ffn2_out[l],
        }
        hp, sp = layer(hp, lp, None)
        hs, ss = layer(hs, lp, (cache_attn_k[l], cache_attn_v[l], state_ssd[l], state_ssd_conv[l],
                                state_mlstm_c[l], state_mlstm_n[l], state_mlstm_m[l]))
        new_p.append(sp)
        new_s.append(ss)
    y_prompt = rmsnorm(hp, final_norm)
    y_sample = rmsnorm(hs, final_norm)
    return (y_prompt, y_sample,
            stack_states(new_p, 0), stack_states(new_p, 1), stack_states(new_p, 2), stack_states(new_p, 3),
            stack_states(new_p, 4), stack_states(new_p, 5), stack_states(new_p, 6),
            stack_states(new_s, 0), stack_states(new_s, 1), stack_states(new_s, 2), stack_states(new_s, 3),
            stack_states(new_s, 4), stack_states(new_s, 5), stack_states(new_s, 6))
```

```python
import contextlib
import os
import numpy as np
import concourse.bass as bass
import concourse.mybir as mybir
from concourse.bass_utils import run_bass_kernel_spmd

F32 = mybir.dt.float32
BF16 = mybir.dt.bfloat16
AF = mybir.ActivationFunctionType
ALU = mybir.AluOpType
AX = mybir.AxisListType

ENGS = ['pe', 'act', 'dve', 'pool', 'sp']
D = 1024
DFF = 4096
IN_DIM = 9752
C_Z, C_XBC, C_DT, C_AQ, C_AK, C_AV, C_MQ, C_MK, C_MV, C_MO, C_MI, C_GL = (
    0, 1024, 3072, 3088, 3600, 4112, 4624, 5136, 5648, 6160, 6672, 6680)
EPS = 1e-6
NEG = -30000.0


class DmaSem:
    def __init__(self, sem):
        self.sem = sem
        self.count = 0


class Prog:
    def __init__(self, nc, stack):
        self.nc = nc
        self.gstack = stack
        self.stack = stack
        self.ops = {e: [] for e in ENGS}
        self.esem = {e: stack.enter_context(nc.semaphore('s_' + e)) for e in ENGS}
        self.count = {e: 0 for e in ENGS}
        self.seen = {e: {} for e in ENGS}
        self.lastw = {}
        self.readers = {}
        self.dsems = {}
        self.uid = 0
        self.evc = 0

    def dsem(self, key):
        if key not in self.dsems:
            self.dsems[key] = DmaSem(self.gstack.enter_context(self.nc.semaphore('d%d' % len(self.dsems))))
        return self.dsems[key]

    def sbuf(self, name, shape, dtype):
        self.uid += 1
        return self.stack.enter_context(self.nc.sbuf_tensor('%s_%d' % (name, self.uid), list(shape), dtype))

    def _deps(self, eng, reads, writes):
        need = {}

        def add(tok, same_ok):
            if tok is None:
                return
            sem, val, teng = tok
            if teng == eng and not same_ok:
                return
            k = id(sem)
            if k not in need or need[k][1] < val:
                need[k] = (sem, val)

        for r in reads:
            add(self.lastw.get(r), True)
        for w in writes:
            add(self.lastw.get(w), True)
            for t in self.readers.get(w, ()):
                add(t, True)
        waits = []
        seen = self.seen[eng]
        for k, (sem, val) in need.items():
            if seen.get(k, 0) >= val:
                continue
            seen[k] = val
            waits.append((sem, val))
        return waits

    def _commit(self, tok, reads, writes):
        for r in reads:
            self.readers.setdefault(r, []).append(tok)
        for w in writes:
            self.lastw[w] = tok
            self.readers[w] = []

    def op(self, eng, fn, reads=(), writes=()):
        waits = self._deps(eng, reads, writes)
        self.count[eng] += 1
        tok = (self.esem[eng], self.count[eng], eng)
        self.ops[eng].append((waits, fn, (self.esem[eng], 1)))
        self._commit(tok, reads, writes)
        return tok

    def dma(self, eng, key, pairs, reads=(), writes=(), **kw):
        ds = self.dsem(key)
        if not isinstance(pairs, list):
            pairs = [pairs]
        waits = self._deps(eng, reads, writes)
        for i, (out, in_) in enumerate(pairs):
            ds.count += 16

            def fn(e, out=out, in_=in_, kw=kw):
                return e.dma_start(out=out, in_=in_, **kw)

            self.ops[eng].append((waits if i == 0 else [], fn, (ds.sem, 16)))
        tok = (ds.sem, ds.count, 'dma')
        self._commit(tok, reads, writes)
        return tok

    def barrier(self):
        toks = [(self.esem[e], self.count[e]) for e in ENGS if self.count[e] > 0]
        toks += [(d.sem, d.count) for d in self.dsems.values() if d.count > 0]
        for e in ENGS:
            seen = self.seen[e]
            waits = []
            for sem, val in toks:
                if sem is self.esem[e]:
                    continue
                if seen.get(id(sem), 0) >= val:
                    continue
                seen[id(sem)] = val
                waits.append((sem, val))
            self.ops[e].append((waits, None, None))
        self.lastw = {}
        self.readers = {}

    def emit(self):
        nc = self.nc
        ops = self.ops
        with nc.Block() as block:
            def mk(ename):
                def body(e):
                    for waits, fn, inc in ops[ename]:
                        for sem, val in waits:
                            e.wait_ge(sem, val)
                        if fn is not None:
                            ins = fn(e)
                            if inc is not None:
                                ins.then_inc(inc[0], inc[1])
                return body
            block.tensor(mk('pe'))
            block.scalar(mk('act'))
            block.vector(mk('dve'))
            block.gpsimd(mk('pool'))
            block.sync(mk('sp'))
        self.ops = {e: [] for e in ENGS}

    @contextlib.contextmanager
    def phase(self):
        st = contextlib.ExitStack()
        old = self.stack
        self.stack = st
        with st:
            yield
            self.barrier()
            self.emit()
        self.stack = old

    def mm(self, mms, reads, writes):
        def fn(e, mms=mms):
            ins = None
            for m in mms:
                if m[0] == 'M':
                    ins = e.matmul(m[1], lhsT=m[2], rhs=m[3], start=m[4], stop=m[5])
                else:
                    ins = e.transpose(m[1], m[2], m[3])
            return ins
        return self.op('pe', fn, reads, writes)

    def act(self, out, in_, func, reads, writes, **kw):
        return self.op('act', lambda e: e.activation(out=out, in_=in_, func=func, **kw), reads, writes)

    def tt(self, out, in0, in1, op, reads, writes, eng='dve'):
        return self.op(eng, lambda e: e.tensor_tensor(out=out, in0=in0, in1=in1, op=op), reads, writes)

    def ts(self, out, in0, s1, s2, op0, op1, reads, writes, eng='dve'):
        if s2 is None:
            return self.op(eng, lambda e: e.tensor_scalar(out=out, in0=in0, scalar1=s1, scalar2=None, op0=op0), reads, writes)
        return self.op(eng, lambda e: e.tensor_scalar(out=out, in0=in0, scalar1=s1, scalar2=s2, op0=op0, op1=op1), reads, writes)

    def stt(self, out, in0, scalar, in1, op0, op1, reads, writes, eng='dve'):
        return self.op(eng, lambda e: e.scalar_tensor_tensor(out=out, in0=in0, scalar=scalar, in1=in1, op0=op0, op1=op1), reads, writes)

    def copy(self, out, in_, reads, writes, eng=None):
        if eng is None:
            self.evc += 1
            eng = 'act' if self.evc % 2 else 'dve'
        if eng == 'act':
            return self.op('act', lambda e: e.copy(out=out, in_=in_), reads, writes)
        return self.op(eng, lambda e: e.tensor_copy(out=out, in_=in_), reads, writes)

    def memset(self, ap, val, writes, eng='dve'):
        return self.op(eng, lambda e: e.memset(ap, val), (), writes)


class Cfg:
    def __init__(self, NP=2, LP=2048, NS=4, DEPTH=2):
        self.NP, self.LP, self.NS, self.DEPTH = NP, LP, NS, DEPTH
        self.TP = NP * LP
        self.TALL = self.TP + NS * 64
        self.tiles = []
        for s in range(NP):
            nt = LP // 512
            for i in range(nt):
                self.tiles.append(dict(t0=s * LP + i * 512, n=512, Q=128, nsub=4, prompt=True, seq=s,
                                       first=(i == 0), last=(i == nt - 1), i=i))
        if NS:
            self.tiles.append(dict(t0=self.TP, n=NS * 64, Q=64, nsub=NS, prompt=False, seq=0,
                                   first=True, last=True, i=0))


def build(cfg):
    NP, LP, NS, DEPTH = cfg.NP, cfg.LP, cfg.NS, cfg.DEPTH
    TALL = cfg.TALL
    KEEP = min(512, LP)
    nc = bass.Bass("TRN2", target_bir_lowering=False)

    def din(name, shape, dt=F32):
        return nc.dram_tensor(name, list(shape), dt, kind="ExternalInput").ap()

    def dout(name, shape, dt=F32):
        return nc.dram_tensor(name, list(shape), dt, kind="ExternalOutput").ap()

    def dscr(name, shape, dt):
        if name in getattr(cfg, 'dbg', ()):
            return nc.dram_tensor(name, list(shape), dt, kind="ExternalOutput").ap()
        return nc.dram_tensor(name, list(shape), dt).ap()

    I = {}
    I['xp'] = din('xp', [max(NP, 1) * LP, D])
    I['xs'] = din('xs', [max(NS, 1) * 64, D])
    I['ck'] = din('ck', [DEPTH, max(NS, 1), 512, 512])
    I['cv'] = din('cv', [DEPTH, max(NS, 1), 512, 512])
    I['sssd'] = din('sssd', [DEPTH, max(NS, 1), 1024, 128])
    I['sconv'] = din('sconv', [DEPTH, max(NS, 1), 3, 2048])
    I['smc'] = din('smc', [DEPTH, max(NS, 1), 4, 128, 128])
    I['smn'] = din('smn', [DEPTH, max(NS, 1), 4, 128])
    I['smm'] = din('smm', [DEPTH, max(NS, 1), 4])
    I['w1i'] = din('w1i', [DEPTH, D, 2 * DFF])
    I['w1o'] = din('w1o', [DEPTH, DFF, D])
    I['win'] = din('win', [DEPTH, D, IN_DIM])
    I['wps'] = din('wps', [DEPTH, 1024, D])
    I['wpa'] = din('wpa', [DEPTH, 512, D])
    I['wpm'] = din('wpm', [DEPTH, 512, D])
    I['wo'] = din('wo', [DEPTH, D, D])
    I['w2i'] = din('w2i', [DEPTH, D, 2 * DFF])
    I['w2o'] = din('w2o', [DEPTH, DFF, D])
    I['gains'] = din('gains', [128, 3 * DEPTH + 1, 8])
    I['convw'] = din('convw', [128, DEPTH, 16, 4])
    I['convb'] = din('convb', [128, DEPTH, 16])
    I['rep16'] = din('rep16', [128, DEPTH, 3, 16])
    I['ssdn'] = din('ssdn', [128, DEPTH, 1024])
    I['mln'] = din('mln', [128, DEPTH, 512])
    I['gbr'] = din('gbr', [128, DEPTH, 8])
    I['abias'] = din('abias', [DEPTH, 128, 4, 8, 128])
    I['cmat'] = din('cmat', [128, 8, 128])

    O = {}
    O['yp'] = dout('yp', [max(NP, 1) * LP, D])
    O['ys'] = dout('ys', [max(NS, 1) * 64, D])
    O['kp'] = dout('kp', [DEPTH, max(NP, 1), KEEP, 512])
    O['vp'] = dout('vp', [DEPTH, max(NP, 1), KEEP, 512])
    O['ssdp'] = dout('ssdp', [DEPTH, max(NP, 1), 1024, 128])
    O['convp'] = dout('convp', [DEPTH, max(NP, 1), 3, 2048])
    O['mcp'] = dout('mcp', [DEPTH, max(NP, 1), 4, 128, 128])
    O['mnp'] = dout('mnp', [DEPTH, max(NP, 1), 4, 128])
    O['mmp'] = dout('mmp', [DEPTH, max(NP, 1), 4])
    O['ks'] = dout('ks', [DEPTH, max(NS, 1), 64, 512])
    O['vs'] = dout('vs', [DEPTH, max(NS, 1), 64, 512])
    O['ssds'] = dout('ssds', [DEPTH, max(NS, 1), 1024, 128])
    O['convs'] = dout('convs', [DEPTH, max(NS, 1), 3, 2048])
    O['mcs'] = dout('mcs', [DEPTH, max(NS, 1), 4, 128, 128])
    O['mns'] = dout('mns', [DEPTH, max(NS, 1), 4, 128])
    O['mms'] = dout('mms', [DEPTH, max(NS, 1), 4])

    S = {}
    S['R'] = dscr('S_R', [D, TALL], F32)
    S['XN'] = dscr('S_XN', [D, TALL], BF16)
    S['G'] = dscr('S_G', [DFF, TALL], BF16)
    S['xtm'] = dscr('S_xtm', [TALL, 1024], BF16)
    S['Btm'] = dscr('S_Btm', [TALL, 512], BF16)
    S['Bfm'] = dscr('S_Bfm', [512, TALL], BF16)
    S['Cfm'] = dscr('S_Cfm', [512, TALL], BF16)
    S['q'] = dscr('S_q', [512, TALL], BF16)
    S['k'] = dscr('S_k', [512, TALL], BF16)
    S['mq'] = dscr('S_mq', [512, TALL], BF16)
    S['mkf'] = dscr('S_mkf', [512, TALL], BF16)
    S['z'] = dscr('S_z', [TALL, 1024], BF16)
    S['v'] = dscr('S_v', [TALL, 520], BF16)
    S['mkt'] = dscr('S_mkt', [TALL, 512], BF16)
    S['mv'] = dscr('S_mv', [TALL, 516], BF16)
    S['mo'] = dscr('S_mo', [TALL, 512], BF16)
    S['sm'] = dscr('S_sm', [TALL, 24], F32)
    S['Y'] = dscr('S_Y', [2048, TALL], BF16)

    out_toks = []

    with contextlib.ExitStack() as gst:
        P = Prog(nc, gst)
        ps = [gst.enter_context(nc.psum_tensor('ps%d' % i, [128, 512], F32)) for i in range(8)]
        psn = [('ps', i) for i in range(8)]
        cm = P.sbuf('cmat', [128, 8, 128], F32)
        idb = P.sbuf('idb', [128, 128], BF16)
        gains = P.sbuf('gains', [128, 3 * DEPTH + 1, 8], F32)
        convw = P.sbuf('convw', [128, DEPTH, 16, 4], F32)
        convb = P.sbuf('convb', [128, DEPTH, 16], F32)
        rep16 = P.sbuf('rep16', [128, DEPTH, 3, 16], F32)
        gbr = P.sbuf('gbr', [128, DEPTH, 8], F32)
        arep = P.sbuf('arep', [128, DEPTH, 16], F32)
        epst = P.sbuf('epst', [128, 1], F32)
        U, SU, IDF, ONES, EL128, EL64, MN = (cm[:, i, :] for i in range(7))

        bank = [0]

        def nb():
            b = bank[0]
            bank[0] = (b + 1) % 6
            return b

        with P.phase():
            P.dma('sp', 'g0', [(cm[:], I['cmat']), (gains[:], I['gains']), (convw[:], I['convw']),
                               (convb[:], I['convb']), (rep16[:], I['rep16']), (gbr[:], I['gbr'])],
                  writes=['cm', 'gains', 'convw', 'convb', 'rep16', 'gbr'])
            P.dma('pool', 'g1', (idb[:], I['cmat'][:, 2, :]), writes=['idb'])
            P.memset(epst[:], EPS, ['epst'])
            P.act(arep[:], rep16[:, :, 1, :], AF.Exp, ['rep16'], ['arep'])
            P.ts(arep[:], arep[:], -1.0, None, ALU.mult, None, ['arep'], ['arep'])

        def finish_residual(tl, Rt, rname, gi, bufs, final=False, store_R=True, defer=False):
            n = tl['n']
            t0 = tl['t0']
            sq, acc, rstd, XNo = bufs['sq'], bufs['acc'], bufs['rstd'], bufs['XNo']
            sq2 = bufs['sq2']
            for c in range(8):
                if c == 0:
                    P.act(acc[:, :n], Rt[:, 0, :n], AF.Square, [rname], ['acc'])
                else:
                    sq_, sqn = (sq, 'sq') if c % 2 else (sq2, 'sq2')
                    P.act(sq_[:, :n], Rt[:, c, :n], AF.Square, [rname], [sqn])
                    P.tt(acc[:, :n], acc[:, :n], sq_[:, :n], ALU.add, ['acc', sqn], ['acc'])

            def part_b():
                _finish_b(tl, Rt, rname, gi, bufs, final, store_R)
            if defer:
                return part_b
            part_b()
            return None

        def _finish_b(tl, Rt, rname, gi, bufs, final, store_R):
            n = tl['n']
            t0 = tl['t0']
            sq, acc, rstd, XNo = bufs['sq'], bufs['acc'], bufs['rstd'], bufs['XNo']
            b = nb()
            P.mm([('M', ps[b][:, :n], ONES, acc[:, :n], True, True)], ['cm', 'acc'], [psn[b]])
            P.act(rstd[:, :n], ps[b][:, :n], AF.Sqrt, [psn[b], 'epst'], ['rstd'], bias=epst[:, 0:1], scale=1.0 / D)
            P.op('dve', lambda e: e.reciprocal(out=rstd[:, :n], in_=rstd[:, :n]), ['rstd'], ['rstd'])
            if store_R:
                P.dma('pool', ('stR', rname), (S['R'][:, t0:t0 + n].rearrange("(c p) t -> p c t", p=128), Rt[:, :, :n]),
                      reads=[rname], writes=[('S_R', t0)])
            if not final:
                for c in range(8):
                    P.stt(XNo[:, c, :n], Rt[:, c, :n], gains[:, gi, c:c + 1], rstd[:, :n], ALU.mult, ALU.mult,
                          [rname, 'rstd', 'gains'], ['XNo'])
                P.dma('pool', 'stXN', (S['XN'][:, t0:t0 + n].rearrange("(c p) t -> p c t", p=128), XNo[:, :, :n]),
                      reads=['XNo'], writes=[('S_XN', t0)])
            else:
                yfm, ytm = bufs['yfm'], bufs['ytm']
                for c in range(8):
                    P.stt(yfm[:, c, :n], Rt[:, c, :n], gains[:, gi, c:c + 1], rstd[:, :n], ALU.mult, ALU.mult,
                          [rname, 'rstd', 'gains'], ['yfm'])
                for j in range(n // 128):
                    for half in range(2):
                        b = nb()
                        P.mm([('T', ps[b][:, cc * 128:(cc + 1) * 128], yfm[:, half * 4 + cc, j * 128:(j + 1) * 128], IDF)
                              for cc in range(4)], ['yfm', 'cm'], [psn[b]])
                        P.copy(ytm[:, half * 512:(half + 1) * 512], ps[b][:, :], [psn[b]], ['ytm'])
                    tg = t0 + j * 128
                    if tl['prompt']:
                        dst = O['yp'][tg:tg + 128, :]
                    else:
                        dst = O['ys'][tg - cfg.TP:tg - cfg.TP + 128, :]
                    out_toks.append(P.dma('pool', 'sty', (dst, ytm[:]), reads=['ytm'], writes=[('yout', tg)]))

        def res_bufs(final=False):
            bufs = dict(sq=P.sbuf('sq', [128, 512], F32), sq2=P.sbuf('sq2', [128, 512], F32), acc=P.sbuf('acc', [128, 512], F32),
                        rstd=P.sbuf('rstd', [128, 512], F32), XNo=P.sbuf('XNo', [128, 8, 512], BF16))
            if final:
                bufs['yfm'] = P.sbuf('yfm', [128, 8, 512], F32)
                bufs['ytm'] = P.sbuf('ytm', [128, 1024], F32)
            return bufs

        with P.phase():
            bufs = res_bufs()
            Rt = P.sbuf('Rt', [128, 8, 512], F32)
            xin = [P.sbuf('xin%d' % i, [128, 1024], F32) for i in range(2)]
            k = 0
            for tl in cfg.tiles:
                n, t0 = tl['n'], tl['t0']
                for j in range(n // 128):
                    tg = t0 + j * 128
                    src = I['xp'][tg:tg + 128, :] if tl['prompt'] else I['xs'][tg - cfg.TP:tg - cfg.TP + 128, :]
                    xb_, xn_ = xin[k % 2], ('xin', k % 2)
                    k += 1
                    P.dma('sp', ('ldx', k % 2), (xb_[:], src), writes=[xn_])
                    for half in range(2):
                        b = nb()
                        P.mm([('T', ps[b][:, cc * 128:(cc + 1) * 128], xb_[:, (half * 4 + cc) * 128:(half * 4 + cc + 1) * 128], IDF)
                              for cc in range(4)], [xn_, 'cm'], [psn[b]])
                        P.copy(Rt[:, half * 4:(half + 1) * 4, j * 128:(j + 1) * 128],
                               ps[b][:, :].rearrange("p (c t) -> p c t", c=4), [psn[b]], ['Rt'])
                finish_residual(tl, Rt, 'Rt', 0, bufs)

        def ffn(Win, Wout, gi, final):
            with P.phase():
                Ws = [P.sbuf('Wffn%d' % part, [128, 8, 8, 512], BF16) for part in range(2)]
                for part in range(2):
                    for bi in (0, 4, 1, 5, 2, 6, 3, 7):
                        col = (part * 4 + (bi % 4)) * 512 + (DFF if bi >= 4 else 0)
                        P.dma('pool', ('w', bi) if part == 0 else ('w2', bi),
                              (Ws[part][:, bi, :, :], Win[:, col:col + 512].rearrange("(kc p) c -> p kc c", p=128)),
                              writes=[('W', part, bi)])
                XNt = [P.sbuf('XNt%d' % i, [128, 8, 512], BF16) for i in range(2)]
                Gt = [P.sbuf('Gt%d' % i, [128, 16, 512], BF16) for i in range(2)]
                sa = [P.sbuf('sa%d' % i, [128, 512], F32) for i in range(2)]
                it = 0
                for part in range(2):
                    W = Ws[part]
                    for ti, tl in enumerate(cfg.tiles):
                        n, t0 = tl['n'], tl['t0']
                        X, xname = XNt[it % 2], ('XNt', it % 2)
                        G_, gname = Gt[it % 2], ('Gt', it % 2)
                        P.dma('sp', ('ldxn', it % 2), (X[:, :, :n], S['XN'][:, t0:t0 + n].rearrange("(c p) t -> p c t", p=128)),
                              writes=[xname])
                        q = 0
                        for pr in range(4):
                            for ch in range(4):
                                ba, bb = nb(), nb()
                                P.mm([('M', ps[ba][:, :n], W[:, pr, kc, ch * 128:(ch + 1) * 128], X[:, kc, :n], kc == 0, kc == 7)
                                      for kc in range(8)], [('W', part, pr), xname], [psn[ba]])
                                P.mm([('M', ps[bb][:, :n], W[:, 4 + pr, kc, ch * 128:(ch + 1) * 128], X[:, kc, :n], kc == 0, kc == 7)
                                      for kc in range(8)], [('W', part, 4 + pr), xname], [psn[bb]])
                                s_, sn_ = sa[q % 2], ('sa', q % 2)
                                q += 1
                                P.act(s_[:, :n], ps[ba][:, :n], AF.Silu, [psn[ba]], [sn_])
                                P.tt(G_[:, pr * 4 + ch, :n], s_[:, :n], ps[bb][:, :n], ALU.mult, [sn_, psn[bb]], [gname])
                        r0 = part * 2048
                        P.dma('pool', ('stg', it % 2), (S['G'][r0:r0 + 2048, t0:t0 + n].rearrange("(c p) t -> p c t", p=128), G_[:, :, :n]),
                              reads=[gname], writes=[('S_G', part, t0)])
                        it += 1
            with P.phase():
                W = P.sbuf('Wffo', [128, 32, 1024], BF16)
                for bi in range(8):
                    P.dma('pool', ('w', bi), (W[:, bi * 4:(bi + 1) * 4, :], Wout[bi * 512:(bi + 1) * 512, :].rearrange("(kc p) c -> p kc c", p=128)),
                          writes=[('W', bi)])
                wreads = [('W', bi) for bi in range(8)]
                bufs = res_bufs(final)
                Gt = [P.sbuf('Gt%d' % i, [128, 32, 512], BF16) for i in range(2)]
                Rts = [P.sbuf('Rt%d' % i, [128, 8, 512], F32) for i in range(2)]
                pend = [None]
                for ti, tl in enumerate(cfg.tiles):
                    n, t0 = tl['n'], tl['t0']
                    Rt, rn = Rts[ti % 2], ('Rt', ti % 2)
                    P.dma('sp', ('ldR', ti % 2), (Rt[:, :, :n], S['R'][:, t0:t0 + n].rearrange("(c p) t -> p c t", p=128)), writes=[rn])
                    G_, gname = Gt[ti % 2], ('Gt', ti % 2)
                    P.dma('sp', ('ldg', ti % 2), [(G_[:, q * 8:(q + 1) * 8, :n], S['G'][q * 1024:(q + 1) * 1024, t0:t0 + n].rearrange("(c p) t -> p c t", p=128))
                                                  for q in range(4)], writes=[gname])
                    for dm in range(8):
                        b = nb()
                        P.mm([('M', ps[b][:, :n], W[:, kc, dm * 128:(dm + 1) * 128], G_[:, kc, :n], kc == 0, kc == 31)
                              for kc in range(32)], wreads + [gname], [psn[b]])
                        P.stt(Rt[:, dm, :n], ps[b][:, :n], 0.5, Rt[:, dm, :n], ALU.mult, ALU.add, [psn[b], rn], [rn])
                        if dm == 2 and pend[0] is not None:
                            pend[0]()
                            pend[0] = None
                    pend[0] = finish_residual(tl, Rt, rn, gi, bufs, final=final, store_R=not final, defer=True)
                if pend[0] is not None:
                    pend[0]()

        def mixer_inproj(l):
            Win = I['win'][l]
            with P.phase():
                W = P.sbuf('Wxbc', [128, 4, 8, 512], BF16)
                for bi in range(4):
                    col = C_XBC + bi * 512
                    P.dma('pool', ('w', bi), (W[:, bi, :, :], Win[:, col:col + 512].rearrange("(kc p) c -> p kc c", p=128)), writes=[('W', bi)])
                XNt = [P.sbuf('XNt%d' % i, [128, 8, 512], BF16) for i in range(2)]
                carry = P.sbuf('carry', [128, 16, 4, 3], F32)
                raw8 = [P.sbuf('raw%d' % i, [128, 515], F32) for i in range(8)]
                cacc8 = [P.sbuf('cacc%d' % i, [128, 512], F32) for i in range(8)]
                ptmp = P.sbuf('ptmp', [128, 512], F32)
                xc8 = [P.sbuf('xc%d' % i, [128, 512], BF16) for i in range(8)]
                xst = [P.sbuf('xst%d' % i, [128, 4, 1024], BF16) for i in range(2)]
                bst = [P.sbuf('bst%d' % i, [128, 4, 512], BF16) for i in range(2)]
                qn = 0
                for ti, tl in enumerate(cfg.tiles):
                    n, t0, Q, nsub = tl['n'], tl['t0'], tl['Q'], tl['nsub']
                    X, xname = XNt[ti % 2], ('XNt', ti % 2)
                    P.dma('sp', ('ldxn', ti % 2), (X[:, :, :n], S['XN'][:, t0:t0 + n].rearrange("(c p) t -> p c t", p=128)), writes=[xname])
                    if tl['prompt']:
                        segs = [(0, n)]
                        if tl['first']:
                            P.memset(carry[:, :, 0, :], 0.0, [('carry', ct) for ct in range(16)])
                    else:
                        segs = [(s * 64, 64) for s in range(NS)]
                        for s in range(NS):
                            P.dma('sp', 'ldcar', [(carry[:, ct, s, :], I['sconv'][l, s, :, ct * 128:(ct + 1) * 128].rearrange("j p -> p j"))
                                                  for ct in range(16)], writes=[('carry', ct) for ct in range(16)], allow_slow_non_contiguous=True)
                    XS, xsn = xst[ti % 2], ('xst', ti % 2)
                    BS, bsn = bst[ti % 2], ('bst', ti % 2)
                    bankd = {}

                    def bufs_of(cg):
                        o8 = (cg % 2) * 4
                        return o8, raw8[o8:o8 + 4], cacc8[o8:o8 + 4], xc8[o8:o8 + 4], [cg * 4 + k for k in range(4)]

                    def st_mm(cg):
                        o8, raw4, cacc4, xc, cts = bufs_of(cg)
                        bankd[cg] = []
                        for k, ct in enumerate(cts):
                            b = nb()
                            bankd[cg].append(b)
                            bi, ch = ct // 4, ct % 4
                            P.mm([('M', ps[b][:, :n], W[:, bi, kc, ch * 128:(ch + 1) * 128], X[:, kc, :n], kc == 0, kc == 7)
                                  for kc in range(8)], [('W', bi), xname], [psn[b]])

                    def st_a(cg, si):
                        o8, raw4, cacc4, xc, cts = bufs_of(cg)
                        c0, ln = segs[si]
                        banks = bankd[cg]
                        for k, ct in enumerate(cts):
                            P.copy(raw4[k][:, 3:3 + ln], ps[banks[k]][:, c0:c0 + ln], [psn[banks[k]]], [('raw', o8 + k)], eng='act')
                        for k, ct in enumerate(cts):
                            P.copy(raw4[k][:, 0:3], carry[:, ct, si, :], [('carry', ct)], [('raw', o8 + k)], eng='dve')
                        for k, ct in enumerate(cts):
                            P.act(cacc4[k][:, :ln], raw4[k][:, 0:ln], AF.Identity, [('raw', o8 + k), 'convw', 'convb'], [('cacc', o8 + k)],
                                  scale=convw[:, l, ct, 0:1], bias=convb[:, l, ct:ct + 1])

                    def st_b(cg, si):
                        o8, raw4, cacc4, xc, cts = bufs_of(cg)
                        c0, ln = segs[si]
                        for j in range(1, 4):
                            for k, ct in enumerate(cts):
                                P.stt(cacc4[k][:, :ln], raw4[k][:, j:j + ln], convw[:, l, ct, j:j + 1], cacc4[k][:, :ln], ALU.mult, ALU.add,
                                      [('raw', o8 + k), ('cacc', o8 + k), 'convw'], [('cacc', o8 + k)], eng='dve')
                        for k, ct in enumerate(cts):
                            P.copy(carry[:, ct, si, :], raw4[k][:, ln:ln + 3], [('raw', o8 + k)], [('carry', ct)], eng='dve')
                        for k, ct in enumerate(cts):
                            P.act(xc[k][:, c0:c0 + ln], cacc4[k][:, :ln], AF.Silu, [('cacc', o8 + k)], [('xc', o8 + k)])

                    def st_c(cg):
                        o8, raw4, cacc4, xc, cts = bufs_of(cg)
                        for k, ct in enumerate(cts):
                            xc_, xcn = xc[k], ('xc', o8 + k)
                            if ct < 12:
                                b2 = nb()
                                psb = ps[b2][:].bitcast(BF16)
                                P.mm([('T', psb[:Q, j * 128:(j + 1) * 128], xc_[:, j * Q:(j + 1) * Q], idb[:]) for j in range(nsub)],
                                     [xcn, 'idb'], [psn[b2]])
                                src = psb[:Q, 0:nsub * 128].rearrange("p (j c) -> p j c", j=nsub)
                                if ct < 8:
                                    P.copy(XS[:Q, :nsub, ct * 128:(ct + 1) * 128], src, [psn[b2]], [xsn], eng='act')
                                else:
                                    P.copy(BS[:Q, :nsub, (ct - 8) * 128:(ct - 7) * 128], src, [psn[b2]], [bsn], eng='act')
                            if ct >= 8:
                                dst = S['Bfm'] if ct < 12 else S['Cfm']
                                g = (ct - 8) % 4
                                P.dma('pool', ('stfm', ct % 8), (dst[g * 128:(g + 1) * 128, t0:t0 + n], xc_[:, :n]), reads=[xcn],
                                      writes=[('S_fm', ct, t0)])

                    if len(segs) == 1:
                        st_mm(0)
                        st_a(0, 0)
                        for cg in range(4):
                            if cg + 1 < 4:
                                st_mm(cg + 1)
                                st_a(cg + 1, 0)
                            st_b(cg, 0)
                            st_c(cg)
                    else:
                        for cg in range(4):
                            st_mm(cg)
                            for si in range(len(segs)):
                                st_a(cg, si)
                                st_b(cg, si)
                            st_c(cg)
                    P.dma('pool', ('stx', ti % 2), (S['xtm'][t0:t0 + n, :].rearrange("(j p) c -> p j c", p=Q), XS[:Q, :nsub, :]),
                          reads=[xsn], writes=[('S_xtm', t0)])
                    P.dma('pool', ('stb', ti % 2), (S['Btm'][t0:t0 + n, :].rearrange("(j p) c -> p j c", p=Q), BS[:Q, :nsub, :]),
                          reads=[bsn], writes=[('S_Btm', t0)])
                    if tl['last']:
                        for si in range(len(segs)):
                            if tl['prompt']:
                                dst = O['convp'][l, tl['seq']]
                            else:
                                dst = O['convs'][l, si]
                            out_toks.append(P.dma('pool', 'stcar', [(dst[:, ct * 128:(ct + 1) * 128].rearrange("j p -> p j"), carry[:, ct, si, :])
                                                                  for ct in range(16)], reads=[('carry', ct) for ct in range(16)], writes=[('convout', ti, si)],
                                                  allow_slow_non_contiguous=True))
            if getattr(cfg, 'sub', 9) < 2:
                return
            with P.phase():
                W = P.sbuf('Wfm', [128, 4, 8, 512], BF16)
                cols = [C_AQ, C_AK, C_MQ, C_MK]
                dsts = [S['q'], S['k'], S['mq'], S['mkf']]
                for bi in range(4):
                    P.dma('pool', ('w', bi), (W[:, bi, :, :], Win[:, cols[bi]:cols[bi] + 512].rearrange("(kc p) c -> p kc c", p=128)), writes=[('W', bi)])
                XNt = [P.sbuf('XNt%d' % i, [128, 8, 512], BF16) for i in range(2)]
                fst = [P.sbuf('fst%d' % i, [128, 4, 512], BF16) for i in range(3)]
                q = 0
                for ti, tl in enumerate(cfg.tiles):
                    n, t0 = tl['n'], tl['t0']
                    X, xname = XNt[ti % 2], ('XNt', ti % 2)
                    P.dma('sp', ('ldxn', ti % 2), (X[:, :, :n], S['XN'][:, t0:t0 + n].rearrange("(c p) t -> p c t", p=128)), writes=[xname])
                    for bi in range(4):
                        F_, fn_ = fst[q % 3], ('fst', q % 3)
                        for ch in range(4):
                            b = nb()
                            P.mm([('M', ps[b][:, :n], W[:, bi, kc, ch * 128:(ch + 1) * 128], X[:, kc, :n], kc == 0, kc == 7)
                                  for kc in range(8)], [('W', bi), xname], [psn[b]])
                            P.copy(F_[:, ch, :n], ps[b][:, :n], [psn[b]], [fn_])
                        P.dma('pool', ('stf', q % 3), (dsts[bi][:, t0:t0 + n].rearrange("(c p) t -> p c t", p=128), F_[:, :, :n]),
                              reads=[fn_], writes=[('S_f', bi, t0)])
                        q += 1
            if getattr(cfg, 'sub', 9) < 3:
                return
            with P.phase():
                W = P.sbuf('Wtm', [128, 7, 8, 512], BF16)
                cols = [C_Z, C_Z + 512, C_AV, C_AK, C_MK, C_MV, C_MO]
                for bi in range(7):
                    P.dma('pool', ('w', bi), (W[:, bi, :, :], Win[:, cols[bi]:cols[bi] + 512].rearrange("(kc p) c -> p kc c", p=128)), writes=[('W', bi)])
                Wsm = P.sbuf('Wsm', [128, 8, 24], BF16)
                P.dma('pool', 'wsm', [(Wsm[:, :, 0:16], Win[:, C_DT:C_DT + 16].rearrange("(kc p) c -> p kc c", p=128)),
                                      (Wsm[:, :, 16:24], Win[:, C_MI:C_MI + 8].rearrange("(kc p) c -> p kc c", p=128))],
                      writes=['Wsm'], allow_slow_non_contiguous=True)
                XNt = [P.sbuf('XNt%d' % i, [128, 8, 512], BF16) for i in range(2)]
                zst = [P.sbuf('zst%d' % i, [128, 4, 1024], BF16) for i in range(2)]
                vst = [P.sbuf('vst%d' % i, [128, 4, 8, 65], BF16) for i in range(2)]
                mkst = [P.sbuf('mkst%d' % i, [128, 4, 512], BF16) for i in range(2)]
                mvst = [P.sbuf('mvst%d' % i, [128, 4, 4, 129], BF16) for i in range(2)]
                most = [P.sbuf('most%d' % i, [128, 4, 512], BF16) for i in range(2)]
                smst = [P.sbuf('smst%d' % i, [128, 4, 24], F32) for i in range(2)]
                kout = P.sbuf('kout', [128, 4, 512], F32)
                vout = P.sbuf('vout', [128, 4, 512], F32)
                for i in range(2):
                    if os.environ.get('DBG_B') == '1':
                        break
                    P.memset(vst[i][:, :, :, 64:65], 1.0, [('vst', i)])
                    P.memset(mvst[i][:, :, :, 128:129], 1.0, [('mvst', i)])
                for ti, tl in enumerate(cfg.tiles):
                    n, t0, Q, nsub = tl['n'], tl['t0'], tl['Q'], tl['nsub']
                    pz = ti % 2
                    X, xname = XNt[pz], ('XNt', pz)
                    P.dma('sp', ('ldxn', pz), (X[:, :, :n], S['XN'][:, t0:t0 + n].rearrange("(c p) t -> p c t", p=128)), writes=[xname])
                    need_kv = tl['last'] if tl['prompt'] else True
                    if os.environ.get('DBG_C') == '1':
                        need_kv = False
                    for j in range(nsub):
                        xs_ = lambda kc: X[:, kc, j * Q:(j + 1) * Q]
                        for bi in range(7):
                            if bi == 3 and (not need_kv or os.environ.get('DBG_C') == '3'):
                                continue
                            b = nb()
                            P.mm([('M', ps[b][:Q, :], xs_(kc), W[:, bi, kc, :], kc == 0, kc == 7) for kc in range(8)],
                                 [('W', bi), xname], [psn[b]])
                            src = ps[b][:Q, :]
                            if bi < 2:
                                P.act(zst[pz][:Q, j, bi * 512:(bi + 1) * 512], src, AF.Silu, [psn[b]], [('zst', pz)])
                            elif bi == 2:
                                P.copy(vst[pz][:Q, j, :, 0:64], src.rearrange("p (h d) -> p h d", h=8), [psn[b]], [('vst', pz)], eng='dve')
                                if need_kv and os.environ.get('DBG_C') != '2' and os.environ.get('DBG_D') != '1':
                                    P.copy(vout[:Q, j, :], src, [psn[b]], ['vout'], eng='dve')
                            elif bi == 3:
                                P.copy(kout[:Q, j, :], src, [psn[b]], ['kout'])
                            elif bi == 4:
                                P.copy(mkst[pz][:Q, j, :], src, [psn[b]], [('mkst', pz)])
                            elif bi == 5:
                                P.copy(mvst[pz][:Q, j, :, 0:128], src.rearrange("p (h d) -> p h d", h=4), [psn[b]], [('mvst', pz)])
                            else:
                                P.act(most[pz][:Q, j, :], src, AF.Sigmoid, [psn[b]], [('most', pz)])
                        if os.environ.get('DBG_A') != '1':
                            b = nb()
                            P.mm([('M', ps[b][:Q, :24], xs_(kc), Wsm[:, kc, :], kc == 0, kc == 7) for kc in range(8)],
                                 ['Wsm', xname], [psn[b]])
                            P.copy(smst[pz][:Q, j, :], ps[b][:Q, :24], [psn[b]], [('smst', pz)])
                    tmv = lambda d: d[t0:t0 + n, :].rearrange("(j p) c -> p j c", p=Q)
                    P.dma('pool', ('st0', pz), (tmv(S['z']), zst[pz][:Q, :nsub, :]), reads=[('zst', pz)], writes=[('S_z', t0)])
                    P.dma('pool', ('st1', pz), (tmv(S['v']), vst[pz][:Q, :nsub].rearrange("p j h d -> p j (h d)")), reads=[('vst', pz)], writes=[('S_v', t0)])
                    P.dma('pool', ('st2', pz), (tmv(S['mkt']), mkst[pz][:Q, :nsub, :]), reads=[('mkst', pz)], writes=[('S_mkt', t0)])
                    P.dma('pool', ('st3', pz), (tmv(S['mv']), mvst[pz][:Q, :nsub].rearrange("p j h d -> p j (h d)")), reads=[('mvst', pz)], writes=[('S_mv', t0)])
                    P.dma('pool', ('st4', pz), (tmv(S['mo']), most[pz][:Q, :nsub, :]), reads=[('most', pz)], writes=[('S_mo', t0)])
                    P.dma('pool', ('st5', pz), (tmv(S['sm']), smst[pz][:Q, :nsub, :]), reads=[('smst', pz)], writes=[('S_sm', t0)])
                    if need_kv:
                        if tl['prompt']:
                            dk = O['kp'][l, tl['seq']].rearrange("(j p) c -> p j c", p=128)
                            dv = O['vp'][l, tl['seq']].rearrange("(j p) c -> p j c", p=128)
                        else:
                            dk = O['ks'][l].rearrange("s p c -> p s c")
                            dv = O['vs'][l].rearrange("s p c -> p s c")
                        if os.environ.get('DBG_C') != '3':
                            out_toks.append(P.dma('pool', 'stk', (dk, kout[:Q, :nsub, :]), reads=['kout'], writes=[('kout_d', ti)]))
                        if os.environ.get('DBG_D') == '2':
                            dv = dk
                        if os.environ.get('DBG_C') != '2':
                            out_toks.append(P.dma('pool', 'stv', (dv, vout[:Q, :nsub, :]), reads=['vout'], writes=[('vout_d', ti)]))

        def mixer_core(l):
            with P.phase():
                ssdn = P.sbuf('ssdn', [128, 1024], F32)
                mln = P.sbuf('mln', [128, 512], F32)
                ab = P.sbuf('ab', [128, 4, 8, 128], F32)
                P.dma('sp', 'ldp', [(ssdn[:], I['ssdn'][:, l, :]), (mln[:], I['mln'][:, l, :]), (ab[:], I['abias'][l])],
                      writes=['ssdn', 'mln', 'ab'])
                xt = P.sbuf('xt', [128, 4, 1024], BF16)
                Bt = P.sbuf('Bt', [128, 4, 512], BF16)
                zt = P.sbuf('zt', [128, 4, 1024], BF16)
                mkt = P.sbuf('mkt', [128, 4, 512], BF16)
                mvt = P.sbuf('mvt', [128, 4, 4, 129], BF16)
                mot = P.sbuf('mot', [128, 4, 512], BF16)
                smt = P.sbuf('smt', [128, 4, 24], F32)
                Bf = P.sbuf('Bf', [128, 4, 512], BF16)
                Cf = P.sbuf('Cf', [128, 4, 512], BF16)
                qf = P.sbuf('qf', [128, 4, 512], BF16)
                mqf = P.sbuf('mqf', [128, 4, 512], BF16)
                mkf = P.sbuf('mkf', [128, 4, 512], BF16)
                KH = P.sbuf('KH', [128, 4, 8, 128], BF16)
                VH = P.sbuf('VH', [128, 8, 8, 65], BF16)
                h32 = P.sbuf('h32', [128, 1024], F32)
                hbf = P.sbuf('hbf', [128, 1024], BF16)
                Cn32 = P.sbuf('Cn32', [128, 4, 129], F32)
                Cnbf = P.sbuf('Cnbf', [128, 4, 129], BF16)
                mbc = P.sbuf('mbc', [128, 4], F32)
                dtt = P.sbuf('dtt', [128, 4, 16], F32)
                dat = P.sbuf('dat', [128, 4, 16], F32)
                tmp16 = P.sbuf('tmp16', [128, 4, 16], F32)
                tmp16b = P.sbuf('tmp16b', [128, 4, 16], F32)
                lft = P.sbuf('lft', [128, 4, 4], F32)
                igt = P.sbuf('igt', [128, 4, 4], F32)
                tmp4 = P.sbuf('tmp4', [128, 4, 4], F32)
                tmp4b = P.sbuf('tmp4b', [128, 4, 4], F32)
                R1 = [P.sbuf('R1_%d' % i, [128, 4, 128], F32) for i in range(2)]
                Eg = [P.sbuf('Eg%d' % i, [128, 4, 128], BF16) for i in range(2)]
                wT = [P.sbuf('wT%d' % i, [128, 4, 128], BF16) for i in range(2)]
                CBM = P.sbuf('CBM', [128, 4, 128], BF16)
                xdt = P.sbuf('xdt', [128, 1024], BF16)
                xD = P.sbuf('xD', [128, 1024], BF16)
                xw = P.sbuf('xw', [128, 1024], BF16)
                ecum = P.sbuf('ecum', [128, 16], F32)
                wd = P.sbuf('wd', [128, 16], F32)
                dec = P.sbuf('dec', [128, 16], F32)
                t1 = P.sbuf('t1', [128, 1024], F32)
                t2 = P.sbuf('t2', [128, 1024], F32)
                ss1 = P.sbuf('ss1', [128, 1], F32)
                ytm = P.sbuf('ytm', [128, 2048], BF16)
                Am = P.sbuf('Am', [128, 4, 128], F32)
                Bg = P.sbuf('Bg', [128, 4, 128], F32)
                LD = P.sbuf('LD', [128, 4, 128], F32)
                Dm = P.sbuf('Dm', [128, 4, 128], BF16)
                Sm = P.sbuf('Sm', [128, 4, 128], BF16)
                STm = P.sbuf('STm', [128, 4, 128], BF16)
                tot = P.sbuf('tot', [128, 4, 129], F32)
                hh = P.sbuf('hh', [128, 4, 128], F32)
                kw = P.sbuf('kw', [128, 4, 128], BF16)
                s4 = {nm: P.sbuf('s4' + nm, [128, 4], F32) for nm in
                      ['mx', 'b', 'inter', 'mt', 'nmt', 'ei', 'emt', 'den', 'ssq', 'rstd', 'wlog', 'w', 'mnew', 'blast', 'decay']}
                sb = [P.sbuf('sb%d' % i, [128, 512], F32) for i in range(2)]
                Pex = [P.sbuf('Pex%d' % i, [128, 8, 128], BF16) for i in range(2)]
                rs8 = P.sbuf('rs8', [128, 8], F32)
                Yst = [P.sbuf('Yst%d' % i, [128, 16, 512], BF16) for i in range(1)]
                stf = P.sbuf('stf', [128, 8, 128], F32)
                cst = P.sbuf('cst', [128, 4, 512], F32)
                cstb = P.sbuf('cstb', [128, 4, 512], BF16)

                def s4v(nm, Q):
                    return s4[nm][:Q, :]

                def load_tile(tl):
                    n, t0, Q, nsub = tl['n'], tl['t0'], tl['Q'], tl['nsub']
                    tmv = lambda d: d[t0:t0 + n, :].rearrange("(j p) c -> p j c", p=Q)
                    fmv = lambda d: d[:, t0:t0 + n].rearrange("(c p) t -> p c t", p=128)
                    P.dma('sp', 'lc0', (xt[:Q, :nsub, :], tmv(S['xtm'])), reads=[('S_xtm', t0)], writes=['xt'])
                    P.dma('sp', 'lc1', (Bt[:Q, :nsub, :], tmv(S['Btm'])), writes=['Bt'])
                    P.dma('sp', 'lc2', (zt[:Q, :nsub, :], tmv(S['z'])), writes=['zt'])
                    P.dma('sp', 'lc3', (mkt[:Q, :nsub, :], tmv(S['mkt'])), writes=['mkt'])
                    P.dma('sp', 'lc4', (mvt[:Q, :nsub].rearrange("p j h d -> p j (h d)"), tmv(S['mv'])), writes=['mvt'])
                    P.dma('sp', 'lc5', (mot[:Q, :nsub, :], tmv(S['mo'])), writes=['mot'])
                    P.dma('sp', 'lc6', (smt[:Q, :nsub, :], tmv(S['sm'])), writes=['smt'])
                    P.dma('sp', 'lc7', (Bf[:, :, :n], fmv(S['Bfm'])), writes=['Bf'])
                    P.dma('sp', 'lc8', (Cf[:, :, :n], fmv(S['Cfm'])), writes=['Cf'])
                    P.dma('sp', 'lc9', (qf[:, :, :n], fmv(S['q'])), writes=['qf'])
                    P.dma('sp', 'lc10', (mqf[:, :, :n], fmv(S['mq'])), writes=['mqf'])
                    P.dma('sp', 'lc11', (mkf[:, :, :n], fmv(S['mkf'])), writes=['mkf'])

                def prep_small(tl):
                    Q, nsub = tl['Q'], tl['nsub']
                    v16 = lambda t: t[:Q, :nsub, :]
                    rb = lambda i: rep16[:Q, l, i, :].unsqueeze(1).to_broadcast([Q, nsub, 16])
                    P.tt(v16(tmp16), smt[:Q, :nsub, 0:16], rb(0), ALU.add, ['smt', 'rep16'], ['tmp16'])
                    P.ts(v16(tmp16b), v16(tmp16), -1.0, None, ALU.mult, None, ['tmp16'], ['tmp16b'])
                    P.tt(v16(tmp16b), v16(tmp16b), v16(tmp16), ALU.max, ['tmp16', 'tmp16b'], ['tmp16b'])
                    P.act(v16(tmp16b), v16(tmp16b), AF.Exp, ['tmp16b'], ['tmp16b'], scale=-1.0)
                    P.act(v16(tmp16b), v16(tmp16b), AF.Ln, ['tmp16b', 'cm'], ['tmp16b'], bias=ONES[:Q, 0:1])
                    P.ts(v16(tmp16), v16(tmp16), 0.0, None, ALU.max, None, ['tmp16'], ['tmp16'])
                    P.tt(v16(dtt), v16(tmp16), v16(tmp16b), ALU.add, ['tmp16', 'tmp16b'], ['dtt'])
                    P.tt(v16(dat), v16(dtt), arep[:Q, l, :].unsqueeze(1).to_broadcast([Q, nsub, 16]), ALU.mult, ['dtt', 'arep'], ['dat'])
                    v4 = lambda t: t[:Q, :nsub, :]
                    gb = lambda o: gbr[:Q, l, o:o + 4].unsqueeze(1).to_broadcast([Q, nsub, 4])
                    P.tt(v4(igt), smt[:Q, :nsub, 16:20], gb(0), ALU.add, ['smt', 'gbr'], ['igt'])
                    P.tt(v4(tmp4), smt[:Q, :nsub, 20:24], gb(4), ALU.add, ['smt', 'gbr'], ['tmp4'])
                    P.ts(v4(tmp4b), v4(tmp4), -1.0, None, ALU.mult, None, ['tmp4'], ['tmp4b'])
                    P.tt(v4(tmp4b), v4(tmp4b), v4(tmp4), ALU.max, ['tmp4', 'tmp4b'], ['tmp4b'])
                    P.act(v4(tmp4b), v4(tmp4b), AF.Exp, ['tmp4b'], ['tmp4b'], scale=-1.0)
                    P.act(v4(tmp4b), v4(tmp4b), AF.Ln, ['tmp4b', 'cm'], ['tmp4b'], bias=ONES[:Q, 0:1])
                    P.ts(v4(tmp4), v4(tmp4), 0.0, None, ALU.min, None, ['tmp4'], ['tmp4'])
                    P.tt(v4(lft), v4(tmp4), v4(tmp4b), ALU.subtract, ['tmp4', 'tmp4b'], ['lft'])

                def init_state_zero():
                    P.memset(h32[:], 0.0, ['h32'])
                    P.memset(hbf[:], 0.0, ['hbf'])
                    P.memset(Cn32[:], 0.0, ['Cn32'])
                    P.memset(Cnbf[:], 0.0, ['Cnbf'])
                    P.memset(mbc[:], 0.0, ['mbc'])

                def init_state_sample(s):
                    P.dma('sp', 'ls0', (stf[:], I['sssd'][l, s].rearrange("(c p) n -> p c n", p=128)), writes=['stf'])
                    for half in range(2):
                        b = nb()
                        P.mm([('T', ps[b][:, cc * 128:(cc + 1) * 128], stf[:, half * 4 + cc, :], IDF) for cc in range(4)],
                             ['stf', 'cm'], [psn[b]])
                        P.copy(h32[:, half * 512:(half + 1) * 512], ps[b][:, :], [psn[b]], ['h32'])
                    P.copy(hbf[:], h32[:], ['h32'], ['hbf'])
                    P.dma('sp', 'ls1', (Cn32[:, :, 0:128], I['smc'][l, s].rearrange("h k v -> k h v")), writes=['Cn32'])
                    P.dma('sp', 'ls2', (Cn32[:, :, 128:129], I['smn'][l, s].rearrange("h (k o) -> k h o", o=1)), writes=['Cn32'],
                          allow_slow_non_contiguous=True)
                    P.copy(Cnbf[:], Cn32[:], ['Cn32'], ['Cnbf'])
                    P.dma('sp', 'ls3', (mbc[:], I['smm'][l, s].partition_broadcast(128)), writes=['mbc'])

                def store_state(prompt, s):
                    od = (O['ssdp'], O['mcp'], O['mnp'], O['mmp']) if prompt else (O['ssds'], O['mcs'], O['mns'], O['mms'])
                    for half in range(2):
                        b = nb()
                        P.mm([('T', ps[b][:, cc * 128:(cc + 1) * 128], h32[:, (half * 4 + cc) * 128:(half * 4 + cc + 1) * 128], IDF)
                              for cc in range(4)], ['h32', 'cm'], [psn[b]])
                        P.copy(stf[:, half * 4:(half + 1) * 4, :], ps[b][:, :].rearrange("p (c n) -> p c n", c=4), [psn[b]], ['stf'])
                    key = (prompt, s, l)
                    out_toks.append(P.dma('pool', 'ss0', (od[0][l, s].rearrange("(c p) n -> p c n", p=128), stf[:]), reads=['stf'],
                                          writes=[('o_ssd', key)]))
                    out_toks.append(P.dma('pool', 'ss1', (od[1][l, s].rearrange("h k v -> k h v"), Cn32[:, :, 0:128]), reads=['Cn32'],
                                          writes=[('o_c', key)]))
                    out_toks.append(P.dma('pool', 'ss2', (od[2][l, s].rearrange("h (k o) -> k h o", o=1), Cn32[:, :, 128:129]), reads=['Cn32'],
                                          writes=[('o_n', key)], allow_slow_non_contiguous=True))
                    out_toks.append(P.dma('pool', 'ss3', (od[3][l, s:s + 1, :], mbc[0:1, :]), reads=['mbc'], writes=[('o_m', key)]))

                def ssd_chunk(Q, j, c0):
                    Uq, SUq, Iq = U[:Q, :Q], SU[:Q, :Q], IDF[:Q, :Q]
                    da_j = dat[:Q, j, :]
                    b = 0
                    P.mm([('M', ps[b][:Q, 0:16], Uq, da_j, True, True),
                          ('M', ps[b][:Q, 16:32], SUq, da_j, True, True),
                          ('M', ps[b][:, 32:48], ONES[:Q, :], da_j, True, True)], ['cm', 'dat'], [psn[b]])
                    x3 = xt[:Q, j, :].rearrange("p (h d) -> p h d", h=16)
                    bc16 = lambda a: a.unsqueeze(2).to_broadcast([Q, 16, 64])
                    P.tt(xdt[:Q, :].rearrange("p (h d) -> p h d", h=16), x3, bc16(dtt[:Q, j, :]), ALU.mult, ['xt', 'dtt'], ['xdt'])
                    P.tt(xD[:Q, :].rearrange("p (h d) -> p h d", h=16), x3, bc16(rep16[:Q, l, 2, :]), ALU.mult, ['xt', 'rep16'], ['xD'])
                    yield
                    P.act(ecum[:Q, :], ps[b][:Q, 0:16], AF.Exp, [psn[b]], ['ecum'])
                    P.act(wd[:Q, :], ps[b][:Q, 16:32], AF.Exp, [psn[b]], ['wd'])
                    P.act(dec[:, :], ps[b][:, 32:48], AF.Exp, [psn[b]], ['dec'])
                    b = 1
                    P.mm([('M', ps[b][:Q, g * 128:g * 128 + Q], Bf[:, g, c0:c0 + Q], Cf[:, g, c0:c0 + Q], True, True) for g in range(4)],
                         ['Bf', 'Cf'], [psn[b]])
                    yield
                    P.tt(CBM[:Q, :, :Q], ps[b][:Q, :].rearrange("p (g t) -> p g t", g=4)[:, :, :Q],
                         Uq.unsqueeze(1).to_broadcast([Q, 4, Q]), ALU.mult, [psn[b], 'cm'], ['CBM'])
                    P.tt(xw[:Q, :].rearrange("p (h d) -> p h d", h=16), xdt[:Q, :].rearrange("p (h d) -> p h d", h=16), bc16(wd[:Q, :]),
                         ALU.mult, ['xdt', 'wd'], ['xw'])
                    bi_ = [2, 0]
                    P.mm([('M', ps[bi_[g // 2]][:Q, (g % 2) * 256:(g % 2 + 1) * 256], Cf[:, g, c0:c0 + Q], hbf[:, g * 256:(g + 1) * 256], True, True)
                          for g in range(4)], ['Cf', 'hbf'], [psn[bi_[0]], psn[bi_[1]]])
                    yield
                    for hf in range(2):
                        sl = slice(hf * 512, (hf + 1) * 512)
                        P.tt(t1[:Q, sl].rearrange("p (h d) -> p h d", h=8), ps[bi_[hf]][:Q, :].rearrange("p (h d) -> p h d", h=8),
                             ecum[:Q, hf * 8:(hf + 1) * 8].unsqueeze(2).to_broadcast([Q, 8, 64]), ALU.mult, [psn[bi_[hf]], 'ecum'], ['t1'])
                    by = [1, 2]
                    P.mm([('M', ps[by[hf]][:Q, :], idb[:Q, :Q], xD[:Q, hf * 512:(hf + 1) * 512], True, False) for hf in range(2)],
                         ['idb', 'xD'], [psn[by[0]], psn[by[1]]])
                    for g in range(4):
                        r1, r1n = R1[g % 2], ('R1', g % 2)
                        P.tt(r1[:Q, :, :Q], Uq.unsqueeze(1).to_broadcast([Q, 4, Q]),
                             dat[:Q, j, g * 4:(g + 1) * 4].unsqueeze(2).to_broadcast([Q, 4, Q]), ALU.mult, ['cm', 'dat'], [r1n])
                        yield
                        eg, egn = Eg[g % 2], ('Eg', g % 2)
                        w_, wn = wT[g % 2], ('wT', g % 2)
                        b = 0
                        P.mm([('M', ps[b][:Q, r * 128:r * 128 + Q], SUq, r1[:Q, r, :Q], True, True) for r in range(4)],
                             ['cm', r1n], [psn[b]])
                        yield
                        P.act(eg[:Q, :, :Q], ps[b][:Q, :].rearrange("p (r t) -> p r t", r=4)[:, :, :Q], AF.Exp, [psn[b]], [egn])
                        yield
                        P.tt(w_[:Q, :, :Q], eg[:Q, :, :Q], CBM[:Q, g, :Q].unsqueeze(1).to_broadcast([Q, 4, Q]), ALU.mult,
                             [egn, 'CBM'], [wn])
                        yield
                        hf = g // 2
                        P.mm([('M', ps[by[hf]][:Q, ((g % 2) * 4 + r) * 64:((g % 2) * 4 + r + 1) * 64], w_[:Q, r, :Q],
                               xdt[:Q, (g * 4 + r) * 64:(g * 4 + r + 1) * 64], False, (g % 2 == 1 and r == 3)) for r in range(4)],
                             [wn, 'xdt'], [psn[by[hf]]])
                    yield
                    for hf in range(2):
                        sl = slice(hf * 512, (hf + 1) * 512)
                        P.tt(t1[:Q, sl], ps[by[hf]][:Q, :], t1[:Q, sl], ALU.add, [psn[by[hf]], 't1'], ['t1'])
                    bs_ = [0, 1]
                    P.mm([('M', ps[bs_[g // 2]][:, (g % 2) * 256:(g % 2 + 1) * 256], Bt[:Q, j, g * 128:(g + 1) * 128], xw[:Q, g * 256:(g + 1) * 256], True, True)
                          for g in range(4)], ['Bt', 'xw'], [psn[bs_[0]], psn[bs_[1]]])
                    P.tt(t2[:Q, :], t1[:Q, :], zt[:Q, j, :], ALU.mult, ['t1', 'zt'], ['t2'])
                    P.memset(ss1[:Q, :], 0.0, ['ss1'])
                    yield
                    P.act(t1[:Q, :], t2[:Q, :], AF.Square, ['t2', 'ss1'], ['t1', 'ss1'], accum_out=ss1[:Q, :])
                    P.act(ss1[:Q, :], ss1[:Q, :], AF.Sqrt, ['ss1', 'epst'], ['ss1'], bias=epst[:Q, 0:1], scale=1.0 / 1024)
                    P.tt(h32[:, :].rearrange("p (h d) -> p h d", h=16), h32[:, :].rearrange("p (h d) -> p h d", h=16),
                         dec[:, :].unsqueeze(2).to_broadcast([128, 16, 64]), ALU.mult, ['h32', 'dec'], ['h32'])
                    for hf in range(2):
                        sl = slice(hf * 512, (hf + 1) * 512)
                        P.tt(h32[:, sl], h32[:, sl], ps[bs_[hf]][:, :], ALU.add, ['h32', psn[bs_[hf]]], ['h32'])
                    yield
                    P.copy(hbf[:], h32[:], ['h32'], ['hbf'], eng='act')
                    P.op('dve', lambda e: e.reciprocal(out=ss1[:Q, :], in_=ss1[:Q, :]), ['ss1'], ['ss1'])
                    yield
                    P.stt(ytm[:Q, 0:1024], t2[:Q, :], ss1[:Q, 0:1], ssdn[:Q, :], ALU.mult, ALU.mult, ['t2', 'ss1', 'ssdn'], ['ytm_s'])
                    yield

                def mlstm_chunk(Q, j, c0):
                    Uq, SUq, Iq = U[:Q, :Q], SU[:Q, :Q], IDF[:Q, :Q]
                    EL = (EL128 if Q == 128 else EL64)[:Q, :]
                    lf_j, ig_j = lft[:Q, j, :], igt[:Q, j, :]
                    sc = 128 ** -0.5
                    P.tt(Am[:Q, :, :Q], Uq.unsqueeze(1).to_broadcast([Q, 4, Q]), lf_j.unsqueeze(2).to_broadcast([Q, 4, Q]), ALU.mult,
                         ['cm', 'lft'], ['Am'])
                    P.tt(Bg[:Q, :, :Q], ONES[:Q, :Q].unsqueeze(1).to_broadcast([Q, 4, Q]), ig_j.unsqueeze(2).to_broadcast([Q, 4, Q]), ALU.mult,
                         ['cm', 'igt'], ['Bg'])
                    b2 = 4
                    P.mm([('M', ps[b2][:Q, 0:4], Uq, lf_j, True, True), ('M', ps[b2][:Q, 4:8], SUq, lf_j, True, True)],
                         ['cm', 'lft'], [psn[b2]])
                    yield
                    b = 3
                    mms = []
                    for h in range(4):
                        mms.append(('M', ps[b][:Q, h * 128:h * 128 + Q], Am[:Q, h, :Q], SUq, True, False))
                        mms.append(('M', ps[b][:Q, h * 128:h * 128 + Q], Bg[:Q, h, :Q], Iq, False, True))
                    P.mm(mms, ['Am', 'Bg', 'cm'], [psn[b]])
                    P.copy(s4v('b', Q), ps[b2][:Q, 0:4], [psn[b2]], ['b'], eng='dve')
                    P.tt(s4v('wlog', Q), ps[b2][:Q, 4:8], ig_j, ALU.add, [psn[b2], 'igt'], ['wlog'])
                    P.tt(s4v('inter', Q), s4v('b', Q), mbc[:Q, :], ALU.add, ['b', 'mbc'], ['inter'])
                    yield
                    P.tt(LD[:Q, :, :Q], ps[b][:Q, :].rearrange("p (h s) -> p h s", h=4)[:, :, :Q], MN[:Q, :Q].unsqueeze(1).to_broadcast([Q, 4, Q]),
                         ALU.add, [psn[b], 'cm'], ['LD'])
                    b3 = 3
                    P.mm([('M', ps[b3][:Q, h * 128:h * 128 + Q], mqf[:, h, c0:c0 + Q], mkf[:, h, c0:c0 + Q], True, True) for h in range(4)],
                         ['mqf', 'mkf'], [psn[b3]])
                    yield
                    P.op('dve', lambda e: e.tensor_reduce(out=s4v('mx', Q), in_=LD[:Q, :, :Q], axis=AX.X, op=ALU.max), ['LD'], ['mx'])
                    yield
                    P.tt(s4v('mt', Q), s4v('mx', Q), s4v('inter', Q), ALU.max, ['mx', 'inter'], ['mt'])
                    yield
                    P.ts(s4v('nmt', Q), s4v('mt', Q), -1.0, None, ALU.mult, None, ['mt'], ['nmt'])
                    P.tt(s4v('ei', Q), s4v('inter', Q), s4v('mt', Q), ALU.subtract, ['inter', 'mt'], ['ei'])
                    yield
                    for h in range(4):
                        P.act(Dm[:Q, h, :Q], LD[:Q, h, :Q], AF.Exp, ['LD', 'nmt'], ['Dm'], bias=s4['nmt'][:Q, h:h + 1])
                    P.act(s4v('ei', Q), s4v('ei', Q), AF.Exp, ['ei'], ['ei'])
                    P.act(s4v('emt', Q), s4v('mt', Q), AF.Exp, ['mt'], ['emt'], scale=-1.0)
                    b5 = 4
                    P.mm([('M', ps[b5][:, 0:4], EL, s4v('mt', Q), True, True), ('M', ps[b5][:, 4:8], EL, s4v('b', Q), True, True)],
                         ['cm', 'mt', 'b'], [psn[b5]])
                    yield
                    P.stt(Sm[:Q, :, :Q], ps[b3][:Q, :].rearrange("p (h s) -> p h s", h=4)[:, :, :Q], sc, Dm[:Q, :, :Q], ALU.mult, ALU.mult,
                          [psn[b3], 'Dm'], ['Sm'])
                    P.copy(s4['mnew'][:, :], ps[b5][:, 0:4], [psn[b5]], ['mnew'], eng='dve')
                    P.tt(s4['decay'][:, :], ps[b5][:, 4:8], mbc[:, :], ALU.add, [psn[b5], 'mbc'], ['decay'])
                    yield
                    b4 = 4
                    psb = ps[b4][:].bitcast(BF16)
                    P.mm([('T', psb[:Q, h * 128:h * 128 + Q], Sm[:Q, h, :Q], idb[:Q, :Q]) for h in range(4)], ['Sm', 'idb'], [psn[b4]])
                    P.tt(s4['decay'][:, :], s4['decay'][:, :], s4['mnew'][:, :], ALU.subtract, ['decay', 'mnew'], ['decay'])
                    P.tt(s4v('w', Q), s4v('wlog', Q), s4['mnew'][:Q, :], ALU.subtract, ['wlog', 'mnew'], ['w'])
                    yield
                    P.copy(STm[:Q, :, :Q], psb[:Q, 0:512].rearrange("p (h t) -> p h t", h=4)[:, :, :Q], [psn[b4]], ['STm'], eng='act')
                    P.act(s4['decay'][:, :], s4['decay'][:, :], AF.Exp, ['decay'], ['decay'])
                    P.act(s4v('w', Q), s4v('w', Q), AF.Exp, ['w'], ['w'])
                    yield
                    P.ts(s4v('w', Q), s4v('w', Q), sc, None, ALU.mult, None, ['w'], ['w'])
                    P.tt(kw[:Q, :, :], mkt[:Q, j, :].rearrange("p (h d) -> p h d", h=4), s4v('w', Q).unsqueeze(2).to_broadcast([Q, 4, 128]),
                         ALU.mult, ['mkt', 'w'], ['kw'])
                    for hf in range(2):
                        P.mm([('M', ps[3][:Q, (h % 2) * 129:(h % 2 + 1) * 129], STm[:Q, h, :Q], mvt[:Q, j, h, :], True, True) for h in (2 * hf, 2 * hf + 1)],
                             ['STm', 'mvt'], [psn[3]])
                        P.mm([('M', ps[4][:Q, (h % 2) * 129:(h % 2 + 1) * 129], mqf[:, h, c0:c0 + Q], Cnbf[:, h, :], True, True) for h in (2 * hf, 2 * hf + 1)],
                             ['mqf', 'Cnbf'], [psn[4]])
                        yield
                        v = lambda bk: ps[bk][:Q, 0:258].rearrange("p (h d) -> p h d", h=2)
                        P.tt(tot[:Q, hf * 2:hf * 2 + 2, :], v(4), s4['ei'][:Q, hf * 2:hf * 2 + 2].unsqueeze(2).to_broadcast([Q, 2, 129]),
                             ALU.mult, [psn[4], 'ei'], ['tot'])
                        P.tt(tot[:Q, hf * 2:hf * 2 + 2, :], v(3), tot[:Q, hf * 2:hf * 2 + 2, :], ALU.add, [psn[3], 'tot'], ['tot'])
                        yield
                    bu = [3, 4]
                    P.mm([('M', ps[bu[h // 2]][:, (h % 2) * 129:(h % 2 + 1) * 129], kw[:Q, h, :], mvt[:Q, j, h, :], True, True) for h in range(4)],
                         ['kw', 'mvt'], [psn[bu[0]], psn[bu[1]]])
                    P.ts(s4v('den', Q), tot[:Q, :, 128], -1.0, None, ALU.mult, None, ['tot'], ['den'])
                    yield
                    P.tt(s4v('den', Q), s4v('den', Q), tot[:Q, :, 128], ALU.max, ['tot', 'den'], ['den'])
                    yield
                    P.tt(s4v('den', Q), s4v('den', Q), s4v('emt', Q), ALU.max, ['den', 'emt'], ['den'])
                    yield
                    P.op('dve', lambda e: e.reciprocal(out=s4v('den', Q), in_=s4v('den', Q)), ['den'], ['den'])
                    P.tt(Cn32[:, :, :], Cn32[:, :, :], s4['decay'][:, :].unsqueeze(2).to_broadcast([128, 4, 129]), ALU.mult, ['Cn32', 'decay'], ['Cn32'])
                    yield
                    P.tt(hh[:Q, :, :], tot[:Q, :, 0:128], s4v('den', Q).unsqueeze(2).to_broadcast([Q, 4, 128]), ALU.mult, ['tot', 'den'], ['hh'])
                    for hf in range(2):
                        P.tt(Cn32[:, hf * 2:hf * 2 + 2, :], Cn32[:, hf * 2:hf * 2 + 2, :], ps[bu[hf]][:, 0:258].rearrange("p (h d) -> p h d", h=2),
                             ALU.add, ['Cn32', psn[bu[hf]]], ['Cn32'])
                    yield
                    P.tt(LD[:Q, :, :], hh[:Q, :, :], hh[:Q, :, :], ALU.mult, ['hh'], ['LD'])
                    P.copy(Cnbf[:], Cn32[:], ['Cn32'], ['Cnbf'], eng='act')
                    P.copy(mbc[:, :], s4['mnew'][:, :], ['mnew'], ['mbc'], eng='dve')
                    yield
                    P.op('dve', lambda e: e.tensor_reduce(out=s4v('ssq', Q), in_=LD[:Q, :, :], axis=AX.X, op=ALU.add), ['LD'], ['ssq'])
                    yield
                    P.act(s4v('rstd', Q), s4v('ssq', Q), AF.Sqrt, ['ssq', 'epst'], ['rstd'], bias=epst[:Q, 0:1], scale=1.0 / 128)
                    yield
                    P.op('dve', lambda e: e.reciprocal(out=s4v('rstd', Q), in_=s4v('rstd', Q)), ['rstd'], ['rstd'])
                    yield
                    P.tt(hh[:Q, :, :], hh[:Q, :, :], s4v('rstd', Q).unsqueeze(2).to_broadcast([Q, 4, 128]), ALU.mult, ['hh', 'rstd'], ['hh'])
                    yield
                    hf_ = hh[:Q, :, :].rearrange("p h d -> p (h d)")
                    P.tt(hf_, hf_, mln[:Q, :], ALU.mult, ['hh', 'mln'], ['hh'])
                    yield
                    P.tt(ytm[:Q, 1536:2048], hf_, mot[:Q, j, :], ALU.mult, ['hh', 'mot'], ['ytm_m'])
                    yield

                def attn_chunk(Q, c0, keys):
                    HB, nbk = 4, 2
                    nkt = len(keys)
                    W_ = HB * Q
                    b = 5
                    for ki, (slot, nk, var) in enumerate(keys):
                        pe_, pen = Pex[ki % 2], ('Pex', ki % 2)
                        for bk in range(nbk):
                            po = bk * 64
                            mms = []
                            for hh_ in range(HB):
                                h = 2 * hh_ + bk
                                mms.append(('M', ps[b][:nk, hh_ * Q:(hh_ + 1) * Q], KH[po:po + 64, h // 2, slot, :nk], qf[po:po + 64, h // 2, c0:c0 + Q], True, True))
                            P.mm(mms, [('KH', slot), 'qf'], [psn[b]])
                            yield
                            s_, sn_ = sb[bk % 2], ('sb', bk % 2)
                            P.stt(s_[:nk, :W_].rearrange("p (h q) -> p h q", h=HB), ps[b][:nk, :W_].rearrange("p (h q) -> p h q", h=HB), 0.125,
                                  ab[:nk, var, bk::2, :Q], ALU.mult, ALU.add, [psn[b], 'ab'], [sn_])
                            yield
                            P.act(pe_[:nk, bk::2, :Q], s_[:nk, :W_].rearrange("p (h q) -> p h q", h=HB), AF.Exp, [sn_], [pen])
                            yield
                        P.mm([('M', ps[6 + h // 4][:Q, (h % 4) * 65:(h % 4 + 1) * 65], pe_[:nk, h, :Q], VH[:nk, slot, h, :], (ki == 0 and h % 4 == 0), (ki == nkt - 1 and h % 4 == 3))
                              for h in range(8)], [pen, ('VH', slot)], [psn[6], psn[7]])
                        yield
                    for hf in range(2):
                        o3 = ps[6 + hf][:Q, 0:260].rearrange("p (h d) -> p h d", h=4)
                        P.op('dve', lambda e, o3=o3, hf=hf: e.reciprocal(out=rs8[:Q, hf * 4:(hf + 1) * 4], in_=o3[:, :, 64]), [psn[6 + hf]], ['rs8'])
                        yield
                        P.tt(ytm[:Q, 1024 + hf * 256:1024 + (hf + 1) * 256].rearrange("p (h d) -> p h d", h=4), o3[:, :, 0:64],
                             rs8[:Q, hf * 4:(hf + 1) * 4].unsqueeze(2).to_broadcast([Q, 4, 64]), ALU.mult, [psn[6 + hf], 'rs8'], ['ytm_a'])
                        yield

                def run_chains(*gens):
                    gens = [g for g in gens if g is not None]
                    while gens:
                        for g in list(gens):
                            try:
                                next(g)
                            except StopIteration:
                                gens.remove(g)

                def y_transposes(Q, j, Y_):
                    for half in range(2):
                        b = nb()
                        psb = ps[b][:].bitcast(BF16)
                        P.mm([('T', psb[:, cc * 128:cc * 128 + Q], ytm[:Q, (half * 8 + cc) * 128:(half * 8 + cc + 1) * 128], idb[:Q, :Q]) for cc in range(8)],
                             ['ytm_s', 'ytm_a', 'ytm_m', 'idb'], [psn[b]])
                        P.copy(Y_[:, half * 8:(half + 1) * 8, j * Q:(j + 1) * Q], psb[:, :].rearrange("p (c t) -> p c t", c=8)[:, :, :Q], [psn[b]], ['Yst'])

                DC = os.environ.get('DBG_CORE', 'asmyiopq')
                _a, _s, _m, _y = attn_chunk, ssd_chunk, mlstm_chunk, y_transposes
                if 'a' not in DC:
                    attn_chunk = lambda *a, **k: None
                if 's' not in DC:
                    ssd_chunk = lambda *a, **k: None
                if 'm' not in DC:
                    mlstm_chunk = lambda *a, **k: None
                if 'y' not in DC:
                    y_transposes = lambda *a, **k: None
                if 'i' not in DC:
                    init_state_sample = lambda *a, **k: init_state_zero()
                if 'o' not in DC:
                    store_state = lambda *a, **k: None
                for ti, tl in enumerate(cfg.tiles):
                    n, t0, Q, nsub = tl['n'], tl['t0'], tl['Q'], tl['nsub']
                    if tl['prompt'] and 'q' not in DC:
                        continue
                    if not tl['prompt'] and 'p' not in DC:
                        continue
                    load_tile(tl)
                    prep_small(tl)
                    Y_ = Yst[0]
                    if tl['prompt']:
                        if tl['first']:
                            init_state_zero()
                        cur = (tl['i'] % 2) * 4
                        prev = 4 - cur
                        P.dma('sp', 'lk', (KH[:, :, cur:cur + 4, :], S['k'][:, t0:t0 + 512].rearrange("(c p) (s t) -> p c s t", p=128, s=4)),
                              writes=[('KH', cur + s) for s in range(4)])
                        P.dma('sp', 'lv', (VH[:, cur:cur + 4].rearrange("p s h d -> p s (h d)"), S['v'][t0:t0 + 512, :].rearrange("(s p) c -> p s c", p=128)),
                              writes=[('VH', cur + s) for s in range(4)])
                        for j in range(4):
                            keys = []
                            if not tl['first']:
                                for u in range(j, 4):
                                    m = 4 + j - u
                                    keys.append((prev + u, 128, {4: 3, 3: 2, 2: 2, 1: 1}[m]))
                            for u in range(0, j + 1):
                                m = j - u
                                keys.append((cur + u, 128, {0: 0, 1: 1, 2: 2, 3: 2}[m]))
                            run_chains(mlstm_chunk(128, j, j * 128), ssd_chunk(128, j, j * 128), attn_chunk(128, j * 128, keys))
                            y_transposes(128, j, Y_)
                        if tl['last']:
                            store_state(True, tl['seq'])
                    else:
                        for s in range(NS):
                            init_state_sample(s)
                            P.dma('sp', 'lck', (cst[:], I['ck'][l, s].rearrange("(j p) c -> p j c", p=128)), writes=['cst'])
                            P.copy(cstb[:], cst[:], ['cst'], ['cstb'])
                            for jt in range(4):
                                b = nb()
                                psb = ps[b][:].bitcast(BF16)
                                P.mm([('T', psb[:, cc * 128:(cc + 1) * 128], cstb[:, jt, cc * 128:(cc + 1) * 128], idb[:]) for cc in range(4)],
                                     ['cstb', 'idb'], [psn[b]])
                                P.copy(KH[:, :, jt, :], psb[:, 0:512].rearrange("p (c t) -> p c t", c=4), [psn[b]], [('KH', jt)])
                            P.dma('sp', 'lck', (cst[:], I['cv'][l, s].rearrange("(j p) c -> p j c", p=128)), writes=['cst'])
                            P.copy(VH[:, 0:4, :, 0:64], cst[:].rearrange("p j (h d) -> p j h d", h=8), ['cst'], [('VH', jt) for jt in range(4)])
                            P.memset(VH[:, 0:4, :, 64:65], 1.0, [('VH', jt) for jt in range(4)])
                            cs0 = t0 + s * 64
                            P.dma('sp', 'lk', (KH[:, :, 4, 0:64], S['k'][:, cs0:cs0 + 64].rearrange("(c p) t -> p c t", p=128)), writes=[('KH', 4)])
                            P.dma('sp', 'lv', (VH[:64, 4].rearrange("p h d -> p (h d)"), S['v'][cs0:cs0 + 64, :]), writes=[('VH', 4)])
                            keys = [(0, 128, 3), (1, 128, 2), (2, 128, 2), (3, 128, 1), (4, 64, 0)]
                            run_chains(mlstm_chunk(64, s, s * 64), ssd_chunk(64, s, s * 64), attn_chunk(64, s * 64, keys))
                            y_transposes(64, s, Y_)
                            store_state(False, s)
                    P.dma('pool', 'sty_', (S['Y'][:, t0:t0 + n].rearrange("(c p) t -> p c t", p=128), Y_[:, :, :n]), reads=['Yst'],
                          writes=[('S_Y', t0)])

        def mixer_merge(l, gi):
            Win = I['win'][l]
            with P.phase():
                Wg = P.sbuf('Wg', [128, 6, 8, 512], BF16)
                for bi in range(6):
                    col = C_GL + bi * 512
                    P.dma('pool', ('w', bi), (Wg[:, bi, :, :], Win[:, col:col + 512].rearrange("(kc p) c -> p kc c", p=128)), writes=[('Wg', bi)])
                Wp = P.sbuf('Wp', [128, 16, 1024], BF16)
                P.dma('pool', ('w', 6), [(Wp[:, 0:4, :], I['wps'][l, 0:512, :].rearrange("(kc p) c -> p kc c", p=128)),
                                         (Wp[:, 4:8, :], I['wps'][l, 512:1024, :].rearrange("(kc p) c -> p kc c", p=128)),
                                         (Wp[:, 8:12, :], I['wpa'][l].rearrange("(kc p) c -> p kc c", p=128)),
                                         (Wp[:, 12:16, :], I['wpm'][l].rearrange("(kc p) c -> p kc c", p=128))], writes=['Wp'])
                Wo = P.sbuf('Wo', [128, 8, 1024], BF16)
                P.dma('pool', ('w', 7), [(Wo[:, 0:4, :], I['wo'][l, 0:512, :].rearrange("(kc p) c -> p kc c", p=128)),
                                         (Wo[:, 4:8, :], I['wo'][l, 512:1024, :].rearrange("(kc p) c -> p kc c", p=128))], writes=['Wo'])
                bufs = res_bufs()
                XNts = [P.sbuf('XNt%d' % i, [128, 8, 512], BF16) for i in range(2)]
                Yts = [P.sbuf('Yt%d' % i, [128, 16, 512], BF16) for i in range(2)]
                Rt = P.sbuf('Rt', [128, 8, 512], F32)
                mg = P.sbuf('mg', [128, 8, 512], BF16)
                sig = [P.sbuf('sig%d' % i, [128, 512], F32) for i in range(3)]
                macc = P.sbuf('macc', [128, 512], F32)
                mtmp = P.sbuf('mtmp', [128, 512], F32)
                kcs = [(0, 8), (8, 4), (12, 4)]
                pend = [None]
                for ti, tl in enumerate(cfg.tiles):
                    n, t0 = tl['n'], tl['t0']
                    fmv = lambda d: d[:, t0:t0 + n].rearrange("(c p) t -> p c t", p=128)
                    XNt, xnn = XNts[ti % 2], ('XNt', ti % 2)
                    Yt, ytn = Yts[ti % 2], ('Yt', ti % 2)
                    P.dma('sp', ('ldxn', ti % 2), (XNt[:, :, :n], fmv(S['XN'])), writes=[xnn])
                    P.dma('sp', ('ldy', ti % 2), (Yt[:, :, :n], fmv(S['Y'])), writes=[ytn])
                    for dm in range(8):
                        for br in range(3):
                            bi = br * 2 + dm // 4
                            ch = dm % 4
                            bg = nb()
                            P.mm([('M', ps[bg][:, :n], Wg[:, bi, kc, ch * 128:(ch + 1) * 128], XNt[:, kc, :n], kc == 0, kc == 7) for kc in range(8)],
                                 [('Wg', bi), xnn], [psn[bg]])
                            P.act(sig[br][:, :n], ps[bg][:, :n], AF.Sigmoid, [psn[bg]], [('sig', br)])
                            k0, nk = kcs[br]
                            bp = nb()
                            P.mm([('M', ps[bp][:, :n], Wp[:, k0 + kc, dm * 128:(dm + 1) * 128], Yt[:, k0 + kc, :n], kc == 0, kc == nk - 1) for kc in range(nk)],
                                 ['Wp', ytn], [psn[bp]])
                            if dm == 1 and br == 0:
                                if pend[0] is not None:
                                    pend[0]()
                                    pend[0] = None
                                P.dma('sp', 'ldR', (Rt[:, :, :n], fmv(S['R'])), writes=['Rt'])
                            if br == 0:
                                P.tt(macc[:, :n], sig[0][:, :n], ps[bp][:, :n], ALU.mult, [('sig', 0), psn[bp]], ['macc'])
                            else:
                                P.tt(mtmp[:, :n], sig[br][:, :n], ps[bp][:, :n], ALU.mult, [('sig', br), psn[bp]], ['mtmp'])
                                if br == 1:
                                    P.tt(macc[:, :n], macc[:, :n], mtmp[:, :n], ALU.add, ['macc', 'mtmp'], ['macc'])
                                else:
                                    P.tt(mg[:, dm, :n], macc[:, :n], mtmp[:, :n], ALU.add, ['macc', 'mtmp'], ['mg'])
                    for dm in range(8):
                        b = nb()
                        P.mm([('M', ps[b][:, :n], Wo[:, kc, dm * 128:(dm + 1) * 128], mg[:, kc, :n], kc == 0, kc == 7) for kc in range(8)],
                             ['Wo', 'mg'], [psn[b]])
                        P.tt(Rt[:, dm, :n], ps[b][:, :n], Rt[:, dm, :n], ALU.add, [psn[b], 'Rt'], ['Rt'])
                    pend[0] = finish_residual(tl, Rt, 'Rt', gi, bufs, defer=True)
                if pend[0] is not None:
                    pend[0]()

        stop = getattr(cfg, 'stop', 99)
        for l in range(DEPTH):
            if stop >= 1:
                ffn(I['w1i'][l], I['w1o'][l], 3 * l + 1, False)
            if stop >= 2:
                mixer_inproj(l)
            if stop >= 3:
                mixer_core(l)
            if stop >= 4:
                mixer_merge(l, 3 * l + 2)
            last = (l == DEPTH - 1)
            if stop >= 5:
                ffn(I['w2i'][l], I['w2o'][l], 3 * DEPTH if last else 3 * (l + 1), last)
    return nc


def _host_params(inp, DEPTH):
    f = np.float32
    g = []
    for l in range(DEPTH):
        g += [inp['norm_ffn1'][l], inp['norm_mix'][l], inp['norm_ffn2'][l]]
    g.append(inp['final_norm'])
    gains = np.stack([np.asarray(v, f).reshape(8, 128).T for v in g], axis=1)
    cw = np.asarray(inp['ssd_conv_w'], f)
    convw = np.ascontiguousarray(cw.reshape(DEPTH, 4, 16, 128).transpose(3, 0, 2, 1))
    convb = np.ascontiguousarray(np.asarray(inp['ssd_conv_b'], f).reshape(DEPTH, 16, 128).transpose(2, 0, 1))
    r16 = np.stack([np.asarray(inp['ssd_dt_bias'], f), np.asarray(inp['ssd_a_log'], f), np.asarray(inp['ssd_d'], f)], axis=1)
    rep16 = np.ascontiguousarray(np.broadcast_to(r16[None], (128, DEPTH, 3, 16)))
    ssdn = np.ascontiguousarray(np.broadcast_to(np.asarray(inp['ssd_norm'], f)[None], (128, DEPTH, 1024)))
    mln = np.ascontiguousarray(np.broadcast_to(np.asarray(inp['mlstm_norm'], f)[None], (128, DEPTH, 512)))
    gbr = np.ascontiguousarray(np.broadcast_to(np.asarray(inp['mlstm_gate_bias'], f)[None], (128, DEPTH, 8)))
    tab = np.asarray(inp['attn_rel_bias'], f)
    kk = np.arange(128)[:, None]
    qq = np.arange(128)[None, :]
    kc_, qc_ = kk // 64, qq // 64
    kl, ql = kk % 64, qq % 64
    ab = np.zeros((DEPTH, 128, 4, 8, 128), f)
    for vi, m in enumerate([0, 1, 2, 4]):
        jj = 2 * m + qc_ - kc_
        rel = (ql - kl) + 64 * jj
        idx = np.clip(rel, -128, 128) + 128
        ok = (jj >= 0) & (jj <= 8)
        for l in range(DEPTH):
            t = tab[l][idx]
            t = np.where(ok[..., None], t, f(NEG))
            ab[l, :, vi] = t.transpose(0, 2, 1)
    cm = np.zeros((128, 8, 128), f)
    r = np.arange(128)[:, None]
    c = np.arange(128)[None, :]
    cm[:, 0] = (r <= c)
    cm[:, 1] = (r > c)
    cm[:, 2] = (r == c)
    cm[:, 3] = 1.0
    cm[:, 4] = (r == 127)
    cm[:, 5] = (r == 63)
    cm[:, 6] = np.where(c > r, NEG, 0.0)
    return dict(gains=np.ascontiguousarray(gains), convw=convw, convb=convb, rep16=rep16, ssdn=ssdn, mln=mln, gbr=gbr,
                abias=ab, cmat=cm)


def run(inp, n_cores, cfg):
    f = np.float32
    NP, LP, NS, DEPTH = cfg.NP, cfg.LP, cfg.NS, cfg.DEPTH
    nc = build(cfg)
    hp = _host_params(inp, DEPTH)
    wmap = dict(w1i='w_ffn1_in', w1o='w_ffn1_out', win='w_in', wps='w_proj_ssd', wpa='w_proj_attn', wpm='w_proj_mlstm',
                wo='w_out', w2i='w_ffn2_in', w2o='w_ffn2_out')
    shared = {k: np.ascontiguousarray(np.asarray(inp[v], f)) for k, v in wmap.items()}
    shared.update(hp)
    in_maps = []
    for c in range(n_cores):
        m = dict(shared)
        ps_, ss_ = slice(c * NP, (c + 1) * NP), slice(c * NS, (c + 1) * NS)
        m['xp'] = np.ascontiguousarray(np.asarray(inp['x_prompt'], f)[ps_].reshape(NP * LP, D))
        m['xs'] = np.ascontiguousarray(np.asarray(inp['x_sample'], f)[ss_].reshape(NS * 64, D))
        m['ck'] = np.ascontiguousarray(np.asarray(inp['cache_attn_k'], f)[:, ss_].reshape(DEPTH, NS, 512, 512))
        m['cv'] = np.ascontiguousarray(np.asarray(inp['cache_attn_v'], f)[:, ss_].reshape(DEPTH, NS, 512, 512))
        m['sssd'] = np.ascontiguousarray(np.asarray(inp['state_ssd'], f)[:, ss_].reshape(DEPTH, NS, 1024, 128))
        m['sconv'] = np.ascontiguousarray(np.asarray(inp['state_ssd_conv'], f)[:, ss_])
        m['smc'] = np.ascontiguousarray(np.asarray(inp['state_mlstm_c'], f)[:, ss_])
        m['smn'] = np.ascontiguousarray(np.asarray(inp['state_mlstm_n'], f)[:, ss_])
        m['smm'] = np.ascontiguousarray(np.asarray(inp['state_mlstm_m'], f)[:, ss_])
        in_maps.append(m)
    res = run_bass_kernel_spmd(nc, in_maps, core_ids=list(range(n_cores)))
    R = res.results
    if getattr(cfg, 'dbg', ()):
        cfg.dbg_out = {k: np.asarray(R[0][k]) for k in cfg.dbg}
    KEEP = min(512, LP)

    def cat(name, axis, shape_tail):
        return np.concatenate([np.asarray(r[name]) for r in R], axis=axis)

    yp = np.concatenate([np.asarray(r['yp']).reshape(NP, LP, D) for r in R], 0)
    ys = np.concatenate([np.asarray(r['ys']).reshape(NS, 64, D) for r in R], 0)
    kp = np.concatenate([np.asarray(r['kp']).reshape(DEPTH, NP, KEEP, 8, 64) for r in R], 1)
    vp = np.concatenate([np.asarray(r['vp']).reshape(DEPTH, NP, KEEP, 8, 64) for r in R], 1)
    ssdp = np.concatenate([np.asarray(r['ssdp']).reshape(DEPTH, NP, 16, 64, 128) for r in R], 1)
    convp = np.concatenate([np.asarray(r['convp']) for r in R], 1)
    mcp = np.concatenate([np.asarray(r['mcp']) for r in R], 1)
    mnp = np.concatenate([np.asarray(r['mnp']) for r in R], 1)
    mmp = np.concatenate([np.asarray(r['mmp']) for r in R], 1)
    ks = np.concatenate([np.asarray(r['ks']).reshape(DEPTH, NS, 64, 8, 64) for r in R], 1)
    vs = np.concatenate([np.asarray(r['vs']).reshape(DEPTH, NS, 64, 8, 64) for r in R], 1)
    ssds = np.concatenate([np.asarray(r['ssds']).reshape(DEPTH, NS, 16, 64, 128) for r in R], 1)
    convs = np.concatenate([np.asarray(r['convs']) for r in R], 1)
    mcs = np.concatenate([np.asarray(r['mcs']) for r in R], 1)
    mns = np.concatenate([np.asarray(r['mns']) for r in R], 1)
    mms = np.concatenate([np.asarray(r['mms']) for r in R], 1)
    outs = (yp, ys, kp, vp, ssdp, convp, mcp, mnp, mmp, ks, vs, ssds, convs, mcs, mns, mms)
    return tuple(np.ascontiguousarray(o, dtype=np.float32) for o in outs)


def kernel(**inputs):
    cfg = Cfg(NP=2, LP=2048, NS=4, DEPTH=2)
    return run(inputs, 8, cfg)
```

```python
import contextlib
import os
import numpy as np
import concourse.bass as bass
import concourse.mybir as mybir
from concourse.bass_utils import run_bass_kernel_spmd

F32 = mybir.dt.float32
BF16 = mybir.dt.bfloat16
AF = mybir.ActivationFunctionType
ALU = mybir.AluOpType
AX = mybir.AxisListType

ENGS = ['pe', 'act', 'dve', 'pool', 'sp']
D = 1024
DFF = 4096
IN_DIM = 9752
C_Z, C_XBC, C_DT, C_AQ, C_AK, C_AV, C_MQ, C_MK, C_MV, C_MO, C_MI, C_GL = (
    0, 1024, 3072, 3088, 3600, 4112, 4624, 5136, 5648, 6160, 6672, 6680)
EPS = 1e-6
NEG = -30000.0


class DmaSem:
    def __init__(self, sem):
        self.sem = sem
        self.count = 0


class Prog:
    def __init__(self, nc, stack):
        self.nc = nc
        self.gstack = stack
        self.stack = stack
        self.ops = {e: [] for e in ENGS}
        self.esem = {e: stack.enter_context(nc.semaphore('s_' + e)) for e in ENGS}
        self.count = {e: 0 for e in ENGS}
        self.seen = {e: {} for e in ENGS}
        self.lastw = {}
        self.readers = {}
        self.dsems = {}
        self.uid = 0
        self.evc = 0

    def dsem(self, key):
        if key not in self.dsems:
            self.dsems[key] = DmaSem(self.gstack.enter_context(self.nc.semaphore('d%d' % len(self.dsems))))
        return self.dsems[key]

    def sbuf(self, name, shape, dtype):
        self.uid += 1
        return self.stack.enter_context(self.nc.sbuf_tensor('%s_%d' % (name, self.uid), list(shape), dtype))

    def _deps(self, eng, reads, writes):
        need = {}

        def add(tok, same_ok):
            if tok is None:
                return
            sem, val, teng = tok
            if teng == eng and not same_ok:
                return
            k = id(sem)
            if k not in need or need[k][1] < val:
                need[k] = (sem, val)

        for r in reads:
            add(self.lastw.get(r), True)
        for w in writes:
            add(self.lastw.get(w), True)
            for t in self.readers.get(w, ()):
                add(t, True)
        waits = []
        seen = self.seen[eng]
        for k, (sem, val) in need.items():
            if seen.get(k, 0) >= val:
                continue
            seen[k] = val
            waits.append((sem, val))
        return waits

    def _commit(self, tok, reads, writes):
        for r in reads:
            self.readers.setdefault(r, []).append(tok)
        for w in writes:
            self.lastw[w] = tok
            self.readers[w] = []

    def op(self, eng, fn, reads=(), writes=()):
        waits = self._deps(eng, reads, writes)
        self.count[eng] += 1
        tok = (self.esem[eng], self.count[eng], eng)
        self.ops[eng].append((waits, fn, (self.esem[eng], 1)))
        self._commit(tok, reads, writes)
        return tok

    def dma(self, eng, key, pairs, reads=(), writes=(), **kw):
        ds = self.dsem(key)
        if not isinstance(pairs, list):
            pairs = [pairs]
        waits = self._deps(eng, reads, writes)
        for i, (out, in_) in enumerate(pairs):
            ds.count += 16

            def fn(e, out=out, in_=in_, kw=kw):
                return e.dma_start(out=out, in_=in_, **kw)

            self.ops[eng].append((waits if i == 0 else [], fn, (ds.sem, 16)))
        tok = (ds.sem, ds.count, 'dma')
        self._commit(tok, reads, writes)
        return tok

    def barrier(self):
        toks = [(self.esem[e], self.count[e]) for e in ENGS if self.count[e] > 0]
        toks += [(d.sem, d.count) for d in self.dsems.values() if d.count > 0]
        for e in ENGS:
            seen = self.seen[e]
            waits = []
            for sem, val in toks:
                if sem is self.esem[e]:
                    continue
                if seen.get(id(sem), 0) >= val:
                    continue
                seen[id(sem)] = val
                waits.append((sem, val))
            self.ops[e].append((waits, None, None))
        self.lastw = {}
        self.readers = {}

    def emit(self):
        nc = self.nc
        ops = self.ops
        with nc.Block() as block:
            def mk(ename):
                def body(e):
                    for waits, fn, inc in ops[ename]:
                        for sem, val in waits:
                            e.wait_ge(sem, val)
                        if fn is not None:
                            ins = fn(e)
                            if inc is not None:
                                ins.then_inc(inc[0], inc[1])
                return body
            block.tensor(mk('pe'))
            block.scalar(mk('act'))
            block.vector(mk('dve'))
            block.gpsimd(mk('pool'))
            block.sync(mk('sp'))
        self.ops = {e: [] for e in ENGS}

    @contextlib.contextmanager
    def phase(self):
        st = contextlib.ExitStack()
        old = self.stack
        self.stack = st
        with st:
            yield
            self.barrier()
            self.emit()
        self.stack = old

    def mm(self, mms, reads, writes):
        def fn(e, mms=mms):
            ins = None
            for m in mms:
                if m[0] == 'M':
                    ins = e.matmul(m[1], lhsT=m[2], rhs=m[3], start=m[4], stop=m[5])
                else:
                    ins = e.transpose(m[1], m[2], m[3])
            return ins
        return self.op('pe', fn, reads, writes)

    def act(self, out, in_, func, reads, writes, **kw):
        return self.op('act', lambda e: e.activation(out=out, in_=in_, func=func, **kw), reads, writes)

    def tt(self, out, in0, in1, op, reads, writes, eng='dve'):
        return self.op(eng, lambda e: e.tensor_tensor(out=out, in0=in0, in1=in1, op=op), reads, writes)

    def ts(self, out, in0, s1, s2, op0, op1, reads, writes, eng='dve'):
        if s2 is None:
            return self.op(eng, lambda e: e.tensor_scalar(out=out, in0=in0, scalar1=s1, scalar2=None, op0=op0), reads, writes)
        return self.op(eng, lambda e: e.tensor_scalar(out=out, in0=in0, scalar1=s1, scalar2=s2, op0=op0, op1=op1), reads, writes)

    def stt(self, out, in0, scalar, in1, op0, op1, reads, writes, eng='dve'):
        return self.op(eng, lambda e: e.scalar_tensor_tensor(out=out, in0=in0, scalar=scalar, in1=in1, op0=op0, op1=op1), reads, writes)

    def copy(self, out, in_, reads, writes, eng=None):
        if eng is None:
            self.evc += 1
            eng = 'act' if self.evc % 2 else 'dve'
        if eng == 'act':
            return self.op('act', lambda e: e.copy(out=out, in_=in_), reads, writes)
        return self.op(eng, lambda e: e.tensor_copy(out=out, in_=in_), reads, writes)

    def memset(self, ap, val, writes, eng='dve'):
        return self.op(eng, lambda e: e.memset(ap, val), (), writes)


class Cfg:
    def __init__(self, NP=2, LP=2048, NS=4, DEPTH=2):
        self.NP, self.LP, self.NS, self.DEPTH = NP, LP, NS, DEPTH
        self.TP = NP * LP
        self.TALL = self.TP + NS * 64
        self.tiles = []
        for s in range(NP):
            nt = LP // 512
            for i in range(nt):
                self.tiles.append(dict(t0=s * LP + i * 512, n=512, Q=128, nsub=4, prompt=True, seq=s,
                                       first=(i == 0), last=(i == nt - 1), i=i))
        if NS:
            self.tiles.append(dict(t0=self.TP, n=NS * 64, Q=64, nsub=NS, prompt=False, seq=0,
                                   first=True, last=True, i=0))


def build(cfg):
    NP, LP, NS, DEPTH = cfg.NP, cfg.LP, cfg.NS, cfg.DEPTH
    TALL = cfg.TALL
    KEEP = min(512, LP)
    nc = bass.Bass("TRN2", target_bir_lowering=False)

    def din(name, shape, dt=F32):
        return nc.dram_tensor(name, list(shape), dt, kind="ExternalInput").ap()

    def dout(name, shape, dt=F32):
        return nc.dram_tensor(name, list(shape), dt, kind="ExternalOutput").ap()

    def dscr(name, shape, dt):
        if name in getattr(cfg, 'dbg', ()):
            return nc.dram_tensor(name, list(shape), dt, kind="ExternalOutput").ap()
        return nc.dram_tensor(name, list(shape), dt).ap()

    I = {}
    I['xp'] = din('xp', [max(NP, 1) * LP, D])
    I['xs'] = din('xs', [max(NS, 1) * 64, D])
    I['ck'] = din('ck', [DEPTH, max(NS, 1), 512, 512])
    I['cv'] = din('cv', [DEPTH, max(NS, 1), 512, 512])
    I['sssd'] = din('sssd', [DEPTH, max(NS, 1), 1024, 128])
    I['sconv'] = din('sconv', [DEPTH, max(NS, 1), 3, 2048])
    I['smc'] = din('smc', [DEPTH, max(NS, 1), 4, 128, 128])
    I['smn'] = din('smn', [DEPTH, max(NS, 1), 4, 128])
    I['smm'] = din('smm', [DEPTH, max(NS, 1), 4])
    I['w1i'] = din('w1i', [DEPTH, D, 2 * DFF])
    I['w1o'] = din('w1o', [DEPTH, DFF, D])
    I['win'] = din('win', [DEPTH, D, IN_DIM])
    I['wps'] = din('wps', [DEPTH, 1024, D])
    I['wpa'] = din('wpa', [DEPTH, 512, D])
    I['wpm'] = din('wpm', [DEPTH, 512, D])
    I['wo'] = din('wo', [DEPTH, D, D])
    I['w2i'] = din('w2i', [DEPTH, D, 2 * DFF])
    I['w2o'] = din('w2o', [DEPTH, DFF, D])
    I['gains'] = din('gains', [128, 3 * DEPTH + 1, 8])
    I['convw'] = din('convw', [128, DEPTH, 16, 4])
    I['convb'] = din('convb', [128, DEPTH, 16])
    I['rep16'] = din('rep16', [128, DEPTH, 3, 16])
    I['ssdn'] = din('ssdn', [128, DEPTH, 1024])
    I['mln'] = din('mln', [128, DEPTH, 512])
    I['gbr'] = din('gbr', [128, DEPTH, 8])
    I['abias'] = din('abias', [DEPTH, 128, 4, 8, 128])
    I['cmat'] = din('cmat', [128, 8, 128])

    O = {}
    O['yp'] = dout('yp', [max(NP, 1) * LP, D])
    O['ys'] = dout('ys', [max(NS, 1) * 64, D])
    O['kp'] = dout('kp', [DEPTH, max(NP, 1), KEEP, 512])
    O['vp'] = dout('vp', [DEPTH, max(NP, 1), KEEP, 512])
    O['ssdp'] = dout('ssdp', [DEPTH, max(NP, 1), 1024, 128])
    O['convp'] = dout('convp', [DEPTH, max(NP, 1), 3, 2048])
    O['mcp'] = dout('mcp', [DEPTH, max(NP, 1), 4, 128, 128])
    O['mnp'] = dout('mnp', [DEPTH, max(NP, 1), 4, 128])
    O['mmp'] = dout('mmp', [DEPTH, max(NP, 1), 4])
    O['ks'] = dout('ks', [DEPTH, max(NS, 1), 64, 512])
    O['vs'] = dout('vs', [DEPTH, max(NS, 1), 64, 512])
    O['ssds'] = dout('ssds', [DEPTH, max(NS, 1), 1024, 128])
    O['convs'] = dout('convs', [DEPTH, max(NS, 1), 3, 2048])
    O['mcs'] = dout('mcs', [DEPTH, max(NS, 1), 4, 128, 128])
    O['mns'] = dout('mns', [DEPTH, max(NS, 1), 4, 128])
    O['mms'] = dout('mms', [DEPTH, max(NS, 1), 4])

    S = {}
    S['R'] = dscr('S_R', [D, TALL], F32)
    S['XN'] = dscr('S_XN', [D, TALL], BF16)
    S['G'] = dscr('S_G', [DFF, TALL], BF16)
    S['xtm'] = dscr('S_xtm', [TALL, 1024], BF16)
    S['Btm'] = dscr('S_Btm', [TALL, 512], BF16)
    S['Bfm'] = dscr('S_Bfm', [512, TALL], BF16)
    S['Cfm'] = dscr('S_Cfm', [512, TALL], BF16)
    S['q'] = dscr('S_q', [512, TALL], BF16)
    S['k'] = dscr('S_k', [512, TALL], BF16)
    S['mq'] = dscr('S_mq', [512, TALL], BF16)
    S['mkf'] = dscr('S_mkf', [512, TALL], BF16)
    S['z'] = dscr('S_z', [TALL, 1024], BF16)
    S['v'] = dscr('S_v', [TALL, 520], BF16)
    S['mkt'] = dscr('S_mkt', [TALL, 512], BF16)
    S['mv'] = dscr('S_mv', [TALL, 516], BF16)
    S['mo'] = dscr('S_mo', [TALL, 512], BF16)
    S['sm'] = dscr('S_sm', [TALL, 24], F32)
    S['Y'] = dscr('S_Y', [2048, TALL], BF16)

    out_toks = []

    with contextlib.ExitStack() as gst:
        P = Prog(nc, gst)
        ps = [gst.enter_context(nc.psum_tensor('ps%d' % i, [128, 512], F32)) for i in range(8)]
        psn = [('ps', i) for i in range(8)]
        cm = P.sbuf('cmat', [128, 8, 128], F32)
        idb = P.sbuf('idb', [128, 128], BF16)
        gains = P.sbuf('gains', [128, 3 * DEPTH + 1, 8], F32)
        convw = P.sbuf('convw', [128, DEPTH, 16, 4], F32)
        convb = P.sbuf('convb', [128, DEPTH, 16], F32)
        rep16 = P.sbuf('rep16', [128, DEPTH, 3, 16], F32)
        gbr = P.sbuf('gbr', [128, DEPTH, 8], F32)
        arep = P.sbuf('arep', [128, DEPTH, 16], F32)
        epst = P.sbuf('epst', [128, 1], F32)
        U, SU, IDF, ONES, EL128, EL64, MN = (cm[:, i, :] for i in range(7))

        bank = [0]

        def nb():
            b = bank[0]
            bank[0] = (b + 1) % 6
            return b

        with P.phase():
            P.dma('sp', 'g0', [(cm[:], I['cmat']), (gains[:], I['gains']), (convw[:], I['convw']),
                               (convb[:], I['convb']), (rep16[:], I['rep16']), (gbr[:], I['gbr'])],
                  writes=['cm', 'gains', 'convw', 'convb', 'rep16', 'gbr'])
            P.dma('pool', 'g1', (idb[:], I['cmat'][:, 2, :]), writes=['idb'])
            P.memset(epst[:], EPS, ['epst'])
            P.act(arep[:], rep16[:, :, 1, :], AF.Exp, ['rep16'], ['arep'])
            P.ts(arep[:], arep[:], -1.0, None, ALU.mult, None, ['arep'], ['arep'])

        def finish_residual(tl, Rt, rname, gi, bufs, final=False, store_R=True, defer=False):
            n = tl['n']
            t0 = tl['t0']
            sq, acc, rstd, XNo = bufs['sq'], bufs['acc'], bufs['rstd'], bufs['XNo']
            sq2 = bufs['sq2']
            for c in range(8):
                if c == 0:
                    P.act(acc[:, :n], Rt[:, 0, :n], AF.Square, [rname], ['acc'])
                else:
                    sq_, sqn = (sq, 'sq') if c % 2 else (sq2, 'sq2')
                    P.act(sq_[:, :n], Rt[:, c, :n], AF.Square, [rname], [sqn])
                    P.tt(acc[:, :n], acc[:, :n], sq_[:, :n], ALU.add, ['acc', sqn], ['acc'])

            def part_b():
                _finish_b(tl, Rt, rname, gi, bufs, final, store_R)
            if defer:
                return part_b
            part_b()
            return None

        def _finish_b(tl, Rt, rname, gi, bufs, final, store_R):
            n = tl['n']
            t0 = tl['t0']
            sq, acc, rstd, XNo = bufs['sq'], bufs['acc'], bufs['rstd'], bufs['XNo']
            b = nb()
            P.mm([('M', ps[b][:, :n], ONES, acc[:, :n], True, True)], ['cm', 'acc'], [psn[b]])
            P.act(rstd[:, :n], ps[b][:, :n], AF.Sqrt, [psn[b], 'epst'], ['rstd'], bias=epst[:, 0:1], scale=1.0 / D)
            P.op('dve', lambda e: e.reciprocal(out=rstd[:, :n], in_=rstd[:, :n]), ['rstd'], ['rstd'])
            if store_R:
                P.dma('pool', ('stR', rname), (S['R'][:, t0:t0 + n].rearrange("(c p) t -> p c t", p=128), Rt[:, :, :n]),
                      reads=[rname], writes=[('S_R', t0)])
            if not final:
                for c in range(8):
                    P.stt(XNo[:, c, :n], Rt[:, c, :n], gains[:, gi, c:c + 1], rstd[:, :n], ALU.mult, ALU.mult,
                          [rname, 'rstd', 'gains'], ['XNo'])
                P.dma('pool', 'stXN', (S['XN'][:, t0:t0 + n].rearrange("(c p) t -> p c t", p=128), XNo[:, :, :n]),
                      reads=['XNo'], writes=[('S_XN', t0)])
            else:
                yfm, ytm = bufs['yfm'], bufs['ytm']
                for c in range(8):
                    P.stt(yfm[:, c, :n], Rt[:, c, :n], gains[:, gi, c:c + 1], rstd[:, :n], ALU.mult, ALU.mult,
                          [rname, 'rstd', 'gains'], ['yfm'])
                for j in range(n // 128):
                    for half in range(2):
                        b = nb()
                        P.mm([('T', ps[b][:, cc * 128:(cc + 1) * 128], yfm[:, half * 4 + cc, j * 128:(j + 1) * 128], IDF)
                              for cc in range(4)], ['yfm', 'cm'], [psn[b]])
                        P.copy(ytm[:, half * 512:(half + 1) * 512], ps[b][:, :], [psn[b]], ['ytm'])
                    tg = t0 + j * 128
                    if tl['prompt']:
                        dst = O['yp'][tg:tg + 128, :]
                    else:
                        dst = O['ys'][tg - cfg.TP:tg - cfg.TP + 128, :]
                    out_toks.append(P.dma('pool', 'sty', (dst, ytm[:]), reads=['ytm'], writes=[('yout', tg)]))

        def res_bufs(final=False):
            bufs = dict(sq=P.sbuf('sq', [128, 512], F32), sq2=P.sbuf('sq2', [128, 512], F32), acc=P.sbuf('acc', [128, 512], F32),
                        rstd=P.sbuf('rstd', [128, 512], F32), XNo=P.sbuf('XNo', [128, 8, 512], BF16))
            if final:
                bufs['yfm'] = P.sbuf('yfm', [128, 8, 512], F32)
                bufs['ytm'] = P.sbuf('ytm', [128, 1024], F32)
            return bufs

        with P.phase():
            bufs = res_bufs()
            Rt = P.sbuf('Rt', [128, 8, 512], F32)
            xin = [P.sbuf('xin%d' % i, [128, 1024], F32) for i in range(2)]
            k = 0
            for tl in cfg.tiles:
                n, t0 = tl['n'], tl['t0']
                for j in range(n // 128):
                    tg = t0 + j * 128
                    src = I['xp'][tg:tg + 128, :] if tl['prompt'] else I['xs'][tg - cfg.TP:tg - cfg.TP + 128, :]
                    xb_, xn_ = xin[k % 2], ('xin', k % 2)
                    k += 1
                    P.dma('sp', ('ldx', k % 2), (xb_[:], src), writes=[xn_])
                    for half in range(2):
                        b = nb()
                        P.mm([('T', ps[b][:, cc * 128:(cc + 1) * 128], xb_[:, (half * 4 + cc) * 128:(half * 4 + cc + 1) * 128], IDF)
                              for cc in range(4)], [xn_, 'cm'], [psn[b]])
                        P.copy(Rt[:, half * 4:(half + 1) * 4, j * 128:(j + 1) * 128],
                               ps[b][:, :].rearrange("p (c t) -> p c t", c=4), [psn[b]], ['Rt'])
                finish_residual(tl, Rt, 'Rt', 0, bufs)

        def ffn(Win, Wout, gi, final):
            with P.phase():
                Ws = [P.sbuf('Wffn%d' % part, [128, 8, 8, 512], BF16) for part in range(2)]
                for part in range(2):
                    for bi in (0, 4, 1, 5, 2, 6, 3, 7):
                        col = (part * 4 + (bi % 4)) * 512 + (DFF if bi >= 4 else 0)
                        P.dma('pool', ('w', bi) if part == 0 else ('w2', bi),
                              (Ws[part][:, bi, :, :], Win[:, col:col + 512].rearrange("(kc p) c -> p kc c", p=128)),
                              writes=[('W', part, bi)])
                XNt = [P.sbuf('XNt%d' % i, [128, 8, 512], BF16) for i in range(2)]
                Gt = [P.sbuf('Gt%d' % i, [128, 16, 512], BF16) for i in range(2)]
                sa = [P.sbuf('sa%d' % i, [128, 512], F32) for i in range(2)]
                it = 0
                for part in range(2):
                    W = Ws[part]
                    for ti, tl in enumerate(cfg.tiles):
                        n, t0 = tl['n'], tl['t0']
                        X, xname = XNt[it % 2], ('XNt', it % 2)
                        G_, gname = Gt[it % 2], ('Gt', it % 2)
                        P.dma('sp', ('ldxn', it % 2), (X[:, :, :n], S['XN'][:, t0:t0 + n].rearrange("(c p) t -> p c t", p=128)),
                              writes=[xname])
                        q = 0
                        for pr in range(4):
                            for ch in range(4):
                                ba, bb = nb(), nb()
                                P.mm([('M', ps[ba][:, :n], W[:, pr, kc, ch * 128:(ch + 1) * 128], X[:, kc, :n], kc == 0, kc == 7)
                                      for kc in range(8)], [('W', part, pr), xname], [psn[ba]])
                                P.mm([('M', ps[bb][:, :n], W[:, 4 + pr, kc, ch * 128:(ch + 1) * 128], X[:, kc, :n], kc == 0, kc == 7)
                                      for kc in range(8)], [('W', part, 4 + pr), xname], [psn[bb]])
                                s_, sn_ = sa[q % 2], ('sa', q % 2)
                                q += 1
                                P.act(s_[:, :n], ps[ba][:, :n], AF.Silu, [psn[ba]], [sn_])
                                P.tt(G_[:, pr * 4 + ch, :n], s_[:, :n], ps[bb][:, :n], ALU.mult, [sn_, psn[bb]], [gname])
                        r0 = part * 2048
                        P.dma('pool', ('stg', it % 2), (S['G'][r0:r0 + 2048, t0:t0 + n].rearrange("(c p) t -> p c t", p=128), G_[:, :, :n]),
                              reads=[gname], writes=[('S_G', part, t0)])
                        it += 1
            with P.phase():
                W = P.sbuf('Wffo', [128, 32, 1024], BF16)
                for bi in range(8):
                    P.dma('pool', ('w', bi), (W[:, bi * 4:(bi + 1) * 4, :], Wout[bi * 512:(bi + 1) * 512, :].rearrange("(kc p) c -> p kc c", p=128)),
                          writes=[('W', bi)])
                wreads = [('W', bi) for bi in range(8)]
                bufs = res_bufs(final)
                Gt = [P.sbuf('Gt%d' % i, [128, 32, 512], BF16) for i in range(2)]
                Rts = [P.sbuf('Rt%d' % i, [128, 8, 512], F32) for i in range(2)]
                pend = [None]
                for ti, tl in enumerate(cfg.tiles):
                    n, t0 = tl['n'], tl['t0']
                    Rt, rn = Rts[ti % 2], ('Rt', ti % 2)
                    P.dma('sp', ('ldR', ti % 2), (Rt[:, :, :n], S['R'][:, t0:t0 + n].rearrange("(c p) t -> p c t", p=128)), writes=[rn])
                    G_, gname = Gt[ti % 2], ('Gt', ti % 2)
                    P.dma('sp', ('ldg', ti % 2), [(G_[:, q * 8:(q + 1) * 8, :n], S['G'][q * 1024:(q + 1) * 1024, t0:t0 + n].rearrange("(c p) t -> p c t", p=128))
                                                  for q in range(4)], writes=[gname])
                    for dm in range(8):
                        b = nb()
                        P.mm([('M', ps[b][:, :n], W[:, kc, dm * 128:(dm + 1) * 128], G_[:, kc, :n], kc == 0, kc == 31)
                              for kc in range(32)], wreads + [gname], [psn[b]])
                        P.stt(Rt[:, dm, :n], ps[b][:, :n], 0.5, Rt[:, dm, :n], ALU.mult, ALU.add, [psn[b], rn], [rn])
                        if dm == 2 and pend[0] is not None:
                            pend[0]()
                            pend[0] = None
                    pend[0] = finish_residual(tl, Rt, rn, gi, bufs, final=final, store_R=not final, defer=True)
                if pend[0] is not None:
                    pend[0]()

        def mixer_inproj(l):
            Win = I['win'][l]
            with P.phase():
                W = P.sbuf('Wxbc', [128, 4, 8, 512], BF16)
                for bi in range(4):
                    col = C_XBC + bi * 512
                    P.dma('pool', ('w', bi), (W[:, bi, :, :], Win[:, col:col + 512].rearrange("(kc p) c -> p kc c", p=128)), writes=[('W', bi)])
                XNt = [P.sbuf('XNt%d' % i, [128, 8, 512], BF16) for i in range(2)]
                carry = P.sbuf('carry', [128, 16, 4, 3], F32)
                raw8 = [P.sbuf('raw%d' % i, [128, 515], F32) for i in range(8)]
                cacc8 = [P.sbuf('cacc%d' % i, [128, 512], F32) for i in range(8)]
                ptmp = P.sbuf('ptmp', [128, 512], F32)
                xc8 = [P.sbuf('xc%d' % i, [128, 512], BF16) for i in range(8)]
                xst = [P.sbuf('xst%d' % i, [128, 4, 1024], BF16) for i in range(2)]
                bst = [P.sbuf('bst%d' % i, [128, 4, 512], BF16) for i in range(2)]
                qn = 0
                for ti, tl in enumerate(cfg.tiles):
                    n, t0, Q, nsub = tl['n'], tl['t0'], tl['Q'], tl['nsub']
                    X, xname = XNt[ti % 2], ('XNt', ti % 2)
                    P.dma('sp', ('ldxn', ti % 2), (X[:, :, :n], S['XN'][:, t0:t0 + n].rearrange("(c p) t -> p c t", p=128)), writes=[xname])
                    if tl['prompt']:
                        segs = [(0, n)]
                        if tl['first']:
                            P.memset(carry[:, :, 0, :], 0.0, [('carry', ct) for ct in range(16)])
                    else:
                        segs = [(s * 64, 64) for s in range(NS)]
                        for s in range(NS):
                            P.dma('sp', 'ldcar', [(carry[:, ct, s, :], I['sconv'][l, s, :, ct * 128:(ct + 1) * 128].rearrange("j p -> p j"))
                                                  for ct in range(16)], writes=[('carry', ct) for ct in range(16)], allow_slow_non_contiguous=True)
                    XS, xsn = xst[ti % 2], ('xst', ti % 2)
                    BS, bsn = bst[ti % 2], ('bst', ti % 2)
                    bankd = {}

                    def bufs_of(cg):
                        o8 = (cg % 2) * 4
                        return o8, raw8[o8:o8 + 4], cacc8[o8:o8 + 4], xc8[o8:o8 + 4], [cg * 4 + k for k in range(4)]

                    def st_mm(cg):
                        o8, raw4, cacc4, xc, cts = bufs_of(cg)
                        bankd[cg] = []
                        for k, ct in enumerate(cts):
                            b = nb()
                            bankd[cg].append(b)
                            bi, ch = ct // 4, ct % 4
                            P.mm([('M', ps[b][:, :n], W[:, bi, kc, ch * 128:(ch + 1) * 128], X[:, kc, :n], kc == 0, kc == 7)
                                  for kc in range(8)], [('W', bi), xname], [psn[b]])

                    def st_a(cg, si):
                        o8, raw4, cacc4, xc, cts = bufs_of(cg)
                        c0, ln = segs[si]
                        banks = bankd[cg]
                        for k, ct in enumerate(cts):
                            P.copy(raw4[k][:, 3:3 + ln], ps[banks[k]][:, c0:c0 + ln], [psn[banks[k]]], [('raw', o8 + k)], eng='act')
                        for k, ct in enumerate(cts):
                            P.copy(raw4[k][:, 0:3], carry[:, ct, si, :], [('carry', ct)], [('raw', o8 + k)], eng='dve')
                        for k, ct in enumerate(cts):
                            P.act(cacc4[k][:, :ln], raw4[k][:, 0:ln], AF.Identity, [('raw', o8 + k), 'convw', 'convb'], [('cacc', o8 + k)],
                                  scale=convw[:, l, ct, 0:1], bias=convb[:, l, ct:ct + 1])

                    def st_b(cg, si):
                        o8, raw4, cacc4, xc, cts = bufs_of(cg)
                        c0, ln = segs[si]
                        for j in range(1, 4):
                            for k, ct in enumerate(cts):
                                P.stt(cacc4[k][:, :ln], raw4[k][:, j:j + ln], convw[:, l, ct, j:j + 1], cacc4[k][:, :ln], ALU.mult, ALU.add,
                                      [('raw', o8 + k), ('cacc', o8 + k), 'convw'], [('cacc', o8 + k)], eng='dve')
                        for k, ct in enumerate(cts):
                            P.copy(carry[:, ct, si, :], raw4[k][:, ln:ln + 3], [('raw', o8 + k)], [('carry', ct)], eng='dve')
                        for k, ct in enumerate(cts):
                            P.act(xc[k][:, c0:c0 + ln], cacc4[k][:, :ln], AF.Silu, [('cacc', o8 + k)], [('xc', o8 + k)])

                    def st_c(cg):
                        o8, raw4, cacc4, xc, cts = bufs_of(cg)
                        for k, ct in enumerate(cts):
                            xc_, xcn = xc[k], ('xc', o8 + k)
                            if ct < 12:
                                b2 = nb()
                                psb = ps[b2][:].bitcast(BF16)
                                P.mm([('T', psb[:Q, j * 128:(j + 1) * 128], xc_[:, j * Q:(j + 1) * Q], idb[:]) for j in range(nsub)],
                                     [xcn, 'idb'], [psn[b2]])
                                src = psb[:Q, 0:nsub * 128].rearrange("p (j c) -> p j c", j=nsub)
                                if ct < 8:
                                    P.copy(XS[:Q, :nsub, ct * 128:(ct + 1) * 128], src, [psn[b2]], [xsn], eng='act')
                                else:
                                    P.copy(BS[:Q, :nsub, (ct - 8) * 128:(ct - 7) * 128], src, [psn[b2]], [bsn], eng='act')
                            if ct >= 8:
                                dst = S['Bfm'] if ct < 12 else S['Cfm']
                                g = (ct - 8) % 4
                                P.dma('pool', ('stfm', ct % 8), (dst[g * 128:(g + 1) * 128, t0:t0 + n], xc_[:, :n]), reads=[xcn],
                                      writes=[('S_fm', ct, t0)])

                    if len(segs) == 1:
                        st_mm(0)
                        st_a(0, 0)
                        for cg in range(4):
                            if cg + 1 < 4:
                                st_mm(cg + 1)
                                st_a(cg + 1, 0)
                            st_b(cg, 0)
                            st_c(cg)
                    else:
                        for cg in range(4):
                            st_mm(cg)
                            for si in range(len(segs)):
                                st_a(cg, si)
                                st_b(cg, si)
                            st_c(cg)
                    P.dma('pool', ('stx', ti % 2), (S['xtm'][t0:t0 + n, :].rearrange("(j p) c -> p j c", p=Q), XS[:Q, :nsub, :]),
                          reads=[xsn], writes=[('S_xtm', t0)])
                    P.dma('pool', ('stb', ti % 2), (S['Btm'][t0:t0 + n, :].rearrange("(j p) c -> p j c", p=Q), BS[:Q, :nsub, :]),
                          reads=[bsn], writes=[('S_Btm', t0)])
                    if tl['last']:
                        for si in range(len(segs)):
                            if tl['prompt']:
                                dst = O['convp'][l, tl['seq']]
                            else:
                                dst = O['convs'][l, si]
                            out_toks.append(P.dma('pool', 'stcar', [(dst[:, ct * 128:(ct + 1) * 128].rearrange("j p -> p j"), carry[:, ct, si, :])
                                                                  for ct in range(16)], reads=[('carry', ct) for ct in range(16)], writes=[('convout', ti, si)],
                                                  allow_slow_non_contiguous=True))
            if getattr(cfg, 'sub', 9) < 2:
                return
            with P.phase():
                W = P.sbuf('Wfm', [128, 4, 8, 512], BF16)
                cols = [C_AQ, C_AK, C_MQ, C_MK]
                dsts = [S['q'], S['k'], S['mq'], S['mkf']]
                for bi in range(4):
                    P.dma('pool', ('w', bi), (W[:, bi, :, :], Win[:, cols[bi]:cols[bi] + 512].rearrange("(kc p) c -> p kc c", p=128)), writes=[('W', bi)])
                XNt = [P.sbuf('XNt%d' % i, [128, 8, 512], BF16) for i in range(2)]
                fst = [P.sbuf('fst%d' % i, [128, 4, 512], BF16) for i in range(3)]
                q = 0
                for ti, tl in enumerate(cfg.tiles):
                    n, t0 = tl['n'], tl['t0']
                    X, xname = XNt[ti % 2], ('XNt', ti % 2)
                    P.dma('sp', ('ldxn', ti % 2), (X[:, :, :n], S['XN'][:, t0:t0 + n].rearrange("(c p) t -> p c t", p=128)), writes=[xname])
                    for bi in range(4):
                        F_, fn_ = fst[q % 3], ('fst', q % 3)
                        for ch in range(4):
                            b = nb()
                            P.mm([('M', ps[b][:, :n], W[:, bi, kc, ch * 128:(ch + 1) * 128], X[:, kc, :n], kc == 0, kc == 7)
                                  for kc in range(8)], [('W', bi), xname], [psn[b]])
                            P.copy(F_[:, ch, :n], ps[b][:, :n], [psn[b]], [fn_])
                        P.dma('pool', ('stf', q % 3), (dsts[bi][:, t0:t0 + n].rearrange("(c p) t -> p c t", p=128), F_[:, :, :n]),
                              reads=[fn_], writes=[('S_f', bi, t0)])
                        q += 1
            if getattr(cfg, 'sub', 9) < 3:
                return
            with P.phase():
                W = P.sbuf('Wtm', [128, 7, 8, 512], BF16)
                cols = [C_Z, C_Z + 512, C_AV, C_AK, C_MK, C_MV, C_MO]
                for bi in range(7):
                    P.dma('pool', ('w', bi), (W[:, bi, :, :], Win[:, cols[bi]:cols[bi] + 512].rearrange("(kc p) c -> p kc c", p=128)), writes=[('W', bi)])
                Wsm = P.sbuf('Wsm', [128, 8, 24], BF16)
                P.dma('pool', 'wsm', [(Wsm[:, :, 0:16], Win[:, C_DT:C_DT + 16].rearrange("(kc p) c -> p kc c", p=128)),
                                      (Wsm[:, :, 16:24], Win[:, C_MI:C_MI + 8].rearrange("(kc p) c -> p kc c", p=128))],
                      writes=['Wsm'], allow_slow_non_contiguous=True)
                XNt = [P.sbuf('XNt%d' % i, [128, 8, 512], BF16) for i in range(2)]
                zst = [P.sbuf('zst%d' % i, [128, 4, 1024], BF16) for i in range(2)]
                vst = [P.sbuf('vst%d' % i, [128, 4, 8, 65], BF16) for i in range(2)]
                mkst = [P.sbuf('mkst%d' % i, [128, 4, 512], BF16) for i in range(2)]
                mvst = [P.sbuf('mvst%d' % i, [128, 4, 4, 129], BF16) for i in range(2)]
                most = [P.sbuf('most%d' % i, [128, 4, 512], BF16) for i in range(2)]
                smst = [P.sbuf('smst%d' % i, [128, 4, 24], F32) for i in range(2)]
                kout = P.sbuf('kout', [128, 4, 512], F32)
                vout = P.sbuf('vout', [128, 4, 512], F32)
                for i in range(2):
                    if os.environ.get('DBG_B') == '1':
                        break
                    P.memset(vst[i][:, :, :, 64:65], 1.0, [('vst', i)])
                    P.memset(mvst[i][:, :, :, 128:129], 1.0, [('mvst', i)])
                for ti, tl in enumerate(cfg.tiles):
                    n, t0, Q, nsub = tl['n'], tl['t0'], tl['Q'], tl['nsub']
                    pz = ti % 2
                    X, xname = XNt[pz], ('XNt', pz)
                    P.dma('sp', ('ldxn', pz), (X[:, :, :n], S['XN'][:, t0:t0 + n].rearrange("(c p) t -> p c t", p=128)), writes=[xname])
                    need_kv = tl['last'] if tl['prompt'] else True
                    if os.environ.get('DBG_C') == '1':
                        need_kv = False
                    for j in range(nsub):
                        xs_ = lambda kc: X[:, kc, j * Q:(j + 1) * Q]
                        for bi in range(7):
                            if bi == 3 and (not need_kv or os.environ.get('DBG_C') == '3'):
                                continue
                            b = nb()
                            P.mm([('M', ps[b][:Q, :], xs_(kc), W[:, bi, kc, :], kc == 0, kc == 7) for kc in range(8)],
                                 [('W', bi), xname], [psn[b]])
                            src = ps[b][:Q, :]
                            if bi < 2:
                                P.act(zst[pz][:Q, j, bi * 512:(bi + 1) * 512], src, AF.Silu, [psn[b]], [('zst', pz)])
                            elif bi == 2:
                                P.copy(vst[pz][:Q, j, :, 0:64], src.rearrange("p (h d) -> p h d", h=8), [psn[b]], [('vst', pz)], eng='dve')
                                if need_kv and os.environ.get('DBG_C') != '2' and os.environ.get('DBG_D') != '1':
                                    P.copy(vout[:Q, j, :], src, [psn[b]], ['vout'], eng='dve')
                            elif bi == 3:
                                P.copy(kout[:Q, j, :], src, [psn[b]], ['kout'])
                            elif bi == 4:
                                P.copy(mkst[pz][:Q, j, :], src, [psn[b]], [('mkst', pz)])
                            elif bi == 5:
                                P.copy(mvst[pz][:Q, j, :, 0:128], src.rearrange("p (h d) -> p h d", h=4), [psn[b]], [('mvst', pz)])
                            else:
                                P.act(most[pz][:Q, j, :], src, AF.Sigmoid, [psn[b]], [('most', pz)])
                        if os.environ.get('DBG_A') != '1':
                            b = nb()
                            P.mm([('M', ps[b][:Q, :24], xs_(kc), Wsm[:, kc, :], kc == 0, kc == 7) for kc in range(8)],
                                 ['Wsm', xname], [psn[b]])
                            P.copy(smst[pz][:Q, j, :], ps[b][:Q, :24], [psn[b]], [('smst', pz)])
                    tmv = lambda d: d[t0:t0 + n, :].rearrange("(j p) c -> p j c", p=Q)
                    P.dma('pool', ('st0', pz), (tmv(S['z']), zst[pz][:Q, :nsub, :]), reads=[('zst', pz)], writes=[('S_z', t0)])
                    P.dma('pool', ('st1', pz), (tmv(S['v']), vst[pz][:Q, :nsub].rearrange("p j h d -> p j (h d)")), reads=[('vst', pz)], writes=[('S_v', t0)])
                    P.dma('pool', ('st2', pz), (tmv(S['mkt']), mkst[pz][:Q, :nsub, :]), reads=[('mkst', pz)], writes=[('S_mkt', t0)])
                    P.dma('pool', ('st3', pz), (tmv(S['mv']), mvst[pz][:Q, :nsub].rearrange("p j h d -> p j (h d)")), reads=[('mvst', pz)], writes=[('S_mv', t0)])
                    P.dma('pool', ('st4', pz), (tmv(S['mo']), most[pz][:Q, :nsub, :]), reads=[('most', pz)], writes=[('S_mo', t0)])
                    P.dma('pool', ('st5', pz), (tmv(S['sm']), smst[pz][:Q, :nsub, :]), reads=[('smst', pz)], writes=[('S_sm', t0)])
                    if need_kv:
                        if tl['prompt']:
                            dk = O['kp'][l, tl['seq']].rearrange("(j p) c -> p j c", p=128)
                            dv = O['vp'][l, tl['seq']].rearrange("(j p) c -> p j c", p=128)
                        else:
                            dk = O['ks'][l].rearrange("s p c -> p s c")
                            dv = O['vs'][l].rearrange("s p c -> p s c")
                        if os.environ.get('DBG_C') != '3':
                            out_toks.append(P.dma('pool', 'stk', (dk, kout[:Q, :nsub, :]), reads=['kout'], writes=[('kout_d', ti)]))
                        if os.environ.get('DBG_D') == '2':
                            dv = dk
                        if os.environ.get('DBG_C') != '2':
                            out_toks.append(P.dma('pool', 'stv', (dv, vout[:Q, :nsub, :]), reads=['vout'], writes=[('vout_d', ti)]))

        def mixer_core(l):
            with P.phase():
                ssdn = P.sbuf('ssdn', [128, 1024], F32)
                mln = P.sbuf('mln', [128, 512], F32)
                ab = P.sbuf('ab', [128, 4, 8, 128], F32)
                P.dma('sp', 'ldp', [(ssdn[:], I['ssdn'][:, l, :]), (mln[:], I['mln'][:, l, :]), (ab[:], I['abias'][l])],
                      writes=['ssdn', 'mln', 'ab'])
                xt = P.sbuf('xt', [128, 4, 1024], BF16)
                Bt = P.sbuf('Bt', [128, 4, 512], BF16)
                zt = P.sbuf('zt', [128, 4, 1024], BF16)
                mkt = P.sbuf('mkt', [128, 4, 512], BF16)
                mvt = P.sbuf('mvt', [128, 4, 4, 129], BF16)
                mot = P.sbuf('mot', [128, 4, 512], BF16)
                smt = P.sbuf('smt', [128, 4, 24], F32)
                Bf = P.sbuf('Bf', [128, 4, 512], BF16)
                Cf = P.sbuf('Cf', [128, 4, 512], BF16)
                qf = P.sbuf('qf', [128, 4, 512], BF16)
                mqf = P.sbuf('mqf', [128, 4, 512], BF16)
                mkf = P.sbuf('mkf', [128, 4, 512], BF16)
                KH = P.sbuf('KH', [128, 4, 8, 128], BF16)
                VH = P.sbuf('VH', [128, 8, 8, 65], BF16)
                h32 = P.sbuf('h32', [128, 1024], F32)
                hbf = P.sbuf('hbf', [128, 1024], BF16)
                Cn32 = P.sbuf('Cn32', [128, 4, 129], F32)
                Cnbf = P.sbuf('Cnbf', [128, 4, 129], BF16)
                mbc = P.sbuf('mbc', [128, 4], F32)
                dtt = P.sbuf('dtt', [128, 4, 16], F32)
                dat = P.sbuf('dat', [128, 4, 16], F32)
                tmp16 = P.sbuf('tmp16', [128, 4, 16], F32)
                tmp16b = P.sbuf('tmp16b', [128, 4, 16], F32)
                lft = P.sbuf('lft', [128, 4, 4], F32)
                igt = P.sbuf('igt', [128, 4, 4], F32)
                tmp4 = P.sbuf('tmp4', [128, 4, 4], F32)
                tmp4b = P.sbuf('tmp4b', [128, 4, 4], F32)
                R1 = [P.sbuf('R1_%d' % i, [128, 4, 128], F32) for i in range(2)]
                Eg = [P.sbuf('Eg%d' % i, [128, 4, 128], BF16) for i in range(2)]
                wT = [P.sbuf('wT%d' % i, [128, 4, 128], BF16) for i in range(2)]
                CBM = P.sbuf('CBM', [128, 4, 128], BF16)
                xdt = P.sbuf('xdt', [128, 1024], BF16)
                xD = P.sbuf('xD', [128, 1024], BF16)
                xw = P.sbuf('xw', [128, 1024], BF16)
                ecum = P.sbuf('ecum', [128, 16], F32)
                wd = P.sbuf('wd', [128, 16], F32)
                dec = P.sbuf('dec', [128, 16], F32)
                t1 = P.sbuf('t1', [128, 1024], F32)
                t2 = P.sbuf('t2', [128, 1024], F32)
                ss1 = P.sbuf('ss1', [128, 1], F32)
                ytm = P.sbuf('ytm', [128, 2048], BF16)
                Am = P.sbuf('Am', [128, 4, 128], F32)
                Bg = P.sbuf('Bg', [128, 4, 128], F32)
                LD = P.sbuf('LD', [128, 4, 128], F32)
                Dm = P.sbuf('Dm', [128, 4, 128], BF16)
                Sm = P.sbuf('Sm', [128, 4, 128], BF16)
                STm = P.sbuf('STm', [128, 4, 128], BF16)
                tot = P.sbuf('tot', [128, 4, 129], F32)
                hh = P.sbuf('hh', [128, 4, 128], F32)
                kw = P.sbuf('kw', [128, 4, 128], BF16)
                s4 = {nm: P.sbuf('s4' + nm, [128, 4], F32) for nm in
                      ['mx', 'b', 'inter', 'mt', 'nmt', 'ei', 'emt', 'den', 'ssq', 'rstd', 'wlog', 'w', 'mnew', 'blast', 'decay']}
                sb = [P.sbuf('sb%d' % i, [128, 512], F32) for i in range(2)]
                Pex = [P.sbuf('Pex%d' % i, [128, 8, 128], BF16) for i in range(2)]
                rs8 = P.sbuf('rs8', [128, 8], F32)
                Yst = [P.sbuf('Yst%d' % i, [128, 16, 512], BF16) for i in range(1)]
                stf = P.sbuf('stf', [128, 8, 128], F32)
                cst = P.sbuf('cst', [128, 4, 512], F32)
                cstb = P.sbuf('cstb', [128, 4, 512], BF16)

                def s4v(nm, Q):
                    return s4[nm][:Q, :]

                def load_tile(tl):
                    n, t0, Q, nsub = tl['n'], tl['t0'], tl['Q'], tl['nsub']
                    tmv = lambda d: d[t0:t0 + n, :].rearrange("(j p) c -> p j c", p=Q)
                    fmv = lambda d: d[:, t0:t0 + n].rearrange("(c p) t -> p c t", p=128)
                    P.dma('sp', 'lc0', (xt[:Q, :nsub, :], tmv(S['xtm'])), reads=[('S_xtm', t0)], writes=['xt'])
                    P.dma('sp', 'lc1', (Bt[:Q, :nsub, :], tmv(S['Btm'])), writes=['Bt'])
                    P.dma('sp', 'lc2', (zt[:Q, :nsub, :], tmv(S['z'])), writes=['zt'])
                    P.dma('sp', 'lc3', (mkt[:Q, :nsub, :], tmv(S['mkt'])), writes=['mkt'])
                    P.dma('sp', 'lc4', (mvt[:Q, :nsub].rearrange("p j h d -> p j (h d)"), tmv(S['mv'])), writes=['mvt'])
                    P.dma('sp', 'lc5', (mot[:Q, :nsub, :], tmv(S['mo'])), writes=['mot'])
                    P.dma('sp', 'lc6', (smt[:Q, :nsub, :], tmv(S['sm'])), writes=['smt'])
                    P.dma('sp', 'lc7', (Bf[:, :, :n], fmv(S['Bfm'])), writes=['Bf'])
                    P.dma('sp', 'lc8', (Cf[:, :, :n], fmv(S['Cfm'])), writes=['Cf'])
                    P.dma('sp', 'lc9', (qf[:, :, :n], fmv(S['q'])), writes=['qf'])
                    P.dma('sp', 'lc10', (mqf[:, :, :n], fmv(S['mq'])), writes=['mqf'])
                    P.dma('sp', 'lc11', (mkf[:, :, :n], fmv(S['mkf'])), writes=['mkf'])

                def prep_small(tl):
                    Q, nsub = tl['Q'], tl['nsub']
                    v16 = lambda t: t[:Q, :nsub, :]
                    rb = lambda i: rep16[:Q, l, i, :].unsqueeze(1).to_broadcast([Q, nsub, 16])
                    P.tt(v16(tmp16), smt[:Q, :nsub, 0:16], rb(0), ALU.add, ['smt', 'rep16'], ['tmp16'])
                    P.ts(v16(tmp16b), v16(tmp16), -1.0, None, ALU.mult, None, ['tmp16'], ['tmp16b'])
                    P.tt(v16(tmp16b), v16(tmp16b), v16(tmp16), ALU.max, ['tmp16', 'tmp16b'], ['tmp16b'])
                    P.act(v16(tmp16b), v16(tmp16b), AF.Exp, ['tmp16b'], ['tmp16b'], scale=-1.0)
                    P.act(v16(tmp16b), v16(tmp16b), AF.Ln, ['tmp16b', 'cm'], ['tmp16b'], bias=ONES[:Q, 0:1])
                    P.ts(v16(tmp16), v16(tmp16), 0.0, None, ALU.max, None, ['tmp16'], ['tmp16'])
                    P.tt(v16(dtt), v16(tmp16), v16(tmp16b), ALU.add, ['tmp16', 'tmp16b'], ['dtt'])
                    P.tt(v16(dat), v16(dtt), arep[:Q, l, :].unsqueeze(1).to_broadcast([Q, nsub, 16]), ALU.mult, ['dtt', 'arep'], ['dat'])
                    v4 = lambda t: t[:Q, :nsub, :]
                    gb = lambda o: gbr[:Q, l, o:o + 4].unsqueeze(1).to_broadcast([Q, nsub, 4])
                    P.tt(v4(igt), smt[:Q, :nsub, 16:20], gb(0), ALU.add, ['smt', 'gbr'], ['igt'])
                    P.tt(v4(tmp4), smt[:Q, :nsub, 20:24], gb(4), ALU.add, ['smt', 'gbr'], ['tmp4'])
                    P.ts(v4(tmp4b), v4(tmp4), -1.0, None, ALU.mult, None, ['tmp4'], ['tmp4b'])
                    P.tt(v4(tmp4b), v4(tmp4b), v4(tmp4), ALU.max, ['tmp4', 'tmp4b'], ['tmp4b'])
                    P.act(v4(tmp4b), v4(tmp4b), AF.Exp, ['tmp4b'], ['tmp4b'], scale=-1.0)
                    P.act(v4(tmp4b), v4(tmp4b), AF.Ln, ['tmp4b', 'cm'], ['tmp4b'], bias=ONES[:Q, 0:1])
                    P.ts(v4(tmp4), v4(tmp4), 0.0, None, ALU.min, None, ['tmp4'], ['tmp4'])
                    P.tt(v4(lft), v4(tmp4), v4(tmp4b), ALU.subtract, ['tmp4', 'tmp4b'], ['lft'])

                def init_state_zero():
                    P.memset(h32[:], 0.0, ['h32'])
                    P.memset(hbf[:], 0.0, ['hbf'])
                    P.memset(Cn32[:], 0.0, ['Cn32'])
                    P.memset(Cnbf[:], 0.0, ['Cnbf'])
                    P.memset(mbc[:], 0.0, ['mbc'])

                def init_state_sample(s):
                    P.dma('sp', 'ls0', (stf[:], I['sssd'][l, s].rearrange("(c p) n -> p c n", p=128)), writes=['stf'])
                    for half in range(2):
                        b = nb()
                        P.mm([('T', ps[b][:, cc * 128:(cc + 1) * 128], stf[:, half * 4 + cc, :], IDF) for cc in range(4)],
                             ['stf', 'cm'], [psn[b]])
                        P.copy(h32[:, half * 512:(half + 1) * 512], ps[b][:, :], [psn[b]], ['h32'])
                    P.copy(hbf[:], h32[:], ['h32'], ['hbf'])
                    P.dma('sp', 'ls1', (Cn32[:, :, 0:128], I['smc'][l, s].rearrange("h k v -> k h v")), writes=['Cn32'])
                    P.dma('sp', 'ls2', (Cn32[:, :, 128:129], I['smn'][l, s].rearrange("h (k o) -> k h o", o=1)), writes=['Cn32'],
                          allow_slow_non_contiguous=True)
                    P.copy(Cnbf[:], Cn32[:], ['Cn32'], ['Cnbf'])
                    P.dma('sp', 'ls3', (mbc[:], I['smm'][l, s].partition_broadcast(128)), writes=['mbc'])

                def store_state(prompt, s):
                    od = (O['ssdp'], O['mcp'], O['mnp'], O['mmp']) if prompt else (O['ssds'], O['mcs'], O['mns'], O['mms'])
                    for half in range(2):
                        b = nb()
                        P.mm([('T', ps[b][:, cc * 128:(cc + 1) * 128], h32[:, (half * 4 + cc) * 128:(half * 4 + cc + 1) * 128], IDF)
                              for cc in range(4)], ['h32', 'cm'], [psn[b]])
                        P.copy(stf[:, half * 4:(half + 1) * 4, :], ps[b][:, :].rearrange("p (c n) -> p c n", c=4), [psn[b]], ['stf'])
                    key = (prompt, s, l)
                    out_toks.append(P.dma('pool', 'ss0', (od[0][l, s].rearrange("(c p) n -> p c n", p=128), stf[:]), reads=['stf'],
                                          writes=[('o_ssd', key)]))
                    out_toks.append(P.dma('pool', 'ss1', (od[1][l, s].rearrange("h k v -> k h v"), Cn32[:, :, 0:128]), reads=['Cn32'],
                                          writes=[('o_c', key)]))
                    out_toks.append(P.dma('pool', 'ss2', (od[2][l, s].rearrange("h (k o) -> k h o", o=1), Cn32[:, :, 128:129]), reads=['Cn32'],
                                          writes=[('o_n', key)], allow_slow_non_contiguous=True))
                    out_toks.append(P.dma('pool', 'ss3', (od[3][l, s:s + 1, :], mbc[0:1, :]), reads=['mbc'], writes=[('o_m', key)]))

                def ssd_chunk(Q, j, c0):
                    Uq, SUq, Iq = U[:Q, :Q], SU[:Q, :Q], IDF[:Q, :Q]
                    da_j = dat[:Q, j, :]
                    b = 0
                    P.mm([('M', ps[b][:Q, 0:16], Uq, da_j, True, True),
                          ('M', ps[b][:Q, 16:32], SUq, da_j, True, True),
                          ('M', ps[b][:, 32:48], ONES[:Q, :], da_j, True, True)], ['cm', 'dat'], [psn[b]])
                    x3 = xt[:Q, j, :].rearrange("p (h d) -> p h d", h=16)
                    bc16 = lambda a: a.unsqueeze(2).to_broadcast([Q, 16, 64])
                    P.tt(xdt[:Q, :].rearrange("p (h d) -> p h d", h=16), x3, bc16(dtt[:Q, j, :]), ALU.mult, ['xt', 'dtt'], ['xdt'])
                    P.tt(xD[:Q, :].rearrange("p (h d) -> p h d", h=16), x3, bc16(rep16[:Q, l, 2, :]), ALU.mult, ['xt', 'rep16'], ['xD'])
                    yield
                    P.act(ecum[:Q, :], ps[b][:Q, 0:16], AF.Exp, [psn[b]], ['ecum'])
                    P.act(wd[:Q, :], ps[b][:Q, 16:32], AF.Exp, [psn[b]], ['wd'])
                    P.act(dec[:, :], ps[b][:, 32:48], AF.Exp, [psn[b]], ['dec'])
                    b = 1
                    P.mm([('M', ps[b][:Q, g * 128:g * 128 + Q], Bf[:, g, c0:c0 + Q], Cf[:, g, c0:c0 + Q], True, True) for g in range(4)],
                         ['Bf', 'Cf'], [psn[b]])
                    yield
                    P.tt(CBM[:Q, :, :Q], ps[b][:Q, :].rearrange("p (g t) -> p g t", g=4)[:, :, :Q],
                         Uq.unsqueeze(1).to_broadcast([Q, 4, Q]), ALU.mult, [psn[b], 'cm'], ['CBM'])
                    P.tt(xw[:Q, :].rearrange("p (h d) -> p h d", h=16), xdt[:Q, :].rearrange("p (h d) -> p h d", h=16), bc16(wd[:Q, :]),
                         ALU.mult, ['xdt', 'wd'], ['xw'])
                    bi_ = [2, 0]
                    P.mm([('M', ps[bi_[g // 2]][:Q, (g % 2) * 256:(g % 2 + 1) * 256], Cf[:, g, c0:c0 + Q], hbf[:, g * 256:(g + 1) * 256], True, True)
                          for g in range(4)], ['Cf', 'hbf'], [psn[bi_[0]], psn[bi_[1]]])
                    yield
                    for hf in range(2):
                        sl = slice(hf * 512, (hf + 1) * 512)
                        P.tt(t1[:Q, sl].rearrange("p (h d) -> p h d", h=8), ps[bi_[hf]][:Q, :].rearrange("p (h d) -> p h d", h=8),
                             ecum[:Q, hf * 8:(hf + 1) * 8].unsqueeze(2).to_broadcast([Q, 8, 64]), ALU.mult, [psn[bi_[hf]], 'ecum'], ['t1'])
                    by = [1, 2]
                    P.mm([('M', ps[by[hf]][:Q, :], idb[:Q, :Q], xD[:Q, hf * 512:(hf + 1) * 512], True, False) for hf in range(2)],
                         ['idb', 'xD'], [psn[by[0]], psn[by[1]]])
                    for g in range(4):
                        r1, r1n = R1[g % 2], ('R1', g % 2)
                        for r in range(4):
                            P.act(r1[:Q, r, :Q], Uq, AF.Identity, ['cm', 'dat'], [r1n], scale=dat[:Q, j, g * 4 + r:g * 4 + r + 1])
                        yield
                        eg, egn = Eg[g % 2], ('Eg', g % 2)
                        w_, wn = wT[g % 2], ('wT', g % 2)
                        b = 0
                        P.mm([('M', ps[b][:Q, r * 128:r * 128 + Q], SUq, r1[:Q, r, :Q], True, True) for r in range(4)],
                             ['cm', r1n], [psn[b]])
                        yield
                        P.act(eg[:Q, :, :Q], ps[b][:Q, :].rearrange("p (r t) -> p r t", r=4)[:, :, :Q], AF.Exp, [psn[b]], [egn])
                        yield
                        P.tt(w_[:Q, :, :Q], eg[:Q, :, :Q], CBM[:Q, g, :Q].unsqueeze(1).to_broadcast([Q, 4, Q]), ALU.mult,
                             [egn, 'CBM'], [wn])
                        yield
                        hf = g // 2
                        P.mm([('M', ps[by[hf]][:Q, ((g % 2) * 4 + r) * 64:((g % 2) * 4 + r + 1) * 64], w_[:Q, r, :Q],
                               xdt[:Q, (g * 4 + r) * 64:(g * 4 + r + 1) * 64], False, (g % 2 == 1 and r == 3)) for r in range(4)],
                             [wn, 'xdt'], [psn[by[hf]]])
                    yield
                    for hf in range(2):
                        sl = slice(hf * 512, (hf + 1) * 512)
                        P.tt(t1[:Q, sl], ps[by[hf]][:Q, :], t1[:Q, sl], ALU.add, [psn[by[hf]], 't1'], ['t1'])
                    bs_ = [0, 1]
                    P.mm([('M', ps[bs_[g // 2]][:, (g % 2) * 256:(g % 2 + 1) * 256], Bt[:Q, j, g * 128:(g + 1) * 128], xw[:Q, g * 256:(g + 1) * 256], True, True)
                          for g in range(4)], ['Bt', 'xw'], [psn[bs_[0]], psn[bs_[1]]])
                    P.tt(t2[:Q, :], t1[:Q, :], zt[:Q, j, :], ALU.mult, ['t1', 'zt'], ['t2'])
                    P.memset(ss1[:Q, :], 0.0, ['ss1'])
                    yield
                    P.act(t1[:Q, :], t2[:Q, :], AF.Square, ['t2', 'ss1'], ['t1', 'ss1'], accum_out=ss1[:Q, :])
                    P.act(ss1[:Q, :], ss1[:Q, :], AF.Sqrt, ['ss1', 'epst'], ['ss1'], bias=epst[:Q, 0:1], scale=1.0 / 1024)
                    P.tt(h32[:, :].rearrange("p (h d) -> p h d", h=16), h32[:, :].rearrange("p (h d) -> p h d", h=16),
                         dec[:, :].unsqueeze(2).to_broadcast([128, 16, 64]), ALU.mult, ['h32', 'dec'], ['h32'])
                    for hf in range(2):
                        sl = slice(hf * 512, (hf + 1) * 512)
                        P.tt(h32[:, sl], h32[:, sl], ps[bs_[hf]][:, :], ALU.add, ['h32', psn[bs_[hf]]], ['h32'])
                    yield
                    P.copy(hbf[:], h32[:], ['h32'], ['hbf'], eng='act')
                    P.op('dve', lambda e: e.reciprocal(out=ss1[:Q, :], in_=ss1[:Q, :]), ['ss1'], ['ss1'])
                    yield
                    P.stt(ytm[:Q, 0:1024], t2[:Q, :], ss1[:Q, 0:1], ssdn[:Q, :], ALU.mult, ALU.mult, ['t2', 'ss1', 'ssdn'], ['ytm_s'])
                    yield

                def mlstm_chunk(Q, j, c0):
                    Uq, SUq, Iq = U[:Q, :Q], SU[:Q, :Q], IDF[:Q, :Q]
                    EL = (EL128 if Q == 128 else EL64)[:Q, :]
                    lf_j, ig_j = lft[:Q, j, :], igt[:Q, j, :]
                    sc = 128 ** -0.5
                    for h in range(4):
                        P.act(Am[:Q, h, :Q], Uq, AF.Identity, ['cm', 'lft'], ['Am'], scale=lft[:Q, j, h:h + 1])
                        P.act(Bg[:Q, h, :Q], ONES[:Q, :Q], AF.Identity, ['cm', 'igt'], ['Bg'], scale=igt[:Q, j, h:h + 1])
                    b2 = 4
                    P.mm([('M', ps[b2][:Q, 0:4], Uq, lf_j, True, True), ('M', ps[b2][:Q, 4:8], SUq, lf_j, True, True)],
                         ['cm', 'lft'], [psn[b2]])
                    yield
                    b = 3
                    mms = []
                    for h in range(4):
                        mms.append(('M', ps[b][:Q, h * 128:h * 128 + Q], Am[:Q, h, :Q], SUq, True, False))
                        mms.append(('M', ps[b][:Q, h * 128:h * 128 + Q], Bg[:Q, h, :Q], Iq, False, True))
                    P.mm(mms, ['Am', 'Bg', 'cm'], [psn[b]])
                    P.copy(s4v('b', Q), ps[b2][:Q, 0:4], [psn[b2]], ['b'], eng='dve')
                    P.tt(s4v('wlog', Q), ps[b2][:Q, 4:8], ig_j, ALU.add, [psn[b2], 'igt'], ['wlog'])
                    P.tt(s4v('inter', Q), s4v('b', Q), mbc[:Q, :], ALU.add, ['b', 'mbc'], ['inter'])
                    yield
                    P.tt(LD[:Q, :, :Q], ps[b][:Q, :].rearrange("p (h s) -> p h s", h=4)[:, :, :Q], MN[:Q, :Q].unsqueeze(1).to_broadcast([Q, 4, Q]),
                         ALU.add, [psn[b], 'cm'], ['LD'])
                    b3 = 3
                    P.mm([('M', ps[b3][:Q, h * 128:h * 128 + Q], mqf[:, h, c0:c0 + Q], mkf[:, h, c0:c0 + Q], True, True) for h in range(4)],
                         ['mqf', 'mkf'], [psn[b3]])
                    yield
                    P.op('dve', lambda e: e.tensor_reduce(out=s4v('mx', Q), in_=LD[:Q, :, :Q], axis=AX.X, op=ALU.max), ['LD'], ['mx'])
                    yield
                    P.tt(s4v('mt', Q), s4v('mx', Q), s4v('inter', Q), ALU.max, ['mx', 'inter'], ['mt'])
                    yield
                    P.ts(s4v('nmt', Q), s4v('mt', Q), -1.0, None, ALU.mult, None, ['mt'], ['nmt'])
                    P.tt(s4v('ei', Q), s4v('inter', Q), s4v('mt', Q), ALU.subtract, ['inter', 'mt'], ['ei'])
                    yield
                    for h in range(4):
                        P.act(Dm[:Q, h, :Q], LD[:Q, h, :Q], AF.Exp, ['LD', 'nmt'], ['Dm'], bias=s4['nmt'][:Q, h:h + 1])
                    P.act(s4v('ei', Q), s4v('ei', Q), AF.Exp, ['ei'], ['ei'])
                    P.act(s4v('emt', Q), s4v('mt', Q), AF.Exp, ['mt'], ['emt'], scale=-1.0)
                    b5 = 4
                    P.mm([('M', ps[b5][:, 0:4], EL, s4v('mt', Q), True, True), ('M', ps[b5][:, 4:8], EL, s4v('b', Q), True, True)],
                         ['cm', 'mt', 'b'], [psn[b5]])
                    yield
                    P.stt(Sm[:Q, :, :Q], ps[b3][:Q, :].rearrange("p (h s) -> p h s", h=4)[:, :, :Q], sc, Dm[:Q, :, :Q], ALU.mult, ALU.mult,
                          [psn[b3], 'Dm'], ['Sm'])
                    P.copy(s4['mnew'][:, :], ps[b5][:, 0:4], [psn[b5]], ['mnew'], eng='dve')
                    P.tt(s4['decay'][:, :], ps[b5][:, 4:8], mbc[:, :], ALU.add, [psn[b5], 'mbc'], ['decay'])
                    yield
                    b4 = 4
                    psb = ps[b4][:].bitcast(BF16)
                    P.mm([('T', psb[:Q, h * 128:h * 128 + Q], Sm[:Q, h, :Q], idb[:Q, :Q]) for h in range(4)], ['Sm', 'idb'], [psn[b4]])
                    P.tt(s4['decay'][:, :], s4['decay'][:, :], s4['mnew'][:, :], ALU.subtract, ['decay', 'mnew'], ['decay'])
                    P.tt(s4v('w', Q), s4v('wlog', Q), s4['mnew'][:Q, :], ALU.subtract, ['wlog', 'mnew'], ['w'])
                    yield
                    P.copy(STm[:Q, :, :Q], psb[:Q, 0:512].rearrange("p (h t) -> p h t", h=4)[:, :, :Q], [psn[b4]], ['STm'], eng='act')
                    P.act(s4['decay'][:, :], s4['decay'][:, :], AF.Exp, ['decay'], ['decay'])
                    P.act(s4v('w', Q), s4v('w', Q), AF.Exp, ['w'], ['w'])
                    yield
                    P.ts(s4v('w', Q), s4v('w', Q), sc, None, ALU.mult, None, ['w'], ['w'])
                    P.tt(kw[:Q, :, :], mkt[:Q, j, :].rearrange("p (h d) -> p h d", h=4), s4v('w', Q).unsqueeze(2).to_broadcast([Q, 4, 128]),
                         ALU.mult, ['mkt', 'w'], ['kw'])
                    for hf in range(2):
                        P.mm([('M', ps[3][:Q, (h % 2) * 129:(h % 2 + 1) * 129], STm[:Q, h, :Q], mvt[:Q, j, h, :], True, True) for h in (2 * hf, 2 * hf + 1)],
                             ['STm', 'mvt'], [psn[3]])
                        P.mm([('M', ps[4][:Q, (h % 2) * 129:(h % 2 + 1) * 129], mqf[:, h, c0:c0 + Q], Cnbf[:, h, :], True, True) for h in (2 * hf, 2 * hf + 1)],
                             ['mqf', 'Cnbf'], [psn[4]])
                        yield
                        v = lambda bk: ps[bk][:Q, 0:258].rearrange("p (h d) -> p h d", h=2)
                        P.tt(tot[:Q, hf * 2:hf * 2 + 2, :], v(4), s4['ei'][:Q, hf * 2:hf * 2 + 2].unsqueeze(2).to_broadcast([Q, 2, 129]),
                             ALU.mult, [psn[4], 'ei'], ['tot'])
                        P.tt(tot[:Q, hf * 2:hf * 2 + 2, :], v(3), tot[:Q, hf * 2:hf * 2 + 2, :], ALU.add, [psn[3], 'tot'], ['tot'])
                        yield
                    bu = [3, 4]
                    P.mm([('M', ps[bu[h // 2]][:, (h % 2) * 129:(h % 2 + 1) * 129], kw[:Q, h, :], mvt[:Q, j, h, :], True, True) for h in range(4)],
                         ['kw', 'mvt'], [psn[bu[0]], psn[bu[1]]])
                    P.ts(s4v('den', Q), tot[:Q, :, 128], -1.0, None, ALU.mult, None, ['tot'], ['den'])
                    yield
                    P.tt(s4v('den', Q), s4v('den', Q), tot[:Q, :, 128], ALU.max, ['tot', 'den'], ['den'])
                    yield
                    P.tt(s4v('den', Q), s4v('den', Q), s4v('emt', Q), ALU.max, ['den', 'emt'], ['den'])
                    yield
                    P.op('dve', lambda e: e.reciprocal(out=s4v('den', Q), in_=s4v('den', Q)), ['den'], ['den'])
                    P.tt(Cn32[:, :, :], Cn32[:, :, :], s4['decay'][:, :].unsqueeze(2).to_broadcast([128, 4, 129]), ALU.mult, ['Cn32', 'decay'], ['Cn32'])
                    yield
                    P.tt(hh[:Q, :, :], tot[:Q, :, 0:128], s4v('den', Q).unsqueeze(2).to_broadcast([Q, 4, 128]), ALU.mult, ['tot', 'den'], ['hh'])
                    for hf in range(2):
                        P.tt(Cn32[:, hf * 2:hf * 2 + 2, :], Cn32[:, hf * 2:hf * 2 + 2, :], ps[bu[hf]][:, 0:258].rearrange("p (h d) -> p h d", h=2),
                             ALU.add, ['Cn32', psn[bu[hf]]], ['Cn32'])
                    yield
                    P.tt(LD[:Q, :, :], hh[:Q, :, :], hh[:Q, :, :], ALU.mult, ['hh'], ['LD'])
                    P.copy(Cnbf[:], Cn32[:], ['Cn32'], ['Cnbf'], eng='act')
                    P.copy(mbc[:, :], s4['mnew'][:, :], ['mnew'], ['mbc'], eng='dve')
                    yield
                    P.op('dve', lambda e: e.tensor_reduce(out=s4v('ssq', Q), in_=LD[:Q, :, :], axis=AX.X, op=ALU.add), ['LD'], ['ssq'])
                    yield
                    P.act(s4v('rstd', Q), s4v('ssq', Q), AF.Sqrt, ['ssq', 'epst'], ['rstd'], bias=epst[:Q, 0:1], scale=1.0 / 128)
                    yield
                    P.op('dve', lambda e: e.reciprocal(out=s4v('rstd', Q), in_=s4v('rstd', Q)), ['rstd'], ['rstd'])
                    yield
                    P.tt(hh[:Q, :, :], hh[:Q, :, :], s4v('rstd', Q).unsqueeze(2).to_broadcast([Q, 4, 128]), ALU.mult, ['hh', 'rstd'], ['hh'])
                    yield
                    hf_ = hh[:Q, :, :].rearrange("p h d -> p (h d)")
                    P.tt(hf_, hf_, mln[:Q, :], ALU.mult, ['hh', 'mln'], ['hh'])
                    yield
                    P.tt(ytm[:Q, 1536:2048], hf_, mot[:Q, j, :], ALU.mult, ['hh', 'mot'], ['ytm_m'])
                    yield

                def attn_chunk(Q, c0, keys):
                    HB, nbk = 4, 2
                    nkt = len(keys)
                    W_ = HB * Q
                    b = 5
                    for ki, (slot, nk, var) in enumerate(keys):
                        pe_, pen = Pex[ki % 2], ('Pex', ki % 2)
                        for bk in range(nbk):
                            po = bk * 64
                            mms = []
                            for hh_ in range(HB):
                                h = 2 * hh_ + bk
                                mms.append(('M', ps[b][:nk, hh_ * Q:(hh_ + 1) * Q], KH[po:po + 64, h // 2, slot, :nk], qf[po:po + 64, h // 2, c0:c0 + Q], True, True))
                            P.mm(mms, [('KH', slot), 'qf'], [psn[b]])
                            yield
                            s_, sn_ = sb[bk % 2], ('sb', bk % 2)
                            P.stt(s_[:nk, :W_].rearrange("p (h q) -> p h q", h=HB), ps[b][:nk, :W_].rearrange("p (h q) -> p h q", h=HB), 0.125,
                                  ab[:nk, var, bk::2, :Q], ALU.mult, ALU.add, [psn[b], 'ab'], [sn_])
                            yield
                            P.act(pe_[:nk, bk::2, :Q], s_[:nk, :W_].rearrange("p (h q) -> p h q", h=HB), AF.Exp, [sn_], [pen])
                            yield
                        P.mm([('M', ps[6 + h // 4][:Q, (h % 4) * 65:(h % 4 + 1) * 65], pe_[:nk, h, :Q], VH[:nk, slot, h, :], (ki == 0 and h % 4 == 0), (ki == nkt - 1 and h % 4 == 3))
                              for h in range(8)], [pen, ('VH', slot)], [psn[6], psn[7]])
                        yield
                    for hf in range(2):
                        o3 = ps[6 + hf][:Q, 0:260].rearrange("p (h d) -> p h d", h=4)
                        P.op('dve', lambda e, o3=o3, hf=hf: e.reciprocal(out=rs8[:Q, hf * 4:(hf + 1) * 4], in_=o3[:, :, 64]), [psn[6 + hf]], ['rs8'])
                        yield
                        P.tt(ytm[:Q, 1024 + hf * 256:1024 + (hf + 1) * 256].rearrange("p (h d) -> p h d", h=4), o3[:, :, 0:64],
                             rs8[:Q, hf * 4:(hf + 1) * 4].unsqueeze(2).to_broadcast([Q, 4, 64]), ALU.mult, [psn[6 + hf], 'rs8'], ['ytm_a'])
                        yield

                def run_chains(*gens):
                    gens = [g for g in gens if g is not None]
                    while gens:
                        for g in list(gens):
                            try:
                                next(g)
                            except StopIteration:
                                gens.remove(g)

                def y_transposes(Q, j, Y_):
                    for half in range(2):
                        b = nb()
                        psb = ps[b][:].bitcast(BF16)
                        P.mm([('T', psb[:, cc * 128:cc * 128 + Q], ytm[:Q, (half * 8 + cc) * 128:(half * 8 + cc + 1) * 128], idb[:Q, :Q]) for cc in range(8)],
                             ['ytm_s', 'ytm_a', 'ytm_m', 'idb'], [psn[b]])
                        P.copy(Y_[:, half * 8:(half + 1) * 8, j * Q:(j + 1) * Q], psb[:, :].rearrange("p (c t) -> p c t", c=8)[:, :, :Q], [psn[b]], ['Yst'])

                DC = os.environ.get('DBG_CORE', 'asmyiopq')
                _a, _s, _m, _y = attn_chunk, ssd_chunk, mlstm_chunk, y_transposes
                if 'a' not in DC:
                    attn_chunk = lambda *a, **k: None
                if 's' not in DC:
                    ssd_chunk = lambda *a, **k: None
                if 'm' not in DC:
                    mlstm_chunk = lambda *a, **k: None
                if 'y' not in DC:
                    y_transposes = lambda *a, **k: None
                if 'i' not in DC:
                    init_state_sample = lambda *a, **k: init_state_zero()
                if 'o' not in DC:
                    store_state = lambda *a, **k: None
                for ti, tl in enumerate(cfg.tiles):
                    n, t0, Q, nsub = tl['n'], tl['t0'], tl['Q'], tl['nsub']
                    if tl['prompt'] and 'q' not in DC:
                        continue
                    if not tl['prompt'] and 'p' not in DC:
                        continue
                    load_tile(tl)
                    prep_small(tl)
                    Y_ = Yst[0]
                    if tl['prompt']:
                        if tl['first']:
                            init_state_zero()
                        cur = (tl['i'] % 2) * 4
                        prev = 4 - cur
                        P.dma('sp', 'lk', (KH[:, :, cur:cur + 4, :], S['k'][:, t0:t0 + 512].rearrange("(c p) (s t) -> p c s t", p=128, s=4)),
                              writes=[('KH', cur + s) for s in range(4)])
                        P.dma('sp', 'lv', (VH[:, cur:cur + 4].rearrange("p s h d -> p s (h d)"), S['v'][t0:t0 + 512, :].rearrange("(s p) c -> p s c", p=128)),
                              writes=[('VH', cur + s) for s in range(4)])
                        for j in range(4):
                            keys = []
                            if not tl['first']:
                                for u in range(j, 4):
                                    m = 4 + j - u
                                    keys.append((prev + u, 128, {4: 3, 3: 2, 2: 2, 1: 1}[m]))
                            for u in range(0, j + 1):
                                m = j - u
                                keys.append((cur + u, 128, {0: 0, 1: 1, 2: 2, 3: 2}[m]))
                            run_chains(mlstm_chunk(128, j, j * 128), ssd_chunk(128, j, j * 128), attn_chunk(128, j * 128, keys))
                            y_transposes(128, j, Y_)
                        if tl['last']:
                            store_state(True, tl['seq'])
                    else:
                        for s in range(NS):
                            init_state_sample(s)
                            P.dma('sp', 'lck', (cst[:], I['ck'][l, s].rearrange("(j p) c -> p j c", p=128)), writes=['cst'])
                            P.copy(cstb[:], cst[:], ['cst'], ['cstb'])
                            for jt in range(4):
                                b = nb()
                                psb = ps[b][:].bitcast(BF16)
                                P.mm([('T', psb[:, cc * 128:(cc + 1) * 128], cstb[:, jt, cc * 128:(cc + 1) * 128], idb[:]) for cc in range(4)],
                                     ['cstb', 'idb'], [psn[b]])
                                P.copy(KH[:, :, jt, :], psb[:, 0:512].rearrange("p (c t) -> p c t", c=4), [psn[b]], [('KH', jt)])
                            P.dma('sp', 'lck', (cst[:], I['cv'][l, s].rearrange("(j p) c -> p j c", p=128)), writes=['cst'])
                            P.copy(VH[:, 0:4, :, 0:64], cst[:].rearrange("p j (h d) -> p j h d", h=8), ['cst'], [('VH', jt) for jt in range(4)])
                            P.memset(VH[:, 0:4, :, 64:65], 1.0, [('VH', jt) for jt in range(4)])
                            cs0 = t0 + s * 64
                            P.dma('sp', 'lk', (KH[:, :, 4, 0:64], S['k'][:, cs0:cs0 + 64].rearrange("(c p) t -> p c t", p=128)), writes=[('KH', 4)])
                            P.dma('sp', 'lv', (VH[:64, 4].rearrange("p h d -> p (h d)"), S['v'][cs0:cs0 + 64, :]), writes=[('VH', 4)])
                            keys = [(0, 128, 3), (1, 128, 2), (2, 128, 2), (3, 128, 1), (4, 64, 0)]
                            run_chains(mlstm_chunk(64, s, s * 64), ssd_chunk(64, s, s * 64), attn_chunk(64, s * 64, keys))
                            y_transposes(64, s, Y_)
                            store_state(False, s)
                    P.dma('pool', 'sty_', (S['Y'][:, t0:t0 + n].rearrange("(c p) t -> p c t", p=128), Y_[:, :, :n]), reads=['Yst'],
                          writes=[('S_Y', t0)])

        def mixer_merge(l, gi):
            Win = I['win'][l]
            with P.phase():
                Wg = P.sbuf('Wg', [128, 6, 8, 512], BF16)
                for bi in range(6):
                    col = C_GL + bi * 512
                    P.dma('pool', ('w', bi), (Wg[:, bi, :, :], Win[:, col:col + 512].rearrange("(kc p) c -> p kc c", p=128)), writes=[('Wg', bi)])
                Wp = P.sbuf('Wp', [128, 16, 1024], BF16)
                P.dma('pool', ('w', 6), [(Wp[:, 0:4, :], I['wps'][l, 0:512, :].rearrange("(kc p) c -> p kc c", p=128)),
                                         (Wp[:, 4:8, :], I['wps'][l, 512:1024, :].rearrange("(kc p) c -> p kc c", p=128)),
                                         (Wp[:, 8:12, :], I['wpa'][l].rearrange("(kc p) c -> p kc c", p=128)),
                                         (Wp[:, 12:16, :], I['wpm'][l].rearrange("(kc p) c -> p kc c", p=128))], writes=['Wp'])
                Wo = P.sbuf('Wo', [128, 8, 1024], BF16)
                P.dma('pool', ('w', 7), [(Wo[:, 0:4, :], I['wo'][l, 0:512, :].rearrange("(kc p) c -> p kc c", p=128)),
                                         (Wo[:, 4:8, :], I['wo'][l, 512:1024, :].rearrange("(kc p) c -> p kc c", p=128))], writes=['Wo'])
                bufs = res_bufs()
                XNts = [P.sbuf('XNt%d' % i, [128, 8, 512], BF16) for i in range(2)]
                Yts = [P.sbuf('Yt%d' % i, [128, 16, 512], BF16) for i in range(2)]
                Rt = P.sbuf('Rt', [128, 8, 512], F32)
                mg = P.sbuf('mg', [128, 8, 512], BF16)
                sig = [P.sbuf('sig%d' % i, [128, 512], F32) for i in range(3)]
                macc = P.sbuf('macc', [128, 512], F32)
                mtmp = P.sbuf('mtmp', [128, 512], F32)
                kcs = [(0, 8), (8, 4), (12, 4)]
                pend = [None]
                for ti, tl in enumerate(cfg.tiles):
                    n, t0 = tl['n'], tl['t0']
                    fmv = lambda d: d[:, t0:t0 + n].rearrange("(c p) t -> p c t", p=128)
                    XNt, xnn = XNts[ti % 2], ('XNt', ti % 2)
                    Yt, ytn = Yts[ti % 2], ('Yt', ti % 2)
                    P.dma('sp', ('ldxn', ti % 2), (XNt[:, :, :n], fmv(S['XN'])), writes=[xnn])
                    P.dma('sp', ('ldy', ti % 2), (Yt[:, :, :n], fmv(S['Y'])), writes=[ytn])
                    for dm in range(8):
                        for br in range(3):
                            bi = br * 2 + dm // 4
                            ch = dm % 4
                            bg = nb()
                            P.mm([('M', ps[bg][:, :n], Wg[:, bi, kc, ch * 128:(ch + 1) * 128], XNt[:, kc, :n], kc == 0, kc == 7) for kc in range(8)],
                                 [('Wg', bi), xnn], [psn[bg]])
                            P.act(sig[br][:, :n], ps[bg][:, :n], AF.Sigmoid, [psn[bg]], [('sig', br)])
                            k0, nk = kcs[br]
                            bp = nb()
                            P.mm([('M', ps[bp][:, :n], Wp[:, k0 + kc, dm * 128:(dm + 1) * 128], Yt[:, k0 + kc, :n], kc == 0, kc == nk - 1) for kc in range(nk)],
                                 ['Wp', ytn], [psn[bp]])
                            if dm == 1 and br == 0:
                                if pend[0] is not None:
                                    pend[0]()
                                    pend[0] = None
                                P.dma('sp', 'ldR', (Rt[:, :, :n], fmv(S['R'])), writes=['Rt'])
                            if br == 0:
                                P.tt(macc[:, :n], sig[0][:, :n], ps[bp][:, :n], ALU.mult, [('sig', 0), psn[bp]], ['macc'])
                            else:
                                P.tt(mtmp[:, :n], sig[br][:, :n], ps[bp][:, :n], ALU.mult, [('sig', br), psn[bp]], ['mtmp'])
                                if br == 1:
                                    P.tt(macc[:, :n], macc[:, :n], mtmp[:, :n], ALU.add, ['macc', 'mtmp'], ['macc'])
                                else:
                                    P.tt(mg[:, dm, :n], macc[:, :n], mtmp[:, :n], ALU.add, ['macc', 'mtmp'], ['mg'])
                    for dm in range(8):
                        b = nb()
                        P.mm([('M', ps[b][:, :n], Wo[:, kc, dm * 128:(dm + 1) * 128], mg[:, kc, :n], kc == 0, kc == 7) for kc in range(8)],
                             ['Wo', 'mg'], [psn[b]])
                        P.tt(Rt[:, dm, :n], ps[b][:, :n], Rt[:, dm, :n], ALU.add, [psn[b], 'Rt'], ['Rt'])
                    pend[0] = finish_residual(tl, Rt, 'Rt', gi, bufs, defer=True)
                if pend[0] is not None:
                    pend[0]()

        stop = getattr(cfg, 'stop', 99)
        for l in range(DEPTH):
            if stop >= 1:
                ffn(I['w1i'][l], I['w1o'][l], 3 * l + 1, False)
            if stop >= 2:
                mixer_inproj(l)
            if stop >= 3:
                mixer_core(l)
            if stop >= 4:
                mixer_merge(l, 3 * l + 2)
            last = (l == DEPTH - 1)
            if stop >= 5:
                ffn(I['w2i'][l], I['w2o'][l], 3 * DEPTH if last else 3 * (l + 1), last)
    return nc


def _host_params(inp, DEPTH):
    f = np.float32
    g = []
    for l in range(DEPTH):
        g += [inp['norm_ffn1'][l], inp['norm_mix'][l], inp['norm_ffn2'][l]]
    g.append(inp['final_norm'])
    gains = np.stack([np.asarray(v, f).reshape(8, 128).T for v in g], axis=1)
    cw = np.asarray(inp['ssd_conv_w'], f)
    convw = np.ascontiguousarray(cw.reshape(DEPTH, 4, 16, 128).transpose(3, 0, 2, 1))
    convb = np.ascontiguousarray(np.asarray(inp['ssd_conv_b'], f).reshape(DEPTH, 16, 128).transpose(2, 0, 1))
    r16 = np.stack([np.asarray(inp['ssd_dt_bias'], f), np.asarray(inp['ssd_a_log'], f), np.asarray(inp['ssd_d'], f)], axis=1)
    rep16 = np.ascontiguousarray(np.broadcast_to(r16[None], (128, DEPTH, 3, 16)))
    ssdn = np.ascontiguousarray(np.broadcast_to(np.asarray(inp['ssd_norm'], f)[None], (128, DEPTH, 1024)))
    mln = np.ascontiguousarray(np.broadcast_to(np.asarray(inp['mlstm_norm'], f)[None], (128, DEPTH, 512)))
    gbr = np.ascontiguousarray(np.broadcast_to(np.asarray(inp['mlstm_gate_bias'], f)[None], (128, DEPTH, 8)))
    tab = np.asarray(inp['attn_rel_bias'], f)
    kk = np.arange(128)[:, None]
    qq = np.arange(128)[None, :]
    kc_, qc_ = kk // 64, qq // 64
    kl, ql = kk % 64, qq % 64
    ab = np.zeros((DEPTH, 128, 4, 8, 128), f)
    for vi, m in enumerate([0, 1, 2, 4]):
        jj = 2 * m + qc_ - kc_
        rel = (ql - kl) + 64 * jj
        idx = np.clip(rel, -128, 128) + 128
        ok = (jj >= 0) & (jj <= 8)
        for l in range(DEPTH):
            t = tab[l][idx]
            t = np.where(ok[..., None], t, f(NEG))
            ab[l, :, vi] = t.transpose(0, 2, 1)
    cm = np.zeros((128, 8, 128), f)
    r = np.arange(128)[:, None]
    c = np.arange(128)[None, :]
    cm[:, 0] = (r <= c)
    cm[:, 1] = (r > c)
    cm[:, 2] = (r == c)
    cm[:, 3] = 1.0
    cm[:, 4] = (r == 127)
    cm[:, 5] = (r == 63)
    cm[:, 6] = np.where(c > r, NEG, 0.0)
    return dict(gains=np.ascontiguousarray(gains), convw=convw, convb=convb, rep16=rep16, ssdn=ssdn, mln=mln, gbr=gbr,
                abias=ab, cmat=cm)


def run(inp, n_cores, cfg):
    f = np.float32
    NP, LP, NS, DEPTH = cfg.NP, cfg.LP, cfg.NS, cfg.DEPTH
    nc = build(cfg)
    hp = _host_params(inp, DEPTH)
    wmap = dict(w1i='w_ffn1_in', w1o='w_ffn1_out', win='w_in', wps='w_proj_ssd', wpa='w_proj_attn', wpm='w_proj_mlstm',
                wo='w_out', w2i='w_ffn2_in', w2o='w_ffn2_out')
    shared = {k: np.ascontiguousarray(np.asarray(inp[v], f)) for k, v in wmap.items()}
    shared.update(hp)
    in_maps = []
    for c in range(n_cores):
        m = dict(shared)
        ps_, ss_ = slice(c * NP, (c + 1) * NP), slice(c * NS, (c + 1) * NS)
        m['xp'] = np.ascontiguousarray(np.asarray(inp['x_prompt'], f)[ps_].reshape(NP * LP, D))
        m['xs'] = np.ascontiguousarray(np.asarray(inp['x_sample'], f)[ss_].reshape(NS * 64, D))
        m['ck'] = np.ascontiguousarray(np.asarray(inp['cache_attn_k'], f)[:, ss_].reshape(DEPTH, NS, 512, 512))
        m['cv'] = np.ascontiguousarray(np.asarray(inp['cache_attn_v'], f)[:, ss_].reshape(DEPTH, NS, 512, 512))
        m['sssd'] = np.ascontiguousarray(np.asarray(inp['state_ssd'], f)[:, ss_].reshape(DEPTH, NS, 1024, 128))
        m['sconv'] = np.ascontiguousarray(np.asarray(inp['state_ssd_conv'], f)[:, ss_])
        m['smc'] = np.ascontiguousarray(np.asarray(inp['state_mlstm_c'], f)[:, ss_])
        m['smn'] = np.ascontiguousarray(np.asarray(inp['state_mlstm_n'], f)[:, ss_])
        m['smm'] = np.ascontiguousarray(np.asarray(inp['state_mlstm_m'], f)[:, ss_])
        in_maps.append(m)
    res = run_bass_kernel_spmd(nc, in_maps, core_ids=list(range(n_cores)))
    R = res.results
    if getattr(cfg, 'dbg', ()):
        cfg.dbg_out = {k: np.asarray(R[0][k]) for k in cfg.dbg}
    KEEP = min(512, LP)

    def cat(name, axis, shape_tail):
        return np.concatenate([np.asarray(r[name]) for r in R], axis=axis)

    yp = np.concatenate([np.asarray(r['yp']).reshape(NP, LP, D) for r in R], 0)
    ys = np.concatenate([np.asarray(r['ys']).reshape(NS, 64, D) for r in R], 0)
    kp = np.concatenate([np.asarray(r['kp']).reshape(DEPTH, NP, KEEP, 8, 64) for r in R], 1)
    vp = np.concatenate([np.asarray(r['vp']).reshape(DEPTH, NP, KEEP, 8, 64) for r in R], 1)
    ssdp = np.concatenate([np.asarray(r['ssdp']).reshape(DEPTH, NP, 16, 64, 128) for r in R], 1)
    convp = np.concatenate([np.asarray(r['convp']) for r in R], 1)
    mcp = np.concatenate([np.asarray(r['mcp']) for r in R], 1)
    mnp = np.concatenate([np.asarray(r['mnp']) for r in R], 1)
    mmp = np.concatenate([np.asarray(r['mmp']) for r in R], 1)
    ks = np.concatenate([np.asarray(r['ks']).reshape(DEPTH, NS, 64, 8, 64) for r in R], 1)
    vs = np.concatenate([np.asarray(r['vs']).reshape(DEPTH, NS, 64, 8, 64) for r in R], 1)
    ssds = np.concatenate([np.asarray(r['ssds']).reshape(DEPTH, NS, 16, 64, 128) for r in R], 1)
    convs = np.concatenate([np.asarray(r['convs']) for r in R], 1)
    mcs = np.concatenate([np.asarray(r['mcs']) for r in R], 1)
    mns = np.concatenate([np.asarray(r['mns']) for r in R], 1)
    mms = np.concatenate([np.asarray(r['mms']) for r in R], 1)
    outs = (yp, ys, kp, vp, ssdp, convp, mcp, mnp, mmp, ks, vs, ssds, convs, mcs, mns, mms)
    return tuple(np.ascontiguousarray(o, dtype=np.float32) for o in outs)


def kernel(**inputs):
    cfg = Cfg(NP=2, LP=2048, NS=4, DEPTH=2)
    return run(inputs, 8, cfg)
```
